# Optimizing a Trainium2 kernel written in Bass

```python
import math
import jax, jax.numpy as jnp
from jax import lax
import numpy as np

D_MODEL = 1024
BATCH = 32
SEQ = 2048
DEPTH = 1
DEC_BATCH = 16
DEC_SEQ = 16
PAST_LEN = 4096

CHUNK = 64
HEAD_DIM = 64
H_A = 8
H_B = 8
W_A = H_A * HEAD_DIM
W_B = H_B * HEAD_DIM
BAND_CHUNKS = 8
BAND_PAST = BAND_CHUNKS * CHUNK
BAND_LEN = BAND_PAST + CHUNK
MAX_REL = 128
Q_BLOCK = 128
PLE_DIM = 256
PEER_HEADS = 8
PEER_DK = 128
N_KEYS = 128
N_EXPERTS = N_KEYS * N_KEYS
PEER_TOPK = 16
PEER_TOKEN_BLOCK = 256
RMS_EPS = 1e-6
FORGET_BIAS_INIT = 3.0
IN_W = 3 * W_A + H_A + 3 * W_B + 2 * D_MODEL
IN_SPLITS = [int(s) for s in np.cumsum([W_A, W_A, W_A, H_A, W_B, W_B, W_B, D_MODEL])]

kernel_name = "fox_chunkband_peer_streaming_encoder"


def rms_norm(x, g):
    xf = x.astype(jnp.float32)
    y = xf * lax.rsqrt(jnp.mean(xf * xf, axis=-1, keepdims=True) + RMS_EPS)
    return (y * g.astype(jnp.float32)).astype(x.dtype)


def mixer_projections(h, g_mix, w_in, b_f, qn_a, kn_a, qn_b, kn_b):
    B, T, _ = h.shape
    n1 = rms_norm(h, g_mix)
    z = n1 @ w_in
    qa, ka, va, fl, qb, kb, vb, ga, gb = jnp.split(z, IN_SPLITS, axis=-1)
    qa = rms_norm(qa.reshape(B, T, H_A, HEAD_DIM), qn_a)
    ka = rms_norm(ka.reshape(B, T, H_A, HEAD_DIM), kn_a)
    va = va.reshape(B, T, H_A, HEAD_DIM)
    qb = rms_norm(qb.reshape(B, T, H_B, HEAD_DIM), qn_b)
    kb = rms_norm(kb.reshape(B, T, H_B, HEAD_DIM), kn_b)
    vb = vb.reshape(B, T, H_B, HEAD_DIM)
    log_f = jax.nn.log_sigmoid(fl.astype(jnp.float32) + b_f.astype(jnp.float32))
    return qa, ka, va, log_f, qb, kb, vb, jax.nn.sigmoid(ga), jax.nn.sigmoid(gb)


def forgetting_attention_prompt(q, k, v, log_f):
    B, T, H, Dh = q.shape
    c = jnp.cumsum(log_f, axis=1)
    cT = c.transpose(0, 2, 1)
    nb = T // Q_BLOCK
    qb = q.reshape(B, nb, Q_BLOCK, H, Dh).transpose(1, 0, 2, 3, 4)
    cb = cT.reshape(B, H, nb, Q_BLOCK).transpose(2, 0, 1, 3)
    kpos = jnp.arange(T)
    scale = HEAD_DIM ** -0.5

    def block(args):
        i, qi, ci = args
        s = jnp.einsum('bqhd,bkhd->bhqk', qi, k).astype(jnp.float32) * scale
        s = s + ci[..., None] - cT[:, :, None, :]
        qpos = i * Q_BLOCK + jnp.arange(Q_BLOCK)
        s = jnp.where(kpos[None, :] <= qpos[:, None], s, -jnp.inf)
        p = jax.nn.softmax(s, axis=-1).astype(v.dtype)
        return jnp.einsum('bhqk,bkhd->bqhd', p, v)

    out = lax.map(block, (jnp.arange(nb), qb, cb))
    return out.transpose(1, 0, 2, 3, 4).reshape(B, T, H * Dh)


def forgetting_attention_sample(q, k_new, v_new, logf_new, k_cache, v_cache, logf_cache):
    B, n, H, Dh = q.shape
    P = k_cache.shape[1]
    k = jnp.concatenate([k_cache.astype(k_new.dtype), k_new], axis=1)
    v = jnp.concatenate([v_cache.astype(v_new.dtype), v_new], axis=1)
    c = jnp.cumsum(jnp.concatenate([logf_cache.astype(jnp.float32), logf_new], axis=1), axis=1)
    cT = c.transpose(0, 2, 1)
    s = jnp.einsum('bqhd,bkhd->bhqk', q, k).astype(jnp.float32) * (HEAD_DIM ** -0.5)
    s = s + cT[:, :, P:, None] - cT[:, :, None, :]
    mask = jnp.arange(P + n)[None, :] <= (P + jnp.arange(n))[:, None]
    s = jnp.where(mask, s, -jnp.inf)
    p = jax.nn.softmax(s, axis=-1).astype(v.dtype)
    return jnp.einsum('bhqk,bkhd->bqhd', p, v).reshape(B, n, H * Dh)


def rel_bias_lookup(rel_bias, rel):
    idx = jnp.clip(rel, -MAX_REL, MAX_REL) + MAX_REL
    return rel_bias[:, idx].astype(jnp.float32)


def chunk_band_prompt(q, k, v, rel_bias):
    B, T, H, Dh = q.shape
    nc = T // CHUNK
    kp = jnp.pad(k, ((0, 0), (BAND_PAST, 0), (0, 0), (0, 0)))
    vp = jnp.pad(v, ((0, 0), (BAND_PAST, 0), (0, 0), (0, 0)))
    qc = q.reshape(B, nc, CHUNK, H, Dh).transpose(1, 0, 2, 3, 4)
    i = jnp.arange(CHUNK)
    j = jnp.arange(BAND_LEN)
    bias = rel_bias_lookup(rel_bias, (j[None, :] - BAND_PAST) - i[:, None])
    scale = HEAD_DIM ** -0.5

    def chunk(args):
        c, qi = args
        kb = lax.dynamic_slice_in_dim(kp, c * CHUNK, BAND_LEN, axis=1)
        vb = lax.dynamic_slice_in_dim(vp, c * CHUNK, BAND_LEN, axis=1)
        s = jnp.einsum('bqhd,bkhd->bhqk', qi, kb).astype(jnp.float32) * scale + bias
        s = jnp.where(j >= BAND_PAST - c * CHUNK, s, -jnp.inf)
        p = jax.nn.softmax(s, axis=-1).astype(vb.dtype)
        return jnp.einsum('bhqk,bkhd->bqhd', p, vb)

    out = lax.map(chunk, (jnp.arange(nc), qc))
    return out.transpose(1, 0, 2, 3, 4).reshape(B, T, H * Dh)


def chunk_band_sample(q, k_new, v_new, k_cache, v_cache, rel_bias):
    B, n, H, Dh = q.shape
    L = k_cache.shape[1]
    k = jnp.concatenate([k_cache.astype(k_new.dtype), k_new], axis=1)
    v = jnp.concatenate([v_cache.astype(v_new.dtype), v_new], axis=1)
    i = jnp.arange(n)
    j = jnp.arange(L + n)
    bias = rel_bias_lookup(rel_bias, (j[None, :] - L) - i[:, None])
    s = jnp.einsum('bqhd,bkhd->bhqk', q, k).astype(jnp.float32) * (HEAD_DIM ** -0.5) + bias
    p = jax.nn.softmax(s, axis=-1).astype(v.dtype)
    return jnp.einsum('bhqk,bkhd->bqhd', p, v).reshape(B, n, H * Dh)


def peer_ffn(x, w_q, sub_keys, u, v):
    T, D = x.shape
    pad = (-T) % PEER_TOKEN_BLOCK
    xb = jnp.pad(x, ((0, pad), (0, 0))).reshape(-1, PEER_TOKEN_BLOCK, D)

    def block(xi):
        t = xi.shape[0]
        q = (xi @ w_q).reshape(t, PEER_HEADS, 2, PEER_DK // 2)
        s = jnp.einsum('thpc,hpnc->thpn', q, sub_keys).astype(jnp.float32)
        top_s, top_i = lax.top_k(s, PEER_TOPK)
        cand_s = (top_s[:, :, 0, :, None] + top_s[:, :, 1, None, :]).reshape(t, PEER_HEADS, PEER_TOPK * PEER_TOPK)
        cand_i = (top_i[:, :, 0, :, None] * N_KEYS + top_i[:, :, 1, None, :]).reshape(t, PEER_HEADS, PEER_TOPK * PEER_TOPK)
        best_s, pos = lax.top_k(cand_s, PEER_TOPK)
        idx = jnp.take_along_axis(cand_i, pos, axis=-1)
        g = jax.nn.softmax(best_s, axis=-1)
        hid = jax.nn.gelu(jnp.einsum('thkd,td->thk', u[idx], xi).astype(jnp.float32))
        wgt = (g * hid).astype(xi.dtype)
        return jnp.einsum('thk,thkd->td', wgt, v[idx])

    return lax.map(block, xb).reshape(-1, D)[:T]


def merge_and_channel(h, ya, yb, ga, gb, p_l, w_up_a, w_up_b, w_out, g_ffn, peer_wq, peer_subkeys,
                      peer_u, peer_v, g_ple, w_ple_gate, w_ple_proj):
    merged = ga * (ya @ w_up_a) + gb * (yb @ w_up_b)
    h = h + merged @ w_out
    B, T, D = h.shape
    n2 = rms_norm(h, g_ffn)
    h = h + peer_ffn(n2.reshape(B * T, D), peer_wq, peer_subkeys, peer_u, peer_v).reshape(B, T, D)
    gate = jax.nn.sigmoid(rms_norm(h, g_ple) @ w_ple_gate)
    return h + gate * (p_l @ w_ple_proj)


def setup_inputs(seed: int = 0) -> dict:
    key = jax.random.key(seed)
    ks = jax.random.split(key, 28)
    nrm = lambda k, shape, s=1.0: jax.random.normal(k, shape, jnp.float32) * s
    lb = min(BAND_PAST, PAST_LEN)
    return {
        "x_prompt": nrm(ks[0], (BATCH, SEQ, D_MODEL)),
        "x_sample": nrm(ks[1], (DEC_BATCH, DEC_SEQ, D_MODEL)),
        "cache_a_k": nrm(ks[2], (DEPTH, DEC_BATCH, PAST_LEN, H_A, HEAD_DIM)),
        "cache_a_v": nrm(ks[3], (DEPTH, DEC_BATCH, PAST_LEN, H_A, HEAD_DIM)),
        "cache_a_logf": jax.nn.log_sigmoid(nrm(ks[4], (DEPTH, DEC_BATCH, PAST_LEN, H_A)) + FORGET_BIAS_INIT),
        "cache_b_k": nrm(ks[5], (DEPTH, DEC_BATCH, lb, H_B, HEAD_DIM)),
        "cache_b_v": nrm(ks[6], (DEPTH, DEC_BATCH, lb, H_B, HEAD_DIM)),
        "p_prompt": nrm(ks[7], (DEPTH, BATCH, SEQ, PLE_DIM)),
        "p_sample": nrm(ks[8], (DEPTH, DEC_BATCH, DEC_SEQ, PLE_DIM)),
        "g_mix": 1.0 + nrm(ks[9], (DEPTH, D_MODEL), 0.02),
        "w_in": nrm(ks[10], (DEPTH, D_MODEL, IN_W), D_MODEL ** -0.5),
        "b_f": FORGET_BIAS_INIT + nrm(ks[11], (DEPTH, H_A), 0.1),
        "qn_a": 1.0 + nrm(ks[12], (DEPTH, HEAD_DIM), 0.02),
        "kn_a": 1.0 + nrm(ks[13], (DEPTH, HEAD_DIM), 0.02),
        "qn_b": 1.0 + nrm(ks[14], (DEPTH, HEAD_DIM), 0.02),
        "kn_b": 1.0 + nrm(ks[15], (DEPTH, HEAD_DIM), 0.02),
        "rel_bias_b": nrm(ks[16], (DEPTH, H_B, 2 * MAX_REL + 1), 0.1),
        "w_up_a": nrm(ks[17], (DEPTH, W_A, D_MODEL), W_A ** -0.5),
        "w_up_b": nrm(ks[18], (DEPTH, W_B, D_MODEL), W_B ** -0.5),
        "w_out": nrm(ks[19], (DEPTH, D_MODEL, D_MODEL), D_MODEL ** -0.5),
        "g_ffn": 1.0 + nrm(ks[20], (DEPTH, D_MODEL), 0.02),
        "peer_wq": nrm(ks[21], (DEPTH, D_MODEL, PEER_HEADS * PEER_DK), D_MODEL ** -0.5),
        "peer_subkeys": nrm(ks[22], (DEPTH, PEER_HEADS, 2, N_KEYS, PEER_DK // 2), (PEER_DK // 2) ** -0.5),
        "peer_u": nrm(ks[23], (DEPTH, N_EXPERTS, D_MODEL), D_MODEL ** -0.5),
        "peer_v": nrm(ks[24], (DEPTH, N_EXPERTS, D_MODEL), 0.25),
        "g_ple": 1.0 + nrm(ks[25], (DEPTH, D_MODEL), 0.02),
        "w_ple_gate": nrm(ks[26], (DEPTH, D_MODEL, D_MODEL), D_MODEL ** -0.5),
        "w_ple_proj": nrm(ks[27], (DEPTH, PLE_DIM, D_MODEL), PLE_DIM ** -0.5),
    }


def reference(x_prompt, x_sample, cache_a_k, cache_a_v, cache_a_logf, cache_b_k, cache_b_v, p_prompt, p_sample,
              g_mix, w_in, b_f, qn_a, kn_a, qn_b, kn_b, rel_bias_b, w_up_a, w_up_b, w_out, g_ffn,
              peer_wq, peer_subkeys, peer_u, peer_v, g_ple, w_ple_gate, w_ple_proj):
    hp, hs = x_prompt, x_sample
    akp, avp, afp, bkp, bvp = [], [], [], [], []
    aks, avs, afs, bks, bvs = [], [], [], [], []
    for l in range(DEPTH):
        proj = (g_mix[l], w_in[l], b_f[l], qn_a[l], kn_a[l], qn_b[l], kn_b[l])
        chan = (w_up_a[l], w_up_b[l], w_out[l], g_ffn[l], peer_wq[l], peer_subkeys[l], peer_u[l], peer_v[l],
                g_ple[l], w_ple_gate[l], w_ple_proj[l])
        qa, ka, va, lf, qb, kb, vb, ga, gb = mixer_projections(hp, *proj)
        ya = forgetting_attention_prompt(qa, ka, va, lf)
        yb = chunk_band_prompt(qb, kb, vb, rel_bias_b[l])
        hp = merge_and_channel(hp, ya, yb, ga, gb, p_prompt[l], *chan)
        akp.append(ka); avp.append(va); afp.append(lf)
        bkp.append(kb[:, -BAND_PAST:]); bvp.append(vb[:, -BAND_PAST:])
        qa, ka, va, lf, qb, kb, vb, ga, gb = mixer_projections(hs, *proj)
        ya = forgetting_attention_sample(qa, ka, va, lf, cache_a_k[l], cache_a_v[l], cache_a_logf[l])
        yb = chunk_band_sample(qb, kb, vb, cache_b_k[l], cache_b_v[l], rel_bias_b[l])
        hs = merge_and_channel(hs, ya, yb, ga, gb, p_sample[l], *chan)
        aks.append(ka); avs.append(va); afs.append(lf); bks.append(kb); bvs.append(vb)
    return (hp, hs,
            jnp.stack(akp), jnp.stack(avp), jnp.stack(afp), jnp.stack(bkp), jnp.stack(bvp),
            jnp.stack(aks), jnp.stack(avs), jnp.stack(afs), jnp.stack(bks), jnp.stack(bvs))
```

```python
import numpy as np
from contextlib import ExitStack
import concourse.bass as bass
import concourse.mybir as mybir
from concourse.bass_utils import run_bass_kernel_spmd

F32 = mybir.dt.float32; BF16 = mybir.dt.bfloat16; I32 = mybir.dt.int32; U32 = mybir.dt.uint32
ALU = mybir.AluOpType; AF = mybir.ActivationFunctionType; AX = mybir.AxisListType

NCORES = 8
D = 1024; T = 2048; NH = 8; HD = 64; PAST = 4096; LB = 512; NS = 16
EPS = 1e-6
NEG = -30000.0
GELU = AF.Gelu_apprx_tanh


class Sched:
    def __init__(self, nc, es, n_dma_sems=40):
        self.nc = nc
        self.eng = {"pe": nc.tensor, "dve": nc.vector, "act": nc.scalar, "pool": nc.gpsimd, "sp": nc.sync}
        self.sem = {k: es.enter_context(nc.semaphore("s_" + k)) for k in ("pe", "dve", "act", "pool")}
        self.cnt = {k: 0 for k in self.sem}
        self.dsem = [es.enter_context(nc.semaphore("d%d" % i)) for i in range(n_dma_sems)]
        self.dcnt = [0] * n_dma_sems
        self.dpool = {"sp": list(range(0, n_dma_sems // 2)), "pool": list(range(n_dma_sems // 2, n_dma_sems))}
        self.dnext = {"sp": 0, "pool": 0}
        self.seen = {k: {} for k in self.eng}
        self.lastw = {}
        self.readers = {}
        self.ninstr = 0

    def _need(self, e, deps):
        for (src, n) in deps:
            if src == "pe" and e == "pe":
                continue
            if self.seen[e].get(src, 0) >= n:
                continue
            sem = self.sem[src] if isinstance(src, str) else self.dsem[src]
            self.eng[e].wait_ge(sem, n)
            self.seen[e][src] = n

    def _deps(self, reads, writes, e=None):
        deps = []
        for k in reads:
            if k in self.lastw:
                deps.append(self.lastw[k])
        for k in writes:
            if k in self.lastw and self.lastw[k][0] != e:
                deps.append(self.lastw[k])
            for s, n in self.readers.get(k, {}).items():
                if s != e:
                    deps.append((s, n))
        return deps

    def _commit(self, tag, reads, writes):
        for k in reads:
            self.readers.setdefault(k, {})[tag[0]] = tag[1]
        for k in writes:
            self.lastw[k] = tag
            self.readers[k] = {}

    def op(self, e, reads, writes, fn):
        self._need(e, self._deps(reads, writes, e))
        ins = fn(self.eng[e])
        self.cnt[e] += 1
        ins.then_inc(self.sem[e], 1)
        self._commit((e, self.cnt[e]), reads, writes)
        self.ninstr += 1
        return ins

    def dma(self, reads, writes, out, in_, q="sp", **kw):
        pool_ = self.dpool[q]
        i = pool_[self.dnext[q] % len(pool_)]
        self.dnext[q] += 1
        deps = self._deps(reads, writes)
        if self.dcnt[i] > 0:
            deps.append((i, self.dcnt[i]))
        self._need(q, deps)
        ins = self.eng[q].dma_start(out=out, in_=in_, **kw)
        self.dcnt[i] += 16
        ins.then_inc(self.dsem[i], 16)
        self._commit((i, self.dcnt[i]), reads, writes)
        self.ninstr += 1
        return ins

    def barrier(self):
        deps = [(i, c) for i, c in enumerate(self.dcnt) if c > 0]
        deps += [(k, c) for k, c in self.cnt.items() if c > 0]
        for e in self.eng:
            self._need(e, deps)
        self.lastw = {}
        self.readers = {}


def build(NSEQ=4, NSAMP=2, PEER=True):
    nc = bass.Bass("TRN2", target_bir_lowering=False)
    NTOK = NSEQ * T
    NALL = NTOK + NSAMP * NS

    def DI(name, shape, dt=F32):
        return nc.dram_tensor(name, list(shape), dt, kind="ExternalInput").ap()

    def DO(name, shape, dt=F32):
        return nc.dram_tensor(name, list(shape), dt, kind="ExternalOutput").ap()

    def DS(name, shape, dt=BF16):
        return nc.dram_tensor(name, list(shape), dt, kind="Internal").ap()

    xp = DI("xp", [NTOK, D]); xs = DI("xs", [NSAMP * NS, D])
    pp = DI("pp", [NTOK, 256]); ps_ = DI("ps", [NSAMP * NS, 256])
    cak = DI("cak", [NSAMP, PAST, 512]); cav = DI("cav", [NSAMP, PAST, 512]); calf = DI("calf", [NSAMP, PAST, NH])
    cbk = DI("cbk", [NSAMP, LB, 512]); cbv = DI("cbv", [NSAMP, LB, 512])
    wi = DI("wi", [D, 5120]); wfl = DI("wfl", [D, NH])
    g_mix = DI("g_mix", [D]); g_ffn = DI("g_ffn", [D]); g_ple = DI("g_ple", [D]); b_f = DI("b_f", [NH])
    qn_a = DI("qn_a", [HD]); kn_a = DI("kn_a", [HD]); qn_b = DI("qn_b", [HD]); kn_b = DI("kn_b", [HD])
    biasT = DI("biasT", [NH, 3, 128, 128])
    wupa = DI("wupa", [512, D]); wupb = DI("wupb", [512, D]); wout = DI("wout", [D, D])
    wq = DI("wq", [D, D]); wpg = DI("wpg", [D, D]); wpp = DI("wpp", [256, D])
    skT = DI("skT", [NH, 2, 64, 128])
    uL = DI("uL", [128 * 128, D]); vL = DI("vL", [128 * 128, D])

    y = DO("y", [NALL, D])
    ak = DO("ak", [NTOK, 512]); av = DO("av", [NTOK, 512]); af = DO("af", [NTOK, NH])
    bk = DO("bk", [NSEQ * LB, 512]); bv = DO("bv", [NSEQ * LB, 512])
    aks = DO("aks", [NSAMP * NS, 512]); avs = DO("avs", [NSAMP * NS, 512]); afs = DO("afs", [NSAMP * NS, NH])
    bks = DO("bks", [NSAMP * NS, 512]); bvs = DO("bvs", [NSAMP * NS, 512])

    wi_bf = DS("wi_bf", [D, 5120])
    wupa_bf = DS("wupa_bf", [512, D]); wupb_bf = DS("wupb_bf", [512, D]); wout_bf = DS("wout_bf", [D, D])
    wq_bf = DS("wq_bf", [D, D]); wpg_bf = DS("wpg_bf", [D, D]); wpp_bf = DS("wpp_bf", [256, D])
    u_bf = DS("u_bf", [128 * 128, D]); v_bf = DS("v_bf", [128 * 128, D])
    gate_s = DS("gate_s", [NALL, 2048])
    yTa_s = DS("yTa_s", [512, NALL]); yTb_s = DS("yTb_s", [512, NALL])
    h1_s = DS("h1_s", [NALL, D], F32)

    with ExitStack() as es0:
        S = Sched(nc, es0)

        def PS(name, shape, dt=F32):
            return es0.enter_context(nc.psum_tensor(name, shape, dt))

        Fb = [PS("f%d" % i, [128, 512]) for i in range(8)]
        Tb = [Fb[6][:, :].bitcast(BF16), Fb[7][:, :].bitcast(BF16)]

        uid = [0]

        def SB(es, name, shape, dt=F32):
            uid[0] += 1
            return es.enter_context(nc.sbuf_tensor("%s_%d" % (name, uid[0]), shape, dt))

        ident = SB(es0, "ident", [128, 128], BF16)
        identf = SB(es0, "identf", [128, 128], F32)
        iota_f = SB(es0, "iota_f", [128, 128], F32)
        ones_f = SB(es0, "ones_f", [128, 64], F32)
        es1 = es0.enter_context(ExitStack())
        Umat = SB(es1, "Umat", [128, 128], F32)
        sel127 = SB(es1, "sel127", [128, 128], F32)
        tri = SB(es1, "tri", [128, 128], BF16)
        S.op("pool", [], ["ident"], lambda e: e.memset(ident[:], 1.0))
        S.op("pool", ["ident"], ["ident"], lambda e: e.affine_select(out=ident[:], in_=ident[:], pattern=[[-1, 128]], compare_op=ALU.is_equal, fill=0.0, base=0, channel_multiplier=1))
        S.op("pool", [], ["identf"], lambda e: e.memset(identf[:], 1.0))
        S.op("pool", ["identf"], ["identf"], lambda e: e.affine_select(out=identf[:], in_=identf[:], pattern=[[-1, 128]], compare_op=ALU.is_equal, fill=0.0, base=0, channel_multiplier=1))
        S.op("pool", [], ["Umat"], lambda e: e.memset(Umat[:], 1.0))
        S.op("pool", ["Umat"], ["Umat"], lambda e: e.affine_select(out=Umat[:], in_=Umat[:], pattern=[[1, 128]], compare_op=ALU.is_ge, fill=0.0, base=0, channel_multiplier=-1))
        S.op("pool", [], ["tri"], lambda e: e.memset(tri[:], 1.0))
        S.op("pool", ["tri"], ["tri"], lambda e: e.affine_select(out=tri[:], in_=tri[:], pattern=[[1, 128]], compare_op=ALU.is_ge, fill=0.0, base=0, channel_multiplier=-1))
        S.op("pool", [], ["sel127"], lambda e: e.memset(sel127[:], 1.0))
        S.op("pool", ["sel127"], ["sel127"], lambda e: e.affine_select(out=sel127[:], in_=sel127[:], pattern=[[0, 128]], compare_op=ALU.is_equal, fill=0.0, base=-127, channel_multiplier=1))
        S.op("pool", [], ["iota_f"], lambda e: e.iota(iota_f[:], pattern=[[1, 128]], base=0, channel_multiplier=0, allow_small_or_imprecise_dtypes=True))
        S.op("pool", [], ["ones_f"], lambda e: e.memset(ones_f[:], 1.0))

        with ExitStack() as es:
            stg = [SB(es, "stg%d" % i, [128, 2048], F32) for i in range(6)]
            stb = [SB(es, "stb%d" % i, [128, 2048], BF16) for i in range(6)]
            cnt = [0]

            def conv(src, dst, R, C):
                sv = src.rearrange("(n p) c -> n p c", p=128)
                dv = dst.rearrange("(n p) c -> n p c", p=128)
                for n in range(R // 128):
                    for c0 in range(0, C, 2048):
                        cw = min(2048, C - c0)
                        i = cnt[0] % 6
                        cnt[0] += 1
                        S.dma([], ["stg%d" % i], stg[i][:, :cw], sv[n, :, c0:c0 + cw])
                        e = ("dve", "act", "dve", "act", "dve", "act")[i]
                        if e == "act":
                            S.op(e, ["stg%d" % i], ["stb%d" % i], lambda g, i=i, cw=cw: g.copy(out=stb[i][:, :cw], in_=stg[i][:, :cw]))
                        else:
                            S.op(e, ["stg%d" % i], ["stb%d" % i], lambda g, i=i, cw=cw: g.tensor_copy(out=stb[i][:, :cw], in_=stg[i][:, :cw]))
                        S.dma(["stb%d" % i], [], dv[n, :, c0:c0 + cw], stb[i][:, :cw], q="pool")

            conv(wi, wi_bf, D, 5120)
            conv(wupa, wupa_bf, 512, D); conv(wupb, wupb_bf, 512, D); conv(wout, wout_bf, D, D)
            conv(wq, wq_bf, D, D); conv(wpg, wpg_bf, D, D); conv(wpp, wpp_bf, 256, D)
            if PEER:
                conv(uL, u_bf, 128 * 128, D); conv(vL, v_bf, 128 * 128, D)
        S.barrier()

        gmix_bc = SB(es1, "gmix_bc", [128, D])
        bf_bc = SB(es1, "bf_bc", [128, NH])
        qna_bc = SB(es1, "qna_bc", [128, HD]); kna_bc = SB(es1, "kna_bc", [128, HD])
        qnb_bc = SB(es1, "qnb_bc", [128, HD]); knb_bc = SB(es1, "knb_bc", [128, HD])
        negC = SB(es1, "negC", [128, 4])
        for nm, t_, src in (("gmix_bc", gmix_bc, g_mix), ("bf_bc", bf_bc, b_f),
                            ("qna_bc", qna_bc, qn_a), ("kna_bc", kna_bc, kn_a), ("qnb_bc", qnb_bc, qn_b), ("knb_bc", knb_bc, kn_b)):
            S.dma([], [nm], t_[:], src.partition_broadcast(128))
        for col, (qk, kk, qt, kt) in enumerate((("qna_bc", "kna_bc", qna_bc, kna_bc), ("qnb_bc", "knb_bc", qnb_bc, knb_bc))):
            S.op("dve", [qk], ["negC"], lambda e, qt=qt, col=col: e.tensor_reduce(out=negC[:, 2 + col:3 + col], in_=qt[:], axis=AX.X, op=ALU.max, apply_absolute_value=True))
            S.op("dve", [kk], ["negC"], lambda e, kt=kt, col=col: e.tensor_reduce(out=negC[:, col:col + 1], in_=kt[:], axis=AX.X, op=ALU.max, apply_absolute_value=True))
            S.op("dve", ["negC"], ["negC"], lambda e, col=col: e.scalar_tensor_tensor(out=negC[:, col:col + 1], in0=negC[:, col:col + 1], scalar=-8.0, in1=negC[:, 2 + col:3 + col], op0=ALU.mult, op1=ALU.mult))
        S.op("dve", ["qna_bc"], ["qna_bc"], lambda e: e.tensor_scalar(out=qna_bc[:], in0=qna_bc[:], scalar1=0.125, scalar2=None, op0=ALU.mult))
        S.op("dve", ["qnb_bc"], ["qnb_bc"], lambda e: e.tensor_scalar(out=qnb_bc[:], in0=qnb_bc[:], scalar1=0.125, scalar2=None, op0=ALU.mult))
        wfl_bf = SB(es1, "wfl_bf", [128, 8, NH], BF16)
        bias_bf = SB(es1, "bias_bf", [128, NH, 4, 128], BF16)
        with ExitStack() as es:
            wfl_f = SB(es, "wfl_f", [128, 8, NH]); bias_f = SB(es, "bias_f", [128, NH, 3, 128])
            S.dma([], ["wfl_f"], wfl_f[:], wfl.rearrange("(kc p) h -> p kc h", p=128))
            S.op("dve", ["wfl_f"], ["wfl_bf"], lambda e: e.tensor_copy(out=wfl_bf[:], in_=wfl_f[:]))
            for h in range(NH):
                S.dma([], ["bias_f%d" % h], bias_f[:, h], biasT[h].rearrange("v k q -> k v q"))
                S.op("dve", ["bias_f%d" % h], ["bias_bf"], lambda e, h=h: e.tensor_copy(out=bias_bf[:, h, 0:3, :], in_=bias_f[:, h]))
                S.op("dve", ["bias_f%d" % h], ["bias_bf"], lambda e, h=h: e.tensor_copy(out=bias_bf[:, h, 3, :], in_=bias_f[:, h, 2, :]))
            S.op("pool", [], ["bias_bf"], lambda e: e.memset(bias_bf[64:128, :, 0, 0:64], NEG))
            S.op("pool", [], ["bias_bf"], lambda e: e.memset(bias_bf[0:64, :, 3, 64:128], NEG))
            S.barrier()

        sel15 = SB(es1, "sel15", [128, 128], F32)
        S.op("pool", [], ["sel15"], lambda e: e.memset(sel15[:], 1.0))
        S.op("pool", ["sel15"], ["sel15"], lambda e: e.affine_select(out=sel15[:], in_=sel15[:], pattern=[[0, 128]], compare_op=ALU.is_equal, fill=0.0, base=-15, channel_multiplier=1))

        def rms_rstd(src_key, src_ap, nt, width, acc, acck, scr, scrk):
            S.op("act", [src_key], [scrk, acck], lambda e: e.activation(out=scr[:nt, :width], in_=src_ap, func=AF.Square, scale=float(width) ** -0.5, accum_out=acc[:nt, 0:1]))
            S.op("act", [acck], [acck], lambda e: e.activation(out=acc[:nt, 0:1], in_=acc[:nt, 0:1], func=AF.Ln, bias=EPS, scale=1.0))
            S.op("act", [acck], [acck], lambda e: e.activation(out=acc[:nt, 0:1], in_=acc[:nt, 0:1], func=AF.Exp, scale=-0.5))

        def run_sequence(es, seq, is_sample):
            if not is_sample:
                tiles = [(seq * T + i * 128, 128, i * 128, i) for i in range(16)]
                groups = [tiles[g * 4:(g + 1) * 4] for g in range(4)]
                TK = T; nkt = 16; x_d = xp; tok0 = seq * T
                o_ak, o_av, o_af, o_bk, o_bv = ak, av, af, bk, bv
                TKB = T; nktb = 16; TQ = T; nqt = 16
            else:
                tiles = [(seq * NS, NS, PAST, 32)]
                groups = [tiles]
                TK = PAST + NS; nkt = 33; x_d = xs; tok0 = NTOK + seq * NS
                o_ak, o_av, o_af, o_bk, o_bv = aks, avs, afs, bks, bvs
                TKB = LB + NS; nktb = 5; TQ = NS; nqt = 1
            qTa = SB(es, "qTa", [128, 4, TQ], BF16); kTa = SB(es, "kTa", [128, 4, TK], BF16)
            qTb = SB(es, "qTb", [128, 4, TQ], BF16); kTb = SB(es, "kTb", [128, 4, TKB], BF16)
            va = SB(es, "va", [128, nkt, NH, 65], BF16); vb = SB(es, "vb", [128, nktb, NH, 65], BF16)
            lf_tok = SB(es, "lf_tok", [128, nkt, NH]); c_tok = SB(es, "c_tok", [128, nkt, NH])
            crefbc = SB(es, "crefbc", [128, nqt, NH]); BIASA = SB(es, "BIASA", [128, NH, nkt, nqt])
            xt = [SB(es, "xt%d" % i, [128, D]) for i in range(2)]
            scr = SB(es, "scr", [128, D]); rs = SB(es, "rs", [128, 2])
            n1 = SB(es, "n1", [128, D], BF16); n1T = SB(es, "n1T", [128, 8, 512], BF16)
            wblk = [SB(es, "wblk%d" % i, [128, 8, 512], BF16) for i in range(2)]
            zf = SB(es, "zf", [128, 512]); zf2 = SB(es, "zf2", [128, 512]); zb = SB(es, "zb", [128, 512], BF16)
            ssq = SB(es, "ssq", [128, NH]); lft = SB(es, "lft", [128, NH])
            gbf = SB(es, "gbf", [128, 512], BF16)
            ptb = [SB(es, "pt%d" % i, [128, 512], BF16) for i in range(4)]
            rinv = SB(es, "rinv", [128, 512]); bcs = SB(es, "bcs", [64, 512]); yst = SB(es, "yst", [64, 512], BF16)
            S.op("pool", [], ["va"], lambda e: e.memset(va[:, :, :, 64:65], 1.0))
            S.op("pool", [], ["vb"], lambda e: e.memset(vb[:, :, :, 64:65], 1.0))

            def cumsum_tile(j, nt):
                first = (j == 0)
                if not first:
                    S.op("pe", [("c", j - 1), "sel127"], ["F5"], lambda e: e.matmul(Fb[5][:nt, 0:NH], lhsT=sel127[:, :nt], rhs=c_tok[:, j - 1, :], start=True, stop=False))
                S.op("pe", [("lf", j), "Umat"], ["F5"], lambda e: e.matmul(Fb[5][:nt, 0:NH], lhsT=Umat[:nt, :nt], rhs=lf_tok[:nt, j, :], start=first, stop=True))
                S.op("act", [], ["F5", ("c", j)], lambda e: e.copy(out=c_tok[:nt, j, :], in_=Fb[5][:nt, 0:NH]))

            def transp4(src, nt, dst, c0, dkey):
                for c in range(4):
                    S.op("pe", ["zb", "ident"], ["T1"], lambda e, c=c: e.transpose(out=Tb[1][:, c * 128:c * 128 + nt], in_=src[:nt, c * 128:(c + 1) * 128], identity=ident[:nt, :nt]))
                S.op("dve", [], ["T1", dkey], lambda e: e.tensor_copy(out=dst[:, :, c0:c0 + nt], in_=Tb[1][:, 0:512].rearrange("p (c t) -> p c t", c=4)[:, :, :nt]))

            if is_sample:
                for (ck, cv, kT_, v_, ntl, kn, vn) in ((cak, cav, kTa, va, 32, "kTa", "va"), (cbk, cbv, kTb, vb, 4, "kTb", "vb")):
                    for j in range(ntl):
                        S.dma([], ["xt0"], xt[0][:, 0:512], ck[seq, j * 128:(j + 1) * 128, :])
                        S.dma([], ["xt1"], xt[1][:, 0:512], cv[seq, j * 128:(j + 1) * 128, :])
                        S.op("act", ["xt0"], ["zb"], lambda e: e.copy(out=zb[:], in_=xt[0][:, 0:512]))
                        transp4(zb, 128, kT_, j * 128, (kn, j))
                        S.op("pool", ["xt1"], [(vn, j)], lambda e, v_=v_, j=j: e.tensor_copy(out=v_[:, j, :, 0:64], in_=xt[1][:, 0:512].rearrange("p (h d) -> p h d", h=NH)))
                S.dma([], [("lf", j) for j in range(32)], lf_tok[:, 0:32, :], calf[seq].rearrange("(j p) h -> p j h", p=128))
                for j in range(32):
                    cumsum_tile(j, 128)

            def qknorm(zp, zk, nt, gain, gk, dst, dk):
                S.op("act", [], [zk, "zf"], lambda e: e.activation(out=zf[:nt, :], in_=zp[:nt, :], func=AF.Square))
                S.op("dve", ["zf"], ["ssq"], lambda e: e.tensor_reduce(out=ssq[:nt, :], in_=zf[:nt, :].rearrange("p (h d) -> p h d", h=NH), axis=AX.X, op=ALU.add))
                S.op("act", ["ssq"], ["ssq"], lambda e: e.activation(out=ssq[:nt, :], in_=ssq[:nt, :], func=AF.Sqrt, bias=EPS, scale=1.0 / HD))
                S.op("dve", ["ssq"], ["ssq"], lambda e: e.reciprocal(out=ssq[:nt, :], in_=ssq[:nt, :]))
                S.op("dve", ["ssq"], [zk, "zf"], lambda e: e.tensor_tensor(out=zf[:nt, :].rearrange("p (h d) -> p h d", h=NH), in0=zp[:nt, :].rearrange("p (h d) -> p h d", h=NH),
                                                                           in1=ssq[:nt, :].unsqueeze(2).to_broadcast([nt, NH, HD]), op=ALU.mult))
                S.op("pool", ["zf", gk], [dk], lambda e: e.tensor_tensor(out=dst[:nt, :].rearrange("p (h d) -> p h d", h=NH), in0=zf[:nt, :].rearrange("p (h d) -> p h d", h=NH),
                                                                         in1=gain[:nt, :].unsqueeze(1).to_broadcast([nt, NH, HD]), op=ALU.mult))

            n1T2 = [n1T, SB(es, "n1Tb", [128, 8, 512], BF16)]
            n1b = [n1, SB(es, "n1b", [128, D], BF16)]
            zf_ = [zf, SB(es, "zfb", [128, 512])]; zf2_ = [zf2, SB(es, "zf2b", [128, 512])]; zb_ = [zb] + [SB(es, "zbb%d" % i, [128, 512], BF16) for i in range(3)]
            ssq_ = [ssq, SB(es, "ssqb", [128, NH])]; gbf_ = [gbf, SB(es, "gbfb", [128, 512], BF16)]
            pend = []
            ecnt = [0]; zrot = [0]; trot = [0]

            mmc = [0]

            def flush(lag=0):
                while pend and mmc[0] - pend[0][0] >= lag:
                    pend.pop(0)[1]()

            def zbank():
                k = zrot[0] % 5; zrot[0] += 1
                return Fb[k], "F%d" % k

            def P_nonpe(grp, gi):
                r0, nt, c0, ti = grp[gi]
                xb = xt[gi % 2]; xk = "xt%d" % (gi % 2); nb_ = n1b[gi % 2]; nk_ = "n1b%d" % (gi % 2)
                S.dma([], [xk], xb[:nt, :], x_d[r0:r0 + nt, :])
                rms_rstd(xk, xb[:nt, :], nt, D, rs, "rs", scr, "scr")
                S.op("dve", [xk, "rs", "gmix_bc"], [nk_], lambda e: e.scalar_tensor_tensor(out=nb_[:nt, :], in0=xb[:nt, :], scalar=rs[:nt, 0:1], in1=gmix_bc[:nt, :], op0=ALU.mult, op1=ALU.mult))

            def P_pe(grp, gi, g):
                r0, nt, c0, ti = grp[gi]
                nb_ = n1b[gi % 2]; nk_ = "n1b%d" % (gi % 2); nT = n1T2[g % 2]
                for kc in range(8):
                    S.op("pe", [nk_, "ident"], ["T0"], lambda e, kc=kc: e.transpose(out=Tb[0][:, kc * 128:kc * 128 + nt], in_=nb_[:nt, kc * 128:(kc + 1) * 128], identity=ident[:nt, :nt]))
                S.op("act", [], ["T0", ("n1T", g % 2, gi)], lambda e: e.copy(out=nT[:, :, gi * 128:gi * 128 + nt], in_=Tb[0][:, :].rearrange("p (k t) -> p k t", k=8)[:, :, :nt]))

            def L_(grp, gi, g):
                r0, nt, c0, ti = grp[gi]
                nT = n1T2[g % 2]
                zp, zk = zbank()
                for kc in range(8):
                    S.op("pe", [("n1T", g % 2, gi), "wfl_bf"], [zk], lambda e, kc=kc: e.matmul(zp[:nt, 0:NH], lhsT=nT[:, kc, gi * 128:gi * 128 + nt], rhs=wfl_bf[:, kc, :], start=(kc == 0), stop=(kc == 7)))
                S.op("dve", ["bf_bc"], [zk, "lft"], lambda e: e.tensor_tensor(out=lft[:nt, :], in0=zp[:nt, 0:NH], in1=bf_bc[:nt, :], op=ALU.add))
                S.op("act", ["lft"], ["lft"], lambda e: e.activation(out=lft[:nt, :], in_=lft[:nt, :], func=AF.Exp, scale=-1.0))
                S.op("act", ["lft"], ["lft"], lambda e: e.activation(out=lft[:nt, :], in_=lft[:nt, :], func=AF.Ln, bias=1.0, scale=1.0))
                S.op("dve", ["lft"], [("lf", ti)], lambda e: e.tensor_scalar(out=lf_tok[:nt, ti, :], in0=lft[:nt, :], scalar1=-1.0, scalar2=None, op0=ALU.mult))
                pend.append((mmc[0], lambda: cumsum_tile(ti, nt)))

            def qknorm2(zp, zk, nt, gain, gk, dst, dk, zfx, zfk, sq, sqk):
                S.op("act", [], [zk, zfk], lambda e: e.activation(out=zfx[:nt, :], in_=zp[:nt, :], func=AF.Square))
                S.op("dve", [zfk], [sqk], lambda e: e.tensor_reduce(out=sq[:nt, :], in_=zfx[:nt, :].rearrange("p (h d) -> p h d", h=NH), axis=AX.X, op=ALU.add))
                S.op("act", [sqk], [sqk], lambda e: e.activation(out=sq[:nt, :], in_=sq[:nt, :], func=AF.Ln, bias=EPS, scale=1.0 / HD))
                S.op("act", [sqk], [sqk], lambda e: e.activation(out=sq[:nt, :], in_=sq[:nt, :], func=AF.Exp, scale=-0.5))
                S.op("dve", [sqk], [zk, zfk], lambda e: e.tensor_tensor(out=zfx[:nt, :].rearrange("p (h d) -> p h d", h=NH), in0=zp[:nt, :].rearrange("p (h d) -> p h d", h=NH),
                                                                        in1=sq[:nt, :].unsqueeze(2).to_broadcast([nt, NH, HD]), op=ALU.mult))
                S.op("pool", [zfk, gk], [dk], lambda e: e.tensor_tensor(out=dst[:nt, :].rearrange("p (h d) -> p h d", h=NH), in0=zfx[:nt, :].rearrange("p (h d) -> p h d", h=NH),
                                                                       in1=gain[:nt, :].unsqueeze(1).to_broadcast([nt, NH, HD]), op=ALU.mult))

            def transp4d(src, sk_, nt, dst, c0_, dkey):
                k = trot[0] % 2; trot[0] += 1
                tb = Tb[k]; tk = "T%d" % k
                for c in range(4):
                    S.op("pe", [sk_, "ident"], [tk], lambda e, c=c: e.transpose(out=tb[:, c * 128:c * 128 + nt], in_=src[:nt, c * 128:(c + 1) * 128], identity=ident[:nt, :nt]))
                S.op("dve", [], [tk, dkey], lambda e: e.tensor_copy(out=dst[:, :, c0_:c0_ + nt], in_=tb[:, 0:512].rearrange("p (c t) -> p c t", c=4)[:, :, :nt]))

            wv = wi_bf.rearrange("(kc p) c -> p kc c", p=128)
            for gi in range(len(groups[0])):
                P_nonpe(groups[0], gi); P_pe(groups[0], gi, 0)
            for gi in range(len(groups[0])):
                L_(groups[0], gi, 0)
            for g, grp in enumerate(groups):
                nxt = groups[g + 1] if g + 1 < len(groups) else None
                nT = n1T2[g % 2]
                for bi in range(10):
                    if nxt is not None:
                        if 1 <= bi <= 4:
                            P_pe(nxt, bi - 1, g + 1)
                        if bi <= 3:
                            P_nonpe(nxt, bi)
                        if 2 <= bi <= 5:
                            L_(nxt, bi - 2, g + 1)
                    wb = wblk[bi % 2]; wk = "wblk%d" % (bi % 2)
                    S.dma([], [wk], wb[:], wv[:, :, bi * 512:(bi + 1) * 512])
                    for gi, (r0, nt, c0, ti) in enumerate(grp):
                        zp, zk = zbank()
                        for kc in range(8):
                            S.op("pe", [("n1T", g % 2, gi), wk], [zk], lambda e, kc=kc, gi=gi, nt=nt, zp=zp, wb=wb: e.matmul(zp[:nt, :], lhsT=nT[:, kc, gi * 128:gi * 128 + nt], rhs=wb[:, kc, :], start=(kc == 0), stop=(kc == 7)))
                        mmc[0] += 1
                        flush(3)
                        e_ = ecnt[0] % 2; e4 = ecnt[0] % 4; ecnt[0] += 1
                        zfx = zf_[e_]; zfk = "zf%d" % e_; zf2x = zf2_[e_]; zf2k = "zf2%d" % e_; zbx = zb_[e4]; zbk = "zb%d" % e4; sq = ssq_[e_]; sqk = "ssq%d" % e_
                        isA = bi < 3
                        if bi in (0, 3):
                            qknorm2(zp, zk, nt, qna_bc if isA else qnb_bc, "qna_bc" if isA else "qnb_bc", zbx, zbk, zfx, zfk, sq, sqk)
                            pend.append((mmc[0], lambda zbx=zbx, zbk=zbk, nt=nt, isA=isA, c0=c0, ti=ti: transp4d(zbx, zbk, nt, qTa if isA else qTb, c0 if not is_sample else 0, ("qTa" if isA else "qTb", ti))))
                        elif bi in (1, 4):
                            qknorm2(zp, zk, nt, kna_bc if isA else knb_bc, "kna_bc" if isA else "knb_bc", zf2x, zf2k, zfx, zfk, sq, sqk)
                            S.op("act", [zf2k], [zbk], lambda e, nt=nt, zbx=zbx, zf2x=zf2x: e.copy(out=zbx[:nt, :], in_=zf2x[:nt, :]))
                            kc0 = c0 if (isA or not is_sample) else LB
                            pend.append((mmc[0], lambda zbx=zbx, zbk=zbk, nt=nt, isA=isA, kc0=kc0, ti=ti: transp4d(zbx, zbk, nt, kTa if isA else kTb, kc0, ("kTa" if isA else "kTb", ti))))
                            if isA or is_sample:
                                S.dma([zf2k], [], (o_ak if isA else o_bk)[r0:r0 + nt, :], zf2x[:nt, :], q="pool")
                            elif ti >= 12:
                                S.dma([zf2k], [], o_bk[seq * LB + (ti - 12) * 128:seq * LB + (ti - 11) * 128, :], zf2x[:nt, :], q="pool")
                        elif bi in (2, 5):
                            S.op("act", [], [zk, zf2k], lambda e, nt=nt, zp=zp, zf2x=zf2x: e.copy(out=zf2x[:nt, :], in_=zp[:nt, :]))
                            v_ = va if isA else vb
                            tj = ti if (isA or not is_sample) else 4
                            S.op("pool", [zf2k], [("va" if isA else "vb", tj)], lambda e, nt=nt, v_=v_, tj=tj, zf2x=zf2x: e.tensor_copy(out=v_[:nt, tj, :, 0:64], in_=zf2x[:nt, :].rearrange("p (h d) -> p h d", h=NH)))
                            if isA or is_sample:
                                S.dma([zf2k], [], (o_av if isA else o_bv)[r0:r0 + nt, :], zf2x[:nt, :], q="pool")
                            elif ti >= 12:
                                S.dma([zf2k], [], o_bv[seq * LB + (ti - 12) * 128:seq * LB + (ti - 11) * 128, :], zf2x[:nt, :], q="pool")
                        else:
                            gx = gbf_[e_]; gk_ = "gbf%d" % e_
                            S.op("act", [], [zk, gk_], lambda e, nt=nt, zp=zp, gx=gx: e.activation(out=gx[:nt, :], in_=zp[:nt, :], func=AF.Sigmoid))
                            tr0 = tok0 + (c0 if not is_sample else 0)
                            S.dma([gk_], [], gate_s[tr0:tr0 + nt, (bi - 6) * 512:(bi - 5) * 512], gx[:nt, :], q="pool")
            flush()
            if not is_sample:
                S.dma([("lf", j) for j in range(16)], [], o_af[seq * T:(seq + 1) * T, :].rearrange("(j p) h -> p j h", p=128), lf_tok[:, 0:16, :])
            else:
                S.dma([("lf", 32)], [], o_af[seq * NS:(seq + 1) * NS, :], lf_tok[:NS, 32, :])

            S.barrier()
            ckeys = [("c", j) for j in range(nkt)]
            if not is_sample:
                S.op("pe", ckeys + ["sel127"], ["F5"], lambda e: e.matmul(Fb[5][:, 0:nkt * NH], lhsT=sel127[:, :], rhs=c_tok[:, :, :].rearrange("p j h -> p (j h)"), start=True, stop=True))
            else:
                S.op("pe", ckeys + ["sel15"], ["F5"], lambda e: e.matmul(Fb[5][:, 0:NH], lhsT=sel15[:NS, :], rhs=c_tok[:NS, 32, :], start=True, stop=True))
            S.op("act", [], ["F5", "crefbc"], lambda e: e.copy(out=crefbc[:, :, :].rearrange("p i h -> p (i h)"), in_=Fb[5][:, 0:nqt * NH]))
            for h in range(NH):
                S.op("dve", ckeys + ["crefbc"], [("BIASA", h)], lambda e, h=h: e.tensor_tensor(out=BIASA[:, h], in0=crefbc[:, :, h].unsqueeze(1).to_broadcast([128, nkt, nqt]),
                                                                                              in1=c_tok[:, :, h].unsqueeze(2).to_broadcast([128, nkt, nqt]), op=ALU.subtract))
                S.op("dve", ["negC", ("BIASA", h)], [("BIASA", h)], lambda e, h=h: e.tensor_scalar(out=BIASA[:, h], in0=BIASA[:, h], scalar1=negC[:, 0:1], scalar2=None, op0=ALU.add))

            cnt = [0]

            def normalize(c, h, qn, q0, dst_s, kb):
                acck = "F%d" % c
                sk = "F%d" % (4 + kb % 4); sp_ = Fb[4 + kb % 4]
                S.op("act", [], [acck, "rinv"], lambda e: e.activation(out=rinv[64:65, :qn], in_=Fb[c][64:65, :qn], func=AF.Ln))
                S.op("act", ["rinv"], ["rinv"], lambda e: e.activation(out=rinv[64:65, :qn], in_=rinv[64:65, :qn], func=AF.Exp, scale=-1.0))
                S.op("pe", ["rinv", "ones_f"], [sk], lambda e: e.matmul(sp_[:64, :qn], lhsT=ones_f[64:65, 0:64], rhs=rinv[64:65, :qn], start=True, stop=True))
                S.op("act", [], [sk, "bcs"], lambda e: e.copy(out=bcs[:64, :qn], in_=sp_[:64, :qn]))
                S.op("dve", ["bcs"], [acck, "yst"], lambda e: e.tensor_tensor(out=yst[:64, :qn], in0=Fb[c][:64, :qn], in1=bcs[:64, :qn], op=ALU.mult))
                S.dma(["yst"], [], dst_s[h * 64:(h + 1) * 64, tok0 + q0:tok0 + q0 + qn], yst[:64, :qn])

            qchunks = [(c * 512, 512) for c in range(4)] if not is_sample else [(0, NS)]
            stepsA = []
            for h in range(NH):
                for c, (q0, qn) in enumerate(qchunks):
                    jlast = (4 * c + 3) if not is_sample else 32
                    for j in range(jlast + 1):
                        stepsA.append((h, c, q0, qn, jlast, j))

            def A_qk(st):
                h, c, q0, qn, jlast, j = st
                hp, ho = h // 2, (h % 2) * 64
                nk = 128 if (not is_sample or j < 32) else NS
                qlo = max(q0, j * 128) if not is_sample else 0
                n = q0 + qn - qlo
                k = cnt[0]; cnt[0] += 1
                sk = "F%d" % (4 + k % 4); sp_ = Fb[4 + k % 4]
                S.op("pe", [("kTa", j), ("qTa", 0)], [sk], lambda e: e.matmul(sp_[:nk, :n], lhsT=kTa[ho:ho + 64, hp, j * 128:j * 128 + nk], rhs=qTa[ho:ho + 64, hp, qlo:qlo + n], start=True, stop=True))
                return (k, nk, qlo, n)

            def A_rest(st, info):
                h, c, q0, qn, jlast, j = st
                k, nk, qlo, n = info
                acck = "F%d" % c
                sk = "F%d" % (4 + k % 4); sp_ = Fb[4 + k % 4]
                pt = ptb[k % 4]; ptk = "pt%d" % (k % 4)
                nb_ = max(1, n // 128)
                for bq in range(nb_):
                    bw = min(128, n)
                    i = (qlo // 128 + bq) if not is_sample else 0
                    S.op("act", [("BIASA", h)], [sk, ptk], lambda e, bq=bq, bw=bw, i=i: e.activation(out=pt[:nk, bq * 128:bq * 128 + bw], in_=sp_[:nk, bq * 128:bq * 128 + bw], func=AF.Exp, bias=BIASA[:nk, h, j, i:i + 1], scale=1.0))
                diag = (j * 128 >= q0) if not is_sample else (j == 32)
                if diag:
                    bw = min(128, n)
                    S.op("pool", ["tri", ptk], [ptk], lambda e: e.tensor_tensor(out=pt[:nk, 0:bw], in0=pt[:nk, 0:bw], in1=tri[:nk, :bw], op=ALU.mult))
                S.op("pe", [ptk, ("va", j)], [acck], lambda e: e.matmul(Fb[c][:65, qlo - q0:qlo - q0 + n], lhsT=va[:nk, j, h, :], rhs=pt[:nk, :n], start=(j == 0), stop=(j == jlast), skip_group_check=True))
                if j == jlast:
                    pendN.append([0 if is_sample else 2, (c, h, qn, q0, yTa_s)])
                for pn in list(pendN):
                    if pn[0] == 0:
                        normalize(*pn[1], k); pendN.remove(pn)
                    else:
                        pn[0] -= 1

            LA = 2
            pendN = []
            infoA = [A_qk(stepsA[i]) for i in range(min(LA, len(stepsA)))]
            for si, st in enumerate(stepsA):
                if si + LA < len(stepsA):
                    infoA.append(A_qk(stepsA[si + LA]))
                A_rest(st, infoA[si])
            for pn in pendN:
                normalize(*pn[1], infoA[-1][0])

            blocksB = []
            for h in range(NH):
                nqi = 16 if not is_sample else 1
                for i in range(nqi):
                    js = list(range(max(0, i - 4), i + 1)) if not is_sample else list(range(5))
                    for j in js:
                        if not is_sample:
                            vi = {0: 0, 1: 1, 2: 2, 3: 2, 4: 3}[i - j]; nk = 128
                        else:
                            vi = {0: 2, 1: 2, 2: 2, 3: 1, 4: 0}[j]; nk = 128 if j < 4 else NS
                        blocksB.append((h, i, js, j, vi, nk))
            qwB = 128 if not is_sample else NS
            batchesB = []
            for blk in blocksB:
                if batchesB and len(batchesB[-1]) < 4 and batchesB[-1][0][5] == blk[5]:
                    batchesB[-1].append(blk)
                else:
                    batchesB.append([blk])

            def B_qk(batch):
                k = cnt[0]; cnt[0] += 1
                sk = "F%d" % (4 + k % 4); sp_ = Fb[4 + k % 4]
                for bx, (h, i, js, j, vi, nk) in enumerate(batch):
                    hp, ho = h // 2, (h % 2) * 64
                    S.op("pe", ["ident", "bias_bf"], [sk], lambda e, bx=bx, h=h, vi=vi, nk=nk: e.matmul(sp_[:nk, bx * qwB:(bx + 1) * qwB], lhsT=ident[:nk, :nk], rhs=bias_bf[:nk, h, vi, 0:qwB], start=True, stop=False, skip_group_check=True))
                    S.op("pe", [("kTb", j), ("qTb", 0)], [sk], lambda e, bx=bx, hp=hp, ho=ho, i=i, j=j, nk=nk: e.matmul(sp_[:nk, bx * qwB:(bx + 1) * qwB], lhsT=kTb[ho:ho + 64, hp, j * 128:j * 128 + nk], rhs=qTb[ho:ho + 64, hp, i * 128:i * 128 + qwB], start=False, stop=True, skip_group_check=True))
                return k

            def B_rest(batch, k):
                sk = "F%d" % (4 + k % 4); sp_ = Fb[4 + k % 4]
                pt = ptb[k % 4]; ptk = "pt%d" % (k % 4)
                nk = batch[0][5]; wtot = len(batch) * qwB
                S.op("act", ["negC"], [sk, ptk], lambda e: e.activation(out=pt[:nk, :wtot], in_=sp_[:nk, :wtot], func=AF.Exp, bias=negC[:nk, 1:2], scale=1.0))
                for bx, (h, i, js, j, vi, nk_) in enumerate(batch):
                    c = i // 4
                    S.op("pe", [ptk, ("vb", j)], ["F%d" % c], lambda e, bx=bx, h=h, i=i, j=j, js=js, c=c: e.matmul(Fb[c][:65, (i % 4) * 128:(i % 4) * 128 + qwB], lhsT=vb[:nk, j, h, :], rhs=pt[:nk, bx * qwB:(bx + 1) * qwB], start=(j == js[0]), stop=(j == js[-1]), skip_group_check=True))
                    if j == js[-1]:
                        if is_sample:
                            pendNB.append([0, (0, h, NS, 0, yTb_s)])
                        elif i % 4 == 3:
                            pendNB.append([1, (c, h, 512, c * 512, yTb_s)])
                for pn in list(pendNB):
                    if pn[0] == 0:
                        normalize(*pn[1], k); pendNB.remove(pn)
                    else:
                        pn[0] -= 1

            pendNB = []
            kB = B_qk(batchesB[0])
            for si, bt in enumerate(batchesB):
                kn = B_qk(batchesB[si + 1]) if si + 1 < len(batchesB) else None
                B_rest(bt, kB)
                klast = kB
                kB = kn
            for pn in pendNB:
                normalize(*pn[1], klast)

        for seq in range(NSEQ):
            with ExitStack() as es:
                run_sequence(es, seq, False)
                S.barrier()
        for seq in range(NSAMP):
            with ExitStack() as es:
                run_sequence(es, seq, True)
                S.barrier()

        alltiles = [(r, 128) for r in range(0, NTOK, 128)] + ([(NTOK, NSAMP * NS)] if NSAMP else [])

        def xrows(r0, nt):
            return xp[r0:r0 + nt, :] if r0 < NTOK else xs[0:nt, :]

        with ExitStack() as es:
            wa = SB(es, "wa", [128, 4, D], BF16); wb_ = SB(es, "wb_", [128, 4, D], BF16); wo = SB(es, "wo", [128, 8, D], BF16)
            S.dma([], ["wa"], wa[:], wupa_bf.rearrange("(kc p) c -> p kc c", p=128))
            S.dma([], ["wb_"], wb_[:], wupb_bf.rearrange("(kc p) c -> p kc c", p=128))
            S.dma([], ["wo"], wo[:], wout_bf.rearrange("(kc p) c -> p kc c", p=128))
            yta = [SB(es, "yta%d" % i, [128, 4, 128], BF16) for i in range(2)]
            ytb = [SB(es, "ytb%d" % i, [128, 4, 128], BF16) for i in range(2)]
            gsb = [SB(es, "gsb%d" % i, [128, 2048], BF16) for i in range(2)]
            xm = [SB(es, "xm%d" % i, [128, D]) for i in range(2)]
            m1 = SB(es, "m1", [128, D]); m2 = SB(es, "m2", [128, D]); mb2 = [SB(es, "mb%d" % i, [128, D], BF16) for i in range(2)]
            mT = SB(es, "mT", [128, 8, 128], BF16); h1t = [SB(es, "h1t%d" % i, [128, D]) for i in range(2)]
            def mergeA(it):
                r0, nt = alltiles[it]
                b = it % 2
                S.dma([], ["yta%d" % b], yta[b][:, :, :nt], yTa_s[:, r0:r0 + nt].rearrange("(kc p) t -> p kc t", p=128))
                S.dma([], ["ytb%d" % b], ytb[b][:, :, :nt], yTb_s[:, r0:r0 + nt].rearrange("(kc p) t -> p kc t", p=128))
                S.dma([], ["gsb%d" % b], gsb[b][:nt, :], gate_s[r0:r0 + nt, :])
                S.dma([], ["xm%d" % b], xm[b][:nt, :], xrows(r0, nt))
                for br, (yt, ytk, w_, wk) in enumerate(((yta[b], "yta%d" % b, wa, "wa"), (ytb[b], "ytb%d" % b, wb_, "wb_"))):
                    for half in range(2):
                        bank = br * 2 + half
                        for kc in range(4):
                            S.op("pe", [ytk, wk], ["F%d" % bank], lambda e, yt=yt, w_=w_, kc=kc, half=half, bank=bank, nt=nt: e.matmul(Fb[bank][:nt, :], lhsT=yt[:, kc, :nt], rhs=w_[:, kc, half * 512:(half + 1) * 512], start=(kc == 0), stop=(kc == 3)))
                for half in range(2):
                    S.op("dve", ["gsb%d" % b], ["F%d" % half, "m1"], lambda e, half=half, nt=nt, b=b: e.tensor_tensor(out=m1[:nt, half * 512:(half + 1) * 512], in0=Fb[half][:nt, :], in1=gsb[b][:nt, half * 512:(half + 1) * 512], op=ALU.mult))
                    S.op("dve", ["gsb%d" % b], ["F%d" % (2 + half), "m2"], lambda e, half=half, nt=nt, b=b: e.tensor_tensor(out=m2[:nt, half * 512:(half + 1) * 512], in0=Fb[2 + half][:nt, :], in1=gsb[b][:nt, 1024 + half * 512:1024 + (half + 1) * 512], op=ALU.mult))
                S.op("pool", ["m1", "m2"], ["mb%d" % b], lambda e, nt=nt, b=b: e.tensor_tensor(out=mb2[b][:nt, :], in0=m1[:nt, :], in1=m2[:nt, :], op=ALU.add))

            def mergeB(it):
                r0, nt = alltiles[it]
                b = it % 2
                for kc in range(8):
                    S.op("pe", ["mb%d" % b, "ident"], ["T0"], lambda e, kc=kc, nt=nt, b=b: e.transpose(out=Tb[0][:, kc * 128:kc * 128 + nt], in_=mb2[b][:nt, kc * 128:(kc + 1) * 128], identity=ident[:nt, :nt]))
                S.op("act", [], ["T0", "mT"], lambda e, nt=nt: e.copy(out=mT[:, :, :nt], in_=Tb[0][:, :].rearrange("p (k t) -> p k t", k=8)[:, :, :nt]))
                for half in range(2):
                    for kc in range(8):
                        S.op("pe", ["mT", "wo"], ["F%d" % (4 + half)], lambda e, kc=kc, half=half, nt=nt: e.matmul(Fb[4 + half][:nt, :], lhsT=mT[:, kc, :nt], rhs=wo[:, kc, half * 512:(half + 1) * 512], start=(kc == 0), stop=(kc == 7)))
                    S.op("dve", ["xm%d" % b], ["F%d" % (4 + half), "h1t%d" % b], lambda e, half=half, nt=nt, b=b: e.tensor_tensor(out=h1t[b][:nt, half * 512:(half + 1) * 512], in0=Fb[4 + half][:nt, :], in1=xm[b][:nt, half * 512:(half + 1) * 512], op=ALU.add))
                S.dma(["h1t%d" % b], [], (h1_s if PEER else y)[r0:r0 + nt, :], h1t[b][:nt, :])
            mergeA(0)
            for it in range(len(alltiles)):
                if it + 1 < len(alltiles):
                    mergeA(it + 1)
                mergeB(it)
            S.barrier()

        es1.close()
        if PEER:
          NB = 256
          blocks = []
          for b0 in range(0, NALL, NB):
              nb = min(NB, NALL - b0)
              blocks.append((b0, nb, [(b0 + o, min(128, nb - o), o) for o in range(0, nb, 128)]))
          with ExitStack() as es:
            wsh = SB(es, "wsh", [128, 8, D], BF16); wpp_sb = SB(es, "wpp_sb", [128, 2, D], BF16)
            gffn_bc = SB(es, "gffn_bc", [128, D]); gple_bc = SB(es, "gple_bc", [128, D])
            S.dma([], ["gffn_bc"], gffn_bc[:], g_ffn.partition_broadcast(128))
            S.dma([], ["gple_bc"], gple_bc[:], g_ple.partition_broadcast(128))
            S.dma([], ["wpp_sb"], wpp_sb[:], wpp_bf.rearrange("(kc p) c -> p kc c", p=128))
            Kbd = SB(es, "Kbd", [128, NH, 256], BF16)
            with ExitStack() as es2:
                Kst = SB(es2, "Kst", [128, NH, 256])
                S.op("pool", [], ["Kst"], lambda e: e.memset(Kst[:], 0.0))
                S.dma([], ["Kst"], Kst[0:64, :, 0:128], skT[:, 0].rearrange("h c n -> c h n"))
                S.dma([], ["Kst"], Kst[64:128, :, 128:256], skT[:, 1].rearrange("h c n -> c h n"))
                S.op("dve", ["Kst"], ["Kbd"], lambda e: e.tensor_copy(out=Kbd[:], in_=Kst[:]))
                S.barrier()
            h1b = [SB(es, "h1b%d" % i, [128, 2, D]) for i in range(2)]
            n2T = [SB(es, "n2T%d" % i, [128, 8, NB], BF16) for i in range(2)]
            idxT = [SB(es, "idxT%d" % i, [128, 3, NB]) for i in range(2)]
            rs = SB(es, "rs2", [128, 2]); rs3 = SB(es, "rs3", [128, 2])
            n2 = SB(es, "n2", [128, D], BF16); pqT = SB(es, "pqT", [128, NH, NB], BF16)
            sc = SB(es, "sc", [128, 2048]); wk = SB(es, "wk", [128, 2048])
            ts = SB(es, "ts", [128, 16, 16]); ti_ = SB(es, "ti_", [128, 16, 16], U32); tif = SB(es, "tif", [128, 16, 16])
            cand = SB(es, "cand", [128, NH, 256]); bs = SB(es, "bs", [128, NH, 16]); pos = SB(es, "pos", [128, NH, 16], U32)
            pa = SB(es, "pa", [128, NH, 16], U32); pb = SB(es, "pb", [128, NH, 16], U32); paf = SB(es, "paf", [128, NH, 16]); pbf = SB(es, "pbf", [128, NH, 16])
            iidx2 = [SB(es, "iidx%d" % i, [128, 128]) for i in range(2)]; jidx2 = [SB(es, "jidx%d" % i, [128, 128]) for i in range(2)]; gg2 = [SB(es, "gg%d" % i, [128, NH, 16]) for i in range(2)]; gs = SB(es, "gs", [128, NH])
            gh = SB(es, "gh", [128, 128, NB], BF16)
            sbuf_uv = [SB(es, "uv%d" % i, [128, 4, D], BF16) for i in range(3)]
            AB = [SB(es, "AB%d" % i, [128, 2, 4, 128], BF16) for i in range(2)]
            wsb = [SB(es, "wsb%d" % i, [128, 512], BF16) for i in range(2)]
            n3 = SB(es, "n3", [128, D], BF16); n3T = SB(es, "n3T", [128, 8, 128], BF16); gt = SB(es, "gt", [128, D])
            pin = SB(es, "pin", [128, 256]); pbf16 = SB(es, "pbf16", [128, 256], BF16); pT = SB(es, "pT", [128, 2, 128], BF16)
            yo = gt
            uvv = u_bf.rearrange("(j p) c -> p j c", p=128); vvv = v_bf.rearrange("(j p) c -> p j c", p=128)
            iota_bf = SB(es, "iota_bf", [128, 128], BF16)
            S.op("dve", ["iota_f"], ["iota_bf"], lambda e: e.tensor_copy(out=iota_bf[:], in_=iota_f[:]))
            niota_bf = SB(es, "niota_bf", [128, 128], BF16)
            S.op("dve", ["iota_f"], ["niota_bf"], lambda e: e.tensor_scalar(out=niota_bf[:], in0=iota_f[:], scalar1=-1.0, scalar2=None, op0=ALU.mult))
            rot = [0]
            uvn = [0]

            def rms_rstd_act(src_key, src_ap, nt, acc, acck, junk, junkk):
                S.op("act", [src_key], [junkk, acck], lambda e: e.activation(out=junk[:nt, :], in_=src_ap, func=AF.Square, scale=float(D) ** -0.5, accum_out=acc[:nt, 0:1]))
                S.op("act", [acck], [acck], lambda e: e.activation(out=acc[:nt, 0:1], in_=acc[:nt, 0:1], func=AF.Ln, bias=EPS, scale=1.0))
                S.op("act", [acck], [acck], lambda e: e.activation(out=acc[:nt, 0:1], in_=acc[:nt, 0:1], func=AF.Exp, scale=-0.5))

            def bank(lo=0, n=6):
                b = lo + rot[0] % n
                rot[0] += 1
                return Fb[b], "F%d" % b

            def FE_a1(bi, tl_):
                b0, nb, tls = blocks[bi]
                pbi = bi % 2
                hb = h1b[pbi]; nT = n2T[pbi]; iT = idxT[pbi]
                for (r0, nt, o) in tls[tl_:tl_ + 1]:
                    tl = o // 128
                    S.dma([], [("h1b", pbi, tl)], hb[:nt, tl, :], h1_s[r0:r0 + nt, :], q="pool")
                    rms_rstd_act(("h1b", pbi, tl), hb[:nt, tl, :], nt, rs, "rs2", n2, "n2")
                    S.op("act", [("h1b", pbi, tl), "rs2"], ["n2"], lambda e, nt=nt, tl=tl: e.activation(out=n2[:nt, :], in_=hb[:nt, tl, :], func=AF.Identity, scale=rs[:nt, 0:1]))
                    S.op("pool", ["n2", "gffn_bc"], ["n2"], lambda e, nt=nt: e.tensor_tensor(out=n2[:nt, :], in0=n2[:nt, :], in1=gffn_bc[:nt, :], op=ALU.mult))

            def FE_a2(bi, tl_):
                b0, nb, tls = blocks[bi]
                pbi = bi % 2
                nT = n2T[pbi]
                for (r0, nt, o) in tls[tl_:tl_ + 1]:
                    for kc in range(8):
                        S.op("pe", ["n2", "ident"], ["T0"], lambda e, kc=kc, nt=nt: e.transpose(out=Tb[0][:, kc * 128:kc * 128 + nt], in_=n2[:nt, kc * 128:(kc + 1) * 128], identity=ident[:nt, :nt]))
                    S.op("act", [], ["T0", ("n2T", pbi)], lambda e, nt=nt, o=o: e.copy(out=nT[:, :, o:o + nt], in_=Tb[0][:, :].rearrange("p (k t) -> p k t", k=8)[:, :, :nt]))

            def FE_a3(bi):
                b0, nb, tls = blocks[bi]
                pbi = bi % 2
                nT = n2T[pbi]
                S.dma([], ["wsh"], wsh[:], wq_bf.rearrange("(kc p) c -> p kc c", p=128), q="pool")
                for h in range(NH):
                    bp, bk_ = bank(4, 2)
                    for kc in range(8):
                        S.op("pe", [("n2T", pbi), "wsh"], [bk_], lambda e, bp=bp, kc=kc, h=h: e.matmul(bp[:, :nb], lhsT=wsh[:, kc, h * 128:(h + 1) * 128], rhs=nT[:, kc, :nb], start=(kc == 0), stop=(kc == 7)))
                    S.op("act", [], [bk_, ("pqT", h)], lambda e, bp=bp, h=h: e.copy(out=pqT[:, h, :nb], in_=bp[:, :nb]))

            def FE_t(bi, tl_):
                b0, nb, tls = blocks[bi]
                pbi = bi % 2
                if tl_ >= len(tls):
                    return
                iidx = iidx2[tl_]; jidx = jidx2[tl_]; gg = gg2[tl_]
                IK = "iidx%d" % tl_; JK = "jidx%d" % tl_; GK = "gg%d" % tl_
                for (r0, nt, o) in tls[tl_:tl_ + 1]:
                    for h in range(NH):
                        bp, bk_ = Fb[4 + (h // 2) % 2], "F%d" % (4 + (h // 2) % 2)
                        S.op("pe", [("pqT", h), "Kbd"], [bk_], lambda e, bp=bp, h=h, nt=nt, o=o: e.matmul(bp[:nt, (h % 2) * 256:(h % 2) * 256 + 256], lhsT=pqT[:, h, o:o + nt], rhs=Kbd[:, h, :], start=True, stop=True, skip_group_check=True))
                        if h % 2 == 1:
                            S.op("act", [], [bk_, "sc"], lambda e, bp=bp, h=h, nt=nt: e.copy(out=sc[:nt, (h // 2) * 512:(h // 2 + 1) * 512], in_=bp[:nt, :]))
                    for s_ in range(16):
                        seg = sc[:nt, s_ * 128:(s_ + 1) * 128]; wseg = wk[:nt, s_ * 128:(s_ + 1) * 128]
                        S.op("dve", ["sc"], ["ts"], lambda e, seg=seg, s_=s_, nt=nt: e.max(out=ts[:nt, s_, 0:8], in_=seg))
                        S.op("dve", ["sc", "ts"], ["ti_"], lambda e, seg=seg, s_=s_, nt=nt: e.max_index(out=ti_[:nt, s_, 0:8], in_max=ts[:nt, s_, 0:8], in_values=seg))
                        S.op("dve", ["sc", "ts"], ["wk"], lambda e, seg=seg, wseg=wseg, s_=s_, nt=nt: e.match_replace(out=wseg, in_to_replace=ts[:nt, s_, 0:8], in_values=seg, imm_value=-1e30))
                        S.op("dve", ["wk"], ["ts"], lambda e, wseg=wseg, s_=s_, nt=nt: e.max(out=ts[:nt, s_, 8:16], in_=wseg))
                        S.op("dve", ["wk", "ts"], ["ti_"], lambda e, wseg=wseg, s_=s_, nt=nt: e.max_index(out=ti_[:nt, s_, 8:16], in_max=ts[:nt, s_, 8:16], in_values=wseg))
                    ts4 = ts[:nt].rearrange("p (h two) k -> p h two k", two=2)
                    S.op("dve", ["ts"], ["cand"], lambda e, nt=nt, ts4=ts4: e.tensor_tensor(out=cand[:nt].rearrange("p h (a b) -> p h a b", a=16), in0=ts4[:, :, 0, :].unsqueeze(3).to_broadcast([nt, NH, 16, 16]),
                                                                                       in1=ts4[:, :, 1, :].unsqueeze(2).to_broadcast([nt, NH, 16, 16]), op=ALU.add))
                    for h in range(NH):
                        cs = cand[:nt, h, :]; cws = wk[:nt, h * 256:(h + 1) * 256]
                        S.op("dve", ["cand"], ["bs"], lambda e, cs=cs, h=h, nt=nt: e.max(out=bs[:nt, h, 0:8], in_=cs))
                        S.op("dve", ["cand", "bs"], ["pos"], lambda e, cs=cs, h=h, nt=nt: e.max_index(out=pos[:nt, h, 0:8], in_max=bs[:nt, h, 0:8], in_values=cs))
                        S.op("dve", ["cand", "bs"], ["wk"], lambda e, cs=cs, cws=cws, h=h, nt=nt: e.match_replace(out=cws, in_to_replace=bs[:nt, h, 0:8], in_values=cs, imm_value=-1e30))
                        S.op("dve", ["wk"], ["bs"], lambda e, cws=cws, h=h, nt=nt: e.max(out=bs[:nt, h, 8:16], in_=cws))
                        S.op("dve", ["wk", "bs"], ["pos"], lambda e, cws=cws, h=h, nt=nt: e.max_index(out=pos[:nt, h, 8:16], in_max=bs[:nt, h, 8:16], in_values=cws))
                    S.op("dve", ["pos"], ["pa"], lambda e, nt=nt: e.tensor_single_scalar(out=pa[:nt], in_=pos[:nt], scalar=4, op=ALU.logical_shift_right))
                    S.op("dve", ["pos"], ["pb"], lambda e, nt=nt: e.tensor_single_scalar(out=pb[:nt], in_=pos[:nt], scalar=15, op=ALU.bitwise_and))
                    S.op("dve", ["pa"], ["paf"], lambda e, nt=nt: e.tensor_copy(out=paf[:nt], in_=pa[:nt]))
                    S.op("dve", ["pb"], ["pbf"], lambda e, nt=nt: e.tensor_copy(out=pbf[:nt], in_=pb[:nt]))
                    S.op("dve", ["ti_"], ["tif"], lambda e, nt=nt: e.tensor_copy(out=tif[:nt], in_=ti_[:nt]))
                    tif4 = tif[:nt].rearrange("p (h two) k -> p h two k", two=2)
                    io16 = iota_f[:nt, 0:16].unsqueeze(1).unsqueeze(1).to_broadcast([nt, NH, 16, 16])
                    oh = sc[:nt, :].rearrange("p (h k a) -> p h k a", h=NH, k=16)
                    for (pf, pk, two, dst, dk) in ((paf, "paf", 0, iidx, IK), (pbf, "pbf", 1, jidx, JK)):
                        S.op("dve", [pk], ["sc"], lambda e, pf=pf, nt=nt, oh=oh, io16=io16: e.tensor_tensor(out=oh, in0=pf[:nt].unsqueeze(3).to_broadcast([nt, NH, 16, 16]), in1=io16, op=ALU.is_equal))
                        S.op("dve", ["tif", "sc"], ["sc"], lambda e, two=two, nt=nt, oh=oh, tif4=tif4: e.tensor_tensor(out=oh, in0=oh, in1=tif4[:, :, two, :].unsqueeze(2).to_broadcast([nt, NH, 16, 16]), op=ALU.mult))
                        S.op("dve", ["sc"], [dk], lambda e, dst=dst, nt=nt, oh=oh: e.tensor_reduce(out=dst[:nt, :].rearrange("p (h k) -> p h k", h=NH), in_=oh, axis=AX.X, op=ALU.add))
                    S.op("dve", ["bs"], [GK], lambda e, nt=nt: e.tensor_tensor(out=gg[:nt], in0=bs[:nt], in1=bs[:nt, :, 0:1].to_broadcast([nt, NH, 16]), op=ALU.subtract))

            def FE_x(bi, tl_):
                b0, nb, tls = blocks[bi]
                pbi = bi % 2
                iT = idxT[pbi]
                if tl_ >= len(tls):
                    return
                iidx = iidx2[tl_]; jidx = jidx2[tl_]; gg = gg2[tl_]
                GK = "gg%d" % tl_
                for (r0, nt, o) in tls[tl_:tl_ + 1]:
                    S.op("act", [GK], [GK], lambda e, nt=nt: e.activation(out=gg[:nt], in_=gg[:nt], func=AF.Exp))
                    S.op("dve", [GK], ["gs"], lambda e, nt=nt: e.tensor_reduce(out=gs[:nt], in_=gg[:nt], axis=AX.X, op=ALU.add))
                    S.op("dve", ["gs"], ["gs"], lambda e, nt=nt: e.reciprocal(out=gs[:nt], in_=gs[:nt]))
                    S.op("dve", ["gs", GK], [GK], lambda e, nt=nt: e.tensor_tensor(out=gg[:nt], in0=gg[:nt], in1=gs[:nt].unsqueeze(2).to_broadcast([nt, NH, 16]), op=ALU.mult))
                    bp, bk_ = bank(4, 2)
                    for q_, (src, sk_) in enumerate(((iidx[:nt, :], "iidx%d" % tl_), (jidx[:nt, :], "jidx%d" % tl_), (gg[:nt].rearrange("p h k -> p (h k)"), "gg%d" % tl_))):
                        S.op("pe", [sk_, "identf"], [bk_], lambda e, bp=bp, q_=q_, src=src, nt=nt: e.transpose(out=bp[:, q_ * 128:q_ * 128 + nt], in_=src, identity=identf[:nt, :nt]))
                    S.op("act", [], [bk_, ("idxT", pbi)], lambda e, bp=bp, nt=nt, o=o: e.copy(out=iT[:, 1:3, o:o + nt], in_=bp[:, 128:384].rearrange("p (q t) -> p q t", q=2)[:, :, :nt]))
                    S.op("act", [], [bk_, ("idxT", pbi)], lambda e, bp=bp, nt=nt, o=o: e.activation(out=iT[:, 0, o:o + nt], in_=bp[:, 0:nt], func=AF.Identity, scale=-1.0))

            def BE_hid(bi, inject={}):
                b0, nb, tls = blocks[bi]
                pbi = bi % 2
                hb = h1b[pbi]; nT = n2T[pbi]; iT = idxT[pbi]
                for j0 in range(0, 128, 4):
                    for fn in inject.get(j0 // 4, []):
                        fn()
                    ub = sbuf_uv[uvn[0] % 3]; uk = "uv%d" % (uvn[0] % 3); uvn[0] += 1
                    S.dma([], [uk], ub[:], uvv[:, j0:j0 + 4, :])
                    for jj in range(4):
                        bp, bk_ = bank(0, 4)
                        for dc in range(8):
                            S.op("pe", [uk, ("n2T", pbi)], [bk_], lambda e, bp=bp, ub=ub, jj=jj, dc=dc: e.matmul(bp[:, :nb], lhsT=ub[:, jj, dc * 128:(dc + 1) * 128], rhs=nT[:, dc, :nb], start=(dc == 0), stop=(dc == 7)))
                        S.op("act", [], [bk_, "gh"], lambda e, bp=bp, j=j0 + jj: e.activation(out=gh[:, j, :nb], in_=bp[:, :nb], func=GELU))

            def BE_W(bi):
                b0, nb, tls = blocks[bi]
                pbi = bi % 2
                hb = h1b[pbi]; nT = n2T[pbi]; iT = idxT[pbi]
                iob = iota_f[:, :].unsqueeze(1).to_broadcast([128, 4, 128])
                pend = []

                def flush():
                    bp, bk_, t0, g4 = pend.pop(0)
                    if False:
                        wb_ = wsb[(g4 // 2) % 2]; wbk = "wsb%d" % ((g4 // 2) % 2)
                        S.op("act", [], [bk_, wbk], lambda e: e.copy(out=wb_[:], in_=bp[:, :]))
                        S.op("pool", [wbk], ["gh"], lambda e: e.tensor_tensor(out=gh[:, :, t0:t0 + 4], in0=gh[:, :, t0:t0 + 4], in1=wb_[:, :].rearrange("p (t j) -> p j t", t=4), op=ALU.mult))
                    else:
                        S.op("dve", [], [bk_, "gh"], lambda e: e.tensor_tensor(out=gh[:, :, t0:t0 + 4], in0=gh[:, :, t0:t0 + 4], in1=bp[:, :].rearrange("p (t j) -> p j t", t=4), op=ALU.mult))

                for t0 in range(0, nb, 4):
                    g4 = t0 // 4
                    ab = AB[g4 % 2]; abk = "AB%d" % (g4 % 2)
                    for tt in range(4):
                        t = t0 + tt
                        if tt < 2:
                            S.op("act", [("idxT", pbi)], [abk + "a"], lambda e, ab=ab, t=t, tt=tt: e.activation(out=ab[:, 0, tt, :], in_=iota_f[:, :], func=AF.Square, bias=iT[:, 0, t:t + 1], scale=1.0))
                            S.op("act", [abk + "a"], [abk + "a"], lambda e, ab=ab, tt=tt: e.activation(out=ab[:, 0, tt, :], in_=ab[:, 0, tt, :], func=AF.Relu, bias=1.0, scale=-1.0))
                        else:
                            S.op("dve", [("idxT", pbi)], [abk + "a"], lambda e, ab=ab, t=t, tt=tt: e.tensor_scalar(out=ab[:, 0, tt, :], in0=niota_bf[:, :], scalar1=iT[:, 0, t:t + 1], scalar2=None, op0=ALU.is_equal))
                        S.op("dve", [("idxT", pbi)], [abk + "b"], lambda e, ab=ab, t=t, tt=tt: e.tensor_scalar(out=ab[:, 1, tt, :], in0=iota_bf[:, :], scalar1=iT[:, 1, t:t + 1], scalar2=iT[:, 2, t:t + 1], op0=ALU.is_equal, op1=ALU.mult))
                    bp, bk_ = bank()
                    for tt in range(4):
                        S.op("pe", [abk + "a", abk + "b"], [bk_], lambda e, bp=bp, ab=ab, tt=tt: e.matmul(bp[:, tt * 128:(tt + 1) * 128], lhsT=ab[:, 0, tt, :], rhs=ab[:, 1, tt, :], start=True, stop=True, skip_group_check=True))
                    pend.append((bp, bk_, t0, g4))
                    if len(pend) > 1:
                        flush()
                while pend:
                    flush()

            def BE_out(bi, inject={}):
                b0, nb, tls = blocks[bi]
                pbi = bi % 2
                hb = h1b[pbi]; nT = n2T[pbi]; iT = idxT[pbi]
                for j0 in range(0, 128, 4):
                    for fn in inject.get(j0 // 4, []):
                        fn()
                    vb_ = sbuf_uv[uvn[0] % 3]; vk = "uv%d" % (uvn[0] % 3); uvn[0] += 1
                    S.dma([], [vk], vb_[:], vvv[:, j0:j0 + 4, :])
                    for jj in range(4):
                        j = j0 + jj
                        for (r0, nt, o) in tls:
                            for half in range(2):
                                ab_ = (o // 128) * 2 + half
                                S.op("pe", ["gh", vk], ["F%d" % ab_], lambda e, vb_=vb_, jj=jj, j=j, half=half, nt=nt, o=o, ab_=ab_: e.matmul(Fb[ab_][:nt, :], lhsT=gh[:, j, o:o + nt], rhs=vb_[:, jj, half * 512:(half + 1) * 512], start=(j == 0), stop=(j == 127)))

            def PLE_evac(bi):
                b0, nb, tls = blocks[bi]
                pbi = bi % 2
                hb = h1b[pbi]
                for (r0, nt, o) in tls:
                    tl = o // 128
                    for half in range(2):
                        S.op("act", [], ["F%d" % (tl * 2 + half), "gt"], lambda e, half=half, nt=nt, tl=tl: e.copy(out=gt[:nt, half * 512:(half + 1) * 512], in_=Fb[tl * 2 + half][:nt, :]))
                    S.op("pool", ["gt", ("h1b", pbi, tl)], [("h1b", pbi, tl)], lambda e, nt=nt, tl=tl: e.tensor_tensor(out=hb[:nt, tl, :], in0=hb[:nt, tl, :], in1=gt[:nt, :], op=ALU.add))

            def PLE_p1(bi, tl_):
                b0, nb, tls = blocks[bi]
                pbi = bi % 2
                hb = h1b[pbi]
                if tl_ == 0:
                    S.dma([], ["wsh"], wsh[:], wpg_bf.rearrange("(kc p) c -> p kc c", p=128), q="pool")
                for (r0, nt, o) in tls[tl_:tl_ + 1]:
                    tl = o // 128
                    hk_ = ("h1b", pbi, tl)
                    rms_rstd_act(hk_, hb[:nt, tl, :], nt, rs3, "rs3", n3, "n3")
                    S.op("act", [hk_, "rs3"], ["n3"], lambda e, nt=nt, tl=tl: e.activation(out=n3[:nt, :], in_=hb[:nt, tl, :], func=AF.Identity, scale=rs3[:nt, 0:1]))
                    S.op("pool", ["n3", "gple_bc"], ["n3"], lambda e, nt=nt: e.tensor_tensor(out=n3[:nt, :], in0=n3[:nt, :], in1=gple_bc[:nt, :], op=ALU.mult))
                    S.dma([], ["pin"], pin[:nt, :], (pp[r0:r0 + nt, :] if r0 < NTOK else ps_[r0 - NTOK:r0 - NTOK + nt, :]), q="pool")
                    S.op("pool", ["pin"], ["pbf16"], lambda e, nt=nt: e.tensor_copy(out=pbf16[:nt, :], in_=pin[:nt, :]))

            def PLE_p2(bi, tl_):
                b0, nb, tls = blocks[bi]
                for (r0, nt, o) in tls[tl_:tl_ + 1]:
                    for kc in range(8):
                        S.op("pe", ["n3", "ident"], ["T1"], lambda e, kc=kc, nt=nt: e.transpose(out=Tb[1][:, kc * 128:kc * 128 + nt], in_=n3[:nt, kc * 128:(kc + 1) * 128], identity=ident[:nt, :nt]))
                    S.op("act", [], ["T1", "n3T"], lambda e, nt=nt: e.copy(out=n3T[:, :, :nt], in_=Tb[1][:, :].rearrange("p (k t) -> p k t", k=8)[:, :, :nt]))
                    for kc in range(2):
                        S.op("pe", ["pbf16", "ident"], ["T1"], lambda e, kc=kc, nt=nt: e.transpose(out=Tb[1][:, kc * 128:kc * 128 + nt], in_=pbf16[:nt, kc * 128:(kc + 1) * 128], identity=ident[:nt, :nt]))
                    S.op("act", [], ["T1", "pT"], lambda e, nt=nt: e.copy(out=pT[:, :, :nt], in_=Tb[1][:, 0:256].rearrange("p (k t) -> p k t", k=2)[:, :, :nt]))

            def PLE_p3(bi, tl_):
                b0, nb, tls = blocks[bi]
                pbi = bi % 2
                hb = h1b[pbi]
                for (r0, nt, o) in tls[tl_:tl_ + 1]:
                    tl = o // 128
                    hk_ = ("h1b", pbi, tl)
                    for half in range(2):
                        gb_, gk_ = bank(4, 2)
                        for kc in range(8):
                            S.op("pe", ["n3T", "wsh"], [gk_], lambda e, gb_=gb_, kc=kc, half=half, nt=nt: e.matmul(gb_[:nt, :], lhsT=n3T[:, kc, :nt], rhs=wsh[:, kc, half * 512:(half + 1) * 512], start=(kc == 0), stop=(kc == 7)))
                        S.op("act", [], [gk_, "gt"], lambda e, gb_=gb_, half=half, nt=nt: e.activation(out=gt[:nt, half * 512:(half + 1) * 512], in_=gb_[:nt, :], func=AF.Sigmoid))
                        pb_, pk_ = bank(4, 2)
                        for kc in range(2):
                            S.op("pe", ["pT", "wpp_sb"], [pk_], lambda e, pb_=pb_, kc=kc, half=half, nt=nt: e.matmul(pb_[:nt, :], lhsT=pT[:, kc, :nt], rhs=wpp_sb[:, kc, half * 512:(half + 1) * 512], start=(kc == 0), stop=(kc == 1)))
                        S.op("act", [], [pk_, "n3"], lambda e, pb_=pb_, half=half, nt=nt: e.copy(out=n3[:nt, half * 512:(half + 1) * 512], in_=pb_[:nt, :]))
                    S.op("pool", ["n3", "gt"], ["gt"], lambda e, nt=nt: e.tensor_tensor(out=yo[:nt, :], in0=yo[:nt, :], in1=n3[:nt, :], op=ALU.mult))
                    S.op("pool", [hk_, "gt"], ["gt"], lambda e, nt=nt, tl=tl: e.tensor_tensor(out=yo[:nt, :], in0=yo[:nt, :], in1=hb[:nt, tl, :], op=ALU.add))
                    S.dma(["gt"], [], y[r0:r0 + nt, :], yo[:nt, :], q="pool")

            def F(fn, *a):
                return lambda: fn(*a)

            nblk = len(blocks)
            FE_a1(0, 0); FE_a2(0, 0); FE_a1(0, 1); FE_a2(0, 1); FE_a3(0)
            for tl_ in range(2):
                FE_t(0, tl_); FE_x(0, tl_)
            for bi in range(nblk):
                nx = bi + 1 < nblk
                inj = {}
                if bi > 0:
                    inj.update({1: [F(PLE_p1, bi - 1, 0)], 3: [F(PLE_p2, bi - 1, 0)], 5: [F(PLE_p3, bi - 1, 0)],
                                7: [F(PLE_p1, bi - 1, 1)], 9: [F(PLE_p2, bi - 1, 1)], 11: [F(PLE_p3, bi - 1, 1)]})
                    inj[24] = [F(FE_x, bi, 1)]
                if nx:
                    inj.update({13: [F(FE_a1, bi + 1, 0)], 15: [F(FE_a2, bi + 1, 0)], 17: [F(FE_a1, bi + 1, 1)], 19: [F(FE_a2, bi + 1, 1)], 22: [F(FE_a3, bi + 1)]})
                BE_hid(bi, inj)
                BE_W(bi)
                inj = {}
                if nx:
                    inj = {0: [F(FE_t, bi + 1, 0)], 24: [F(FE_x, bi + 1, 0)], 25: [F(FE_t, bi + 1, 1)]}
                BE_out(bi, inj)
                PLE_evac(bi)
            for tl_ in range(2):
                PLE_p1(nblk - 1, tl_); PLE_p2(nblk - 1, tl_); PLE_p3(nblk - 1, tl_)

        S.barrier()
    return nc


_CACHE = {}


def kernel(**inp):
    f = lambda a: np.ascontiguousarray(np.asarray(a, dtype=np.float32))
    NSEQ, NSAMP = 4, 2
    key = "full"
    if key not in _CACHE:
        _CACHE[key] = build(NSEQ, NSAMP)
    nc = _CACHE[key]
    w_in = f(inp["w_in"])[0]
    wi = np.ascontiguousarray(np.concatenate([w_in[:, 0:1536], w_in[:, 1544:5128]], axis=1))
    wfl = np.ascontiguousarray(w_in[:, 1536:1544])
    rb = f(inp["rel_bias_b"])[0]
    kl = np.arange(128)[:, None]; ql = np.arange(128)[None, :]
    idx0 = np.clip(kl - ql, -128, 128) + 128
    idx1 = np.clip(kl - ql - 128, -128, 128) + 128
    idx2 = np.zeros((128, 128), np.int64)
    biasT = np.ascontiguousarray(np.stack([rb[:, idx0], rb[:, idx1], rb[:, idx2]], axis=1))
    u = f(inp["peer_u"])[0]; v = f(inp["peer_v"])[0]
    uL = np.ascontiguousarray(u.reshape(128, 128, 8, 128).transpose(1, 3, 2, 0)).reshape(128 * 128, 1024)
    vL = np.ascontiguousarray(v.reshape(128, 128, 1024).transpose(1, 0, 2)).reshape(128 * 128, 1024)
    skT = np.ascontiguousarray(f(inp["peer_subkeys"])[0].transpose(0, 1, 3, 2))
    shared = dict(wi=wi, wfl=wfl, g_mix=f(inp["g_mix"])[0], g_ffn=f(inp["g_ffn"])[0], g_ple=f(inp["g_ple"])[0], b_f=f(inp["b_f"])[0],
                  qn_a=f(inp["qn_a"])[0], kn_a=f(inp["kn_a"])[0], qn_b=f(inp["qn_b"])[0], kn_b=f(inp["kn_b"])[0], biasT=biasT,
                  wupa=f(inp["w_up_a"])[0], wupb=f(inp["w_up_b"])[0], wout=f(inp["w_out"])[0], wq=f(inp["peer_wq"])[0],
                  wpg=f(inp["w_ple_gate"])[0], wpp=f(inp["w_ple_proj"])[0], skT=skT, uL=uL, vL=vL)
    xpr = f(inp["x_prompt"]); xsa = f(inp["x_sample"]); ppr = f(inp["p_prompt"])[0]; psa = f(inp["p_sample"])[0]
    cak = f(inp["cache_a_k"])[0]; cav = f(inp["cache_a_v"])[0]; calf = f(inp["cache_a_logf"])[0]
    cbk = f(inp["cache_b_k"])[0]; cbv = f(inp["cache_b_v"])[0]
    in_maps = []
    for c in range(NCORES):
        m = dict(shared)
        m["xp"] = xpr[4 * c:4 * c + 4].reshape(4 * T, D); m["xs"] = xsa[2 * c:2 * c + 2].reshape(2 * NS, D)
        m["pp"] = ppr[4 * c:4 * c + 4].reshape(4 * T, 256); m["ps"] = psa[2 * c:2 * c + 2].reshape(2 * NS, 256)
        m["cak"] = cak[2 * c:2 * c + 2].reshape(2, PAST, 512); m["cav"] = cav[2 * c:2 * c + 2].reshape(2, PAST, 512)
        m["calf"] = calf[2 * c:2 * c + 2]
        m["cbk"] = cbk[2 * c:2 * c + 2].reshape(2, LB, 512); m["cbv"] = cbv[2 * c:2 * c + 2].reshape(2, LB, 512)
        in_maps.append({k: np.ascontiguousarray(a) for k, a in m.items()})
    res = run_bass_kernel_spmd(nc, in_maps, core_ids=list(range(NCORES))).results
    cat = lambda k: np.concatenate([np.asarray(r[k], dtype=np.float32) for r in res], axis=0)
    yall = [np.asarray(r["y"], dtype=np.float32) for r in res]
    y_p = np.concatenate([a[:4 * T] for a in yall], axis=0).reshape(32, T, D)
    y_s = np.concatenate([a[4 * T:] for a in yall], axis=0).reshape(16, NS, D)
    return (y_p, y_s,
            cat("ak").reshape(1, 32, T, NH, HD), cat("av").reshape(1, 32, T, NH, HD), cat("af").reshape(1, 32, T, NH),
            cat("bk").reshape(1, 32, LB, NH, HD), cat("bv").reshape(1, 32, LB, NH, HD),
            cat("aks").reshape(1, 16, NS, NH, HD), cat("avs").reshape(1, 16, NS, NH, HD), cat("afs").reshape(1, 16, NS, NH),
            cat("bks").reshape(1, 16, NS, NH, HD), cat("bvs").reshape(1, 16, NS, NH, HD))
```

```python
import numpy as np
from contextlib import ExitStack
import concourse.bass as bass
import concourse.mybir as mybir
from concourse.bass_utils import run_bass_kernel_spmd

F32 = mybir.dt.float32; BF16 = mybir.dt.bfloat16; I32 = mybir.dt.int32; U32 = mybir.dt.uint32
ALU = mybir.AluOpType; AF = mybir.ActivationFunctionType; AX = mybir.AxisListType

NCORES = 8
D = 1024; T = 2048; NH = 8; HD = 64; PAST = 4096; LB = 512; NS = 16
EPS = 1e-6
NEG = -30000.0
GELU = AF.Gelu_apprx_tanh


class Sched:
    def __init__(self, nc, es, n_dma_sems=40):
        self.nc = nc
        self.eng = {"pe": nc.tensor, "dve": nc.vector, "act": nc.scalar, "pool": nc.gpsimd, "sp": nc.sync}
        self.sem = {k: es.enter_context(nc.semaphore("s_" + k)) for k in ("pe", "dve", "act", "pool")}
        self.cnt = {k: 0 for k in self.sem}
        self.dsem = [es.enter_context(nc.semaphore("d%d" % i)) for i in range(n_dma_sems)]
        self.dcnt = [0] * n_dma_sems
        self.dpool = {"sp": list(range(0, n_dma_sems // 2)), "pool": list(range(n_dma_sems // 2, n_dma_sems))}
        self.dnext = {"sp": 0, "pool": 0}
        self.seen = {k: {} for k in self.eng}
        self.lastw = {}
        self.readers = {}
        self.ninstr = 0

    def _need(self, e, deps):
        for (src, n) in deps:
            if src == "pe" and e == "pe":
                continue
            if self.seen[e].get(src, 0) >= n:
                continue
            sem = self.sem[src] if isinstance(src, str) else self.dsem[src]
            self.eng[e].wait_ge(sem, n)
            self.seen[e][src] = n

    def _deps(self, reads, writes, e=None):
        deps = []
        for k in reads:
            if k in self.lastw:
                deps.append(self.lastw[k])
        for k in writes:
            if k in self.lastw and self.lastw[k][0] != e:
                deps.append(self.lastw[k])
            for s, n in self.readers.get(k, {}).items():
                if s != e:
                    deps.append((s, n))
        return deps

    def _commit(self, tag, reads, writes):
        for k in reads:
            self.readers.setdefault(k, {})[tag[0]] = tag[1]
        for k in writes:
            self.lastw[k] = tag
            self.readers[k] = {}

    def op(self, e, reads, writes, fn):
        self._need(e, self._deps(reads, writes, e))
        ins = fn(self.eng[e])
        self.cnt[e] += 1
        ins.then_inc(self.sem[e], 1)
        self._commit((e, self.cnt[e]), reads, writes)
        self.ninstr += 1
        return ins

    def dma(self, reads, writes, out, in_, q="sp", **kw):
        pool_ = self.dpool[q]
        i = pool_[self.dnext[q] % len(pool_)]
        self.dnext[q] += 1
        deps = self._deps(reads, writes)
        if self.dcnt[i] > 0:
            deps.append((i, self.dcnt[i]))
        self._need(q, deps)
        ins = self.eng[q].dma_start(out=out, in_=in_, **kw)
        self.dcnt[i] += 16
        ins.then_inc(self.dsem[i], 16)
        self._commit((i, self.dcnt[i]), reads, writes)
        self.ninstr += 1
        return ins

    def barrier(self):
        deps = [(i, c) for i, c in enumerate(self.dcnt) if c > 0]
        deps += [(k, c) for k, c in self.cnt.items() if c > 0]
        for e in self.eng:
            self._need(e, deps)
        self.lastw = {}
        self.readers = {}


def build(NSEQ=4, NSAMP=2, PEER=True):
    nc = bass.Bass("TRN2", target_bir_lowering=False)
    NTOK = NSEQ * T
    NALL = NTOK + NSAMP * NS

    def DI(name, shape, dt=F32):
        return nc.dram_tensor(name, list(shape), dt, kind="ExternalInput").ap()

    def DO(name, shape, dt=F32):
        return nc.dram_tensor(name, list(shape), dt, kind="ExternalOutput").ap()

    def DS(name, shape, dt=BF16):
        return nc.dram_tensor(name, list(shape), dt, kind="Internal").ap()

    xp = DI("xp", [NTOK, D]); xs = DI("xs", [NSAMP * NS, D])
    pp = DI("pp", [NTOK, 256]); ps_ = DI("ps", [NSAMP * NS, 256])
    cak = DI("cak", [NSAMP, PAST, 512]); cav = DI("cav", [NSAMP, PAST, 512]); calf = DI("calf", [NSAMP, PAST, NH])
    cbk = DI("cbk", [NSAMP, LB, 512]); cbv = DI("cbv", [NSAMP, LB, 512])
    wi = DI("wi", [D, 5120]); wfl = DI("wfl", [D, NH])
    g_mix = DI("g_mix", [D]); g_ffn = DI("g_ffn", [D]); g_ple = DI("g_ple", [D]); b_f = DI("b_f", [NH])
    qn_a = DI("qn_a", [HD]); kn_a = DI("kn_a", [HD]); qn_b = DI("qn_b", [HD]); kn_b = DI("kn_b", [HD])
    biasT = DI("biasT", [NH, 3, 128, 128])
    wupa = DI("wupa", [512, D]); wupb = DI("wupb", [512, D]); wout = DI("wout", [D, D])
    wq = DI("wq", [D, D]); wpg = DI("wpg", [D, D]); wpp = DI("wpp", [256, D])
    skT = DI("skT", [NH, 2, 64, 128])
    uL = DI("uL", [128 * 128, D]); vL = DI("vL", [128 * 128, D])

    y = DO("y", [NALL, D])
    ak = DO("ak", [NTOK, 512]); av = DO("av", [NTOK, 512]); af = DO("af", [NTOK, NH])
    bk = DO("bk", [NSEQ * LB, 512]); bv = DO("bv", [NSEQ * LB, 512])
    aks = DO("aks", [NSAMP * NS, 512]); avs = DO("avs", [NSAMP * NS, 512]); afs = DO("afs", [NSAMP * NS, NH])
    bks = DO("bks", [NSAMP * NS, 512]); bvs = DO("bvs", [NSAMP * NS, 512])

    wi_bf = DS("wi_bf", [D, 5120])
    wupa_bf = DS("wupa_bf", [512, D]); wupb_bf = DS("wupb_bf", [512, D]); wout_bf = DS("wout_bf", [D, D])
    wq_bf = DS("wq_bf", [D, D]); wpg_bf = DS("wpg_bf", [D, D]); wpp_bf = DS("wpp_bf", [256, D])
    u_bf = DS("u_bf", [128 * 128, D]); v_bf = DS("v_bf", [128 * 128, D])
    gate_s = DS("gate_s", [NALL, 2048])
    yTa_s = DS("yTa_s", [512, NALL]); yTb_s = DS("yTb_s", [512, NALL])
    h1_s = DS("h1_s", [NALL, D], F32)

    with ExitStack() as es0:
        S = Sched(nc, es0)

        def PS(name, shape, dt=F32):
            return es0.enter_context(nc.psum_tensor(name, shape, dt))

        Fb = [PS("f%d" % i, [128, 512]) for i in range(8)]
        Tb = [Fb[6][:, :].bitcast(BF16), Fb[7][:, :].bitcast(BF16)]

        uid = [0]

        def SB(es, name, shape, dt=F32):
            uid[0] += 1
            return es.enter_context(nc.sbuf_tensor("%s_%d" % (name, uid[0]), shape, dt))

        ident = SB(es0, "ident", [128, 128], BF16)
        identf = SB(es0, "identf", [128, 128], F32)
        iota_f = SB(es0, "iota_f", [128, 128], F32)
        ones_f = SB(es0, "ones_f", [128, 64], F32)
        es1 = es0.enter_context(ExitStack())
        Umat = SB(es1, "Umat", [128, 128], F32)
        sel127 = SB(es1, "sel127", [128, 128], F32)
        tri = SB(es1, "tri", [128, 128], BF16)
        S.op("pool", [], ["ident"], lambda e: e.memset(ident[:], 1.0))
        S.op("pool", ["ident"], ["ident"], lambda e: e.affine_select(out=ident[:], in_=ident[:], pattern=[[-1, 128]], compare_op=ALU.is_equal, fill=0.0, base=0, channel_multiplier=1))
        S.op("pool", [], ["identf"], lambda e: e.memset(identf[:], 1.0))
        S.op("pool", ["identf"], ["identf"], lambda e: e.affine_select(out=identf[:], in_=identf[:], pattern=[[-1, 128]], compare_op=ALU.is_equal, fill=0.0, base=0, channel_multiplier=1))
        S.op("pool", [], ["Umat"], lambda e: e.memset(Umat[:], 1.0))
        S.op("pool", ["Umat"], ["Umat"], lambda e: e.affine_select(out=Umat[:], in_=Umat[:], pattern=[[1, 128]], compare_op=ALU.is_ge, fill=0.0, base=0, channel_multiplier=-1))
        S.op("pool", [], ["tri"], lambda e: e.memset(tri[:], 1.0))
        S.op("pool", ["tri"], ["tri"], lambda e: e.affine_select(out=tri[:], in_=tri[:], pattern=[[1, 128]], compare_op=ALU.is_ge, fill=0.0, base=0, channel_multiplier=-1))
        S.op("pool", [], ["sel127"], lambda e: e.memset(sel127[:], 1.0))
        S.op("pool", ["sel127"], ["sel127"], lambda e: e.affine_select(out=sel127[:], in_=sel127[:], pattern=[[0, 128]], compare_op=ALU.is_equal, fill=0.0, base=-127, channel_multiplier=1))
        S.op("pool", [], ["iota_f"], lambda e: e.iota(iota_f[:], pattern=[[1, 128]], base=0, channel_multiplier=0, allow_small_or_imprecise_dtypes=True))
        S.op("pool", [], ["ones_f"], lambda e: e.memset(ones_f[:], 1.0))

        with ExitStack() as es:
            stg = [SB(es, "stg%d" % i, [128, 2048], F32) for i in range(6)]
            stb = [SB(es, "stb%d" % i, [128, 2048], BF16) for i in range(6)]
            cnt = [0]

            def conv(src, dst, R, C):
                sv = src.rearrange("(n p) c -> n p c", p=128)
                dv = dst.rearrange("(n p) c -> n p c", p=128)
                for n in range(R // 128):
                    for c0 in range(0, C, 2048):
                        cw = min(2048, C - c0)
                        i = cnt[0] % 6
                        cnt[0] += 1
                        S.dma([], ["stg%d" % i], stg[i][:, :cw], sv[n, :, c0:c0 + cw])
                        e = ("dve", "act", "dve", "act", "dve", "act")[i]
                        if e == "act":
                            S.op(e, ["stg%d" % i], ["stb%d" % i], lambda g, i=i, cw=cw: g.copy(out=stb[i][:, :cw], in_=stg[i][:, :cw]))
                        else:
                            S.op(e, ["stg%d" % i], ["stb%d" % i], lambda g, i=i, cw=cw: g.tensor_copy(out=stb[i][:, :cw], in_=stg[i][:, :cw]))
                        S.dma(["stb%d" % i], [], dv[n, :, c0:c0 + cw], stb[i][:, :cw], q="pool")

            conv(wi, wi_bf, D, 5120)
            conv(wupa, wupa_bf, 512, D); conv(wupb, wupb_bf, 512, D); conv(wout, wout_bf, D, D)
            conv(wq, wq_bf, D, D); conv(wpg, wpg_bf, D, D); conv(wpp, wpp_bf, 256, D)
            if PEER:
                conv(uL, u_bf, 128 * 128, D); conv(vL, v_bf, 128 * 128, D)
        S.barrier()

        gmix_bc = SB(es1, "gmix_bc", [128, D])
        bf_bc = SB(es1, "bf_bc", [128, NH])
        qna_bc = SB(es1, "qna_bc", [128, HD]); kna_bc = SB(es1, "kna_bc", [128, HD])
        qnb_bc = SB(es1, "qnb_bc", [128, HD]); knb_bc = SB(es1, "knb_bc", [128, HD])
        negC = SB(es1, "negC", [128, 4])
        for nm, t_, src in (("gmix_bc", gmix_bc, g_mix), ("bf_bc", bf_bc, b_f),
                            ("qna_bc", qna_bc, qn_a), ("kna_bc", kna_bc, kn_a), ("qnb_bc", qnb_bc, qn_b), ("knb_bc", knb_bc, kn_b)):
            S.dma([], [nm], t_[:], src.partition_broadcast(128))
        for col, (qk, kk, qt, kt) in enumerate((("qna_bc", "kna_bc", qna_bc, kna_bc), ("qnb_bc", "knb_bc", qnb_bc, knb_bc))):
            S.op("dve", [qk], ["negC"], lambda e, qt=qt, col=col: e.tensor_reduce(out=negC[:, 2 + col:3 + col], in_=qt[:], axis=AX.X, op=ALU.max, apply_absolute_value=True))
            S.op("dve", [kk], ["negC"], lambda e, kt=kt, col=col: e.tensor_reduce(out=negC[:, col:col + 1], in_=kt[:], axis=AX.X, op=ALU.max, apply_absolute_value=True))
            S.op("dve", ["negC"], ["negC"], lambda e, col=col: e.scalar_tensor_tensor(out=negC[:, col:col + 1], in0=negC[:, col:col + 1], scalar=-8.0, in1=negC[:, 2 + col:3 + col], op0=ALU.mult, op1=ALU.mult))
        S.op("dve", ["qna_bc"], ["qna_bc"], lambda e: e.tensor_scalar(out=qna_bc[:], in0=qna_bc[:], scalar1=0.125, scalar2=None, op0=ALU.mult))
        S.op("dve", ["qnb_bc"], ["qnb_bc"], lambda e: e.tensor_scalar(out=qnb_bc[:], in0=qnb_bc[:], scalar1=0.125, scalar2=None, op0=ALU.mult))
        wfl_bf = SB(es1, "wfl_bf", [128, 8, NH], BF16)
        bias_bf = SB(es1, "bias_bf", [128, NH, 4, 128], BF16)
        with ExitStack() as es:
            wfl_f = SB(es, "wfl_f", [128, 8, NH]); bias_f = SB(es, "bias_f", [128, NH, 3, 128])
            S.dma([], ["wfl_f"], wfl_f[:], wfl.rearrange("(kc p) h -> p kc h", p=128))
            S.op("dve", ["wfl_f"], ["wfl_bf"], lambda e: e.tensor_copy(out=wfl_bf[:], in_=wfl_f[:]))
            for h in range(NH):
                S.dma([], ["bias_f%d" % h], bias_f[:, h], biasT[h].rearrange("v k q -> k v q"))
                S.op("dve", ["bias_f%d" % h], ["bias_bf"], lambda e, h=h: e.tensor_copy(out=bias_bf[:, h, 0:3, :], in_=bias_f[:, h]))
                S.op("dve", ["bias_f%d" % h], ["bias_bf"], lambda e, h=h: e.tensor_copy(out=bias_bf[:, h, 3, :], in_=bias_f[:, h, 2, :]))
            S.op("pool", [], ["bias_bf"], lambda e: e.memset(bias_bf[64:128, :, 0, 0:64], NEG))
            S.op("pool", [], ["bias_bf"], lambda e: e.memset(bias_bf[0:64, :, 3, 64:128], NEG))
            S.barrier()

        sel15 = SB(es1, "sel15", [128, 128], F32)
        S.op("pool", [], ["sel15"], lambda e: e.memset(sel15[:], 1.0))
        S.op("pool", ["sel15"], ["sel15"], lambda e: e.affine_select(out=sel15[:], in_=sel15[:], pattern=[[0, 128]], compare_op=ALU.is_equal, fill=0.0, base=-15, channel_multiplier=1))

        def rms_rstd(src_key, src_ap, nt, width, acc, acck, scr, scrk):
            S.op("act", [src_key], [scrk, acck], lambda e: e.activation(out=scr[:nt, :width], in_=src_ap, func=AF.Square, scale=float(width) ** -0.5, accum_out=acc[:nt, 0:1]))
            S.op("act", [acck], [acck], lambda e: e.activation(out=acc[:nt, 0:1], in_=acc[:nt, 0:1], func=AF.Ln, bias=EPS, scale=1.0))
            S.op("act", [acck], [acck], lambda e: e.activation(out=acc[:nt, 0:1], in_=acc[:nt, 0:1], func=AF.Exp, scale=-0.5))

        def run_sequence(es, seq, is_sample):
            if not is_sample:
                tiles = [(seq * T + i * 128, 128, i * 128, i) for i in range(16)]
                groups = [tiles[g * 4:(g + 1) * 4] for g in range(4)]
                TK = T; nkt = 16; x_d = xp; tok0 = seq * T
                o_ak, o_av, o_af, o_bk, o_bv = ak, av, af, bk, bv
                TKB = T; nktb = 16; TQ = T; nqt = 16
            else:
                tiles = [(seq * NS, NS, PAST, 32)]
                groups = [tiles]
                TK = PAST + NS; nkt = 33; x_d = xs; tok0 = NTOK + seq * NS
                o_ak, o_av, o_af, o_bk, o_bv = aks, avs, afs, bks, bvs
                TKB = LB + NS; nktb = 5; TQ = NS; nqt = 1
            qTa = SB(es, "qTa", [128, 4, TQ], BF16); kTa = SB(es, "kTa", [128, 4, TK], BF16)
            qTb = SB(es, "qTb", [128, 4, TQ], BF16); kTb = SB(es, "kTb", [128, 4, TKB], BF16)
            va = SB(es, "va", [128, nkt, NH, 65], BF16); vb = SB(es, "vb", [128, nktb, NH, 65], BF16)
            lf_tok = SB(es, "lf_tok", [128, nkt, NH]); c_tok = SB(es, "c_tok", [128, nkt, NH])
            crefbc = SB(es, "crefbc", [128, nqt, NH]); BIASA = SB(es, "BIASA", [128, NH, nkt, nqt])
            xt = [SB(es, "xt%d" % i, [128, D]) for i in range(2)]
            scr = SB(es, "scr", [128, D]); rs = SB(es, "rs", [128, 2])
            n1 = SB(es, "n1", [128, D], BF16); n1T = SB(es, "n1T", [128, 8, 512], BF16)
            wblk = [SB(es, "wblk%d" % i, [128, 8, 512], BF16) for i in range(2)]
            zf = SB(es, "zf", [128, 512]); zf2 = SB(es, "zf2", [128, 512]); zb = SB(es, "zb", [128, 512], BF16)
            ssq = SB(es, "ssq", [128, NH]); lft = SB(es, "lft", [128, NH])
            gbf = SB(es, "gbf", [128, 512], BF16)
            ptb = [SB(es, "pt%d" % i, [128, 512], BF16) for i in range(4)]
            rinv = SB(es, "rinv", [128, 512]); bcs = SB(es, "bcs", [64, 512]); yst = SB(es, "yst", [64, 512], BF16)
            S.op("pool", [], ["va"], lambda e: e.memset(va[:, :, :, 64:65], 1.0))
            S.op("pool", [], ["vb"], lambda e: e.memset(vb[:, :, :, 64:65], 1.0))

            def cumsum_tile(j, nt):
                first = (j == 0)
                if not first:
                    S.op("pe", [("c", j - 1), "sel127"], ["F5"], lambda e: e.matmul(Fb[5][:nt, 0:NH], lhsT=sel127[:, :nt], rhs=c_tok[:, j - 1, :], start=True, stop=False))
                S.op("pe", [("lf", j), "Umat"], ["F5"], lambda e: e.matmul(Fb[5][:nt, 0:NH], lhsT=Umat[:nt, :nt], rhs=lf_tok[:nt, j, :], start=first, stop=True))
                S.op("act", [], ["F5", ("c", j)], lambda e: e.copy(out=c_tok[:nt, j, :], in_=Fb[5][:nt, 0:NH]))

            def transp4(src, nt, dst, c0, dkey):
                for c in range(4):
                    S.op("pe", ["zb", "ident"], ["T1"], lambda e, c=c: e.transpose(out=Tb[1][:, c * 128:c * 128 + nt], in_=src[:nt, c * 128:(c + 1) * 128], identity=ident[:nt, :nt]))
                S.op("dve", [], ["T1", dkey], lambda e: e.tensor_copy(out=dst[:, :, c0:c0 + nt], in_=Tb[1][:, 0:512].rearrange("p (c t) -> p c t", c=4)[:, :, :nt]))

            if is_sample:
                for (ck, cv, kT_, v_, ntl, kn, vn) in ((cak, cav, kTa, va, 32, "kTa", "va"), (cbk, cbv, kTb, vb, 4, "kTb", "vb")):
                    for j in range(ntl):
                        S.dma([], ["xt0"], xt[0][:, 0:512], ck[seq, j * 128:(j + 1) * 128, :])
                        S.dma([], ["xt1"], xt[1][:, 0:512], cv[seq, j * 128:(j + 1) * 128, :])
                        S.op("act", ["xt0"], ["zb"], lambda e: e.copy(out=zb[:], in_=xt[0][:, 0:512]))
                        transp4(zb, 128, kT_, j * 128, (kn, j))
                        S.op("pool", ["xt1"], [(vn, j)], lambda e, v_=v_, j=j: e.tensor_copy(out=v_[:, j, :, 0:64], in_=xt[1][:, 0:512].rearrange("p (h d) -> p h d", h=NH)))
                S.dma([], [("lf", j) for j in range(32)], lf_tok[:, 0:32, :], calf[seq].rearrange("(j p) h -> p j h", p=128))
                for j in range(32):
                    cumsum_tile(j, 128)

            def qknorm(zp, zk, nt, gain, gk, dst, dk):
                S.op("act", [], [zk, "zf"], lambda e: e.activation(out=zf[:nt, :], in_=zp[:nt, :], func=AF.Square))
                S.op("dve", ["zf"], ["ssq"], lambda e: e.tensor_reduce(out=ssq[:nt, :], in_=zf[:nt, :].rearrange("p (h d) -> p h d", h=NH), axis=AX.X, op=ALU.add))
                S.op("act", ["ssq"], ["ssq"], lambda e: e.activation(out=ssq[:nt, :], in_=ssq[:nt, :], func=AF.Sqrt, bias=EPS, scale=1.0 / HD))
                S.op("dve", ["ssq"], ["ssq"], lambda e: e.reciprocal(out=ssq[:nt, :], in_=ssq[:nt, :]))
                S.op("dve", ["ssq"], [zk, "zf"], lambda e: e.tensor_tensor(out=zf[:nt, :].rearrange("p (h d) -> p h d", h=NH), in0=zp[:nt, :].rearrange("p (h d) -> p h d", h=NH),
                                                                           in1=ssq[:nt, :].unsqueeze(2).to_broadcast([nt, NH, HD]), op=ALU.mult))
                S.op("pool", ["zf", gk], [dk], lambda e: e.tensor_tensor(out=dst[:nt, :].rearrange("p (h d) -> p h d", h=NH), in0=zf[:nt, :].rearrange("p (h d) -> p h d", h=NH),
                                                                         in1=gain[:nt, :].unsqueeze(1).to_broadcast([nt, NH, HD]), op=ALU.mult))

            n1T2 = [n1T, SB(es, "n1Tb", [128, 8, 512], BF16)]
            n1b = [n1, SB(es, "n1b", [128, D], BF16)]
            zf_ = [zf, SB(es, "zfb", [128, 512])]; zf2_ = [zf2, SB(es, "zf2b", [128, 512])]; zb_ = [zb] + [SB(es, "zbb%d" % i, [128, 512], BF16) for i in range(3)]
            ssq_ = [ssq, SB(es, "ssqb", [128, NH])]; gbf_ = [gbf, SB(es, "gbfb", [128, 512], BF16)]
            pend = []
            ecnt = [0]; zrot = [0]; trot = [0]

            mmc = [0]

            def flush(lag=0):
                while pend and mmc[0] - pend[0][0] >= lag:
                    pend.pop(0)[1]()

            def zbank():
                k = zrot[0] % 5; zrot[0] += 1
                return Fb[k], "F%d" % k

            def P_nonpe(grp, gi):
                r0, nt, c0, ti = grp[gi]
                xb = xt[gi % 2]; xk = "xt%d" % (gi % 2); nb_ = n1b[gi % 2]; nk_ = "n1b%d" % (gi % 2)
                S.dma([], [xk], xb[:nt, :], x_d[r0:r0 + nt, :])
                rms_rstd(xk, xb[:nt, :], nt, D, rs, "rs", scr, "scr")
                S.op("dve", [xk, "rs", "gmix_bc"], [nk_], lambda e: e.scalar_tensor_tensor(out=nb_[:nt, :], in0=xb[:nt, :], scalar=rs[:nt, 0:1], in1=gmix_bc[:nt, :], op0=ALU.mult, op1=ALU.mult))

            def P_pe(grp, gi, g):
                r0, nt, c0, ti = grp[gi]
                nb_ = n1b[gi % 2]; nk_ = "n1b%d" % (gi % 2); nT = n1T2[g % 2]
                for kc in range(8):
                    S.op("pe", [nk_, "ident"], ["T0"], lambda e, kc=kc: e.transpose(out=Tb[0][:, kc * 128:kc * 128 + nt], in_=nb_[:nt, kc * 128:(kc + 1) * 128], identity=ident[:nt, :nt]))
                S.op("act", [], ["T0", ("n1T", g % 2, gi)], lambda e: e.copy(out=nT[:, :, gi * 128:gi * 128 + nt], in_=Tb[0][:, :].rearrange("p (k t) -> p k t", k=8)[:, :, :nt]))

            def L_(grp, gi, g):
                r0, nt, c0, ti = grp[gi]
                nT = n1T2[g % 2]
                zp, zk = zbank()
                for kc in range(8):
                    S.op("pe", [("n1T", g % 2, gi), "wfl_bf"], [zk], lambda e, kc=kc: e.matmul(zp[:nt, 0:NH], lhsT=nT[:, kc, gi * 128:gi * 128 + nt], rhs=wfl_bf[:, kc, :], start=(kc == 0), stop=(kc == 7)))
                S.op("dve", ["bf_bc"], [zk, "lft"], lambda e: e.tensor_tensor(out=lft[:nt, :], in0=zp[:nt, 0:NH], in1=bf_bc[:nt, :], op=ALU.add))
                S.op("act", ["lft"], ["lft"], lambda e: e.activation(out=lft[:nt, :], in_=lft[:nt, :], func=AF.Exp, scale=-1.0))
                S.op("act", ["lft"], ["lft"], lambda e: e.activation(out=lft[:nt, :], in_=lft[:nt, :], func=AF.Ln, bias=1.0, scale=1.0))
                S.op("dve", ["lft"], [("lf", ti)], lambda e: e.tensor_scalar(out=lf_tok[:nt, ti, :], in0=lft[:nt, :], scalar1=-1.0, scalar2=None, op0=ALU.mult))
                pend.append((mmc[0], lambda: cumsum_tile(ti, nt)))

            def qknorm2(zp, zk, nt, gain, gk, dst, dk, zfx, zfk, sq, sqk):
                S.op("act", [], [zk, zfk], lambda e: e.activation(out=zfx[:nt, :], in_=zp[:nt, :], func=AF.Square))
                S.op("dve", [zfk], [sqk], lambda e: e.tensor_reduce(out=sq[:nt, :], in_=zfx[:nt, :].rearrange("p (h d) -> p h d", h=NH), axis=AX.X, op=ALU.add))
                S.op("act", [sqk], [sqk], lambda e: e.activation(out=sq[:nt, :], in_=sq[:nt, :], func=AF.Ln, bias=EPS, scale=1.0 / HD))
                S.op("act", [sqk], [sqk], lambda e: e.activation(out=sq[:nt, :], in_=sq[:nt, :], func=AF.Exp, scale=-0.5))
                S.op("dve", [sqk], [zk, zfk], lambda e: e.tensor_tensor(out=zfx[:nt, :].rearrange("p (h d) -> p h d", h=NH), in0=zp[:nt, :].rearrange("p (h d) -> p h d", h=NH),
                                                                        in1=sq[:nt, :].unsqueeze(2).to_broadcast([nt, NH, HD]), op=ALU.mult))
                S.op("pool", [zfk, gk], [dk], lambda e: e.tensor_tensor(out=dst[:nt, :].rearrange("p (h d) -> p h d", h=NH), in0=zfx[:nt, :].rearrange("p (h d) -> p h d", h=NH),
                                                                       in1=gain[:nt, :].unsqueeze(1).to_broadcast([nt, NH, HD]), op=ALU.mult))

            def transp4d(src, sk_, nt, dst, c0_, dkey):
                k = trot[0] % 2; trot[0] += 1
                tb = Tb[k]; tk = "T%d" % k
                for c in range(4):
                    S.op("pe", [sk_, "ident"], [tk], lambda e, c=c: e.transpose(out=tb[:, c * 128:c * 128 + nt], in_=src[:nt, c * 128:(c + 1) * 128], identity=ident[:nt, :nt]))
                S.op("dve", [], [tk, dkey], lambda e: e.tensor_copy(out=dst[:, :, c0_:c0_ + nt], in_=tb[:, 0:512].rearrange("p (c t) -> p c t", c=4)[:, :, :nt]))

            wv = wi_bf.rearrange("(kc p) c -> p kc c", p=128)
            for gi in range(len(groups[0])):
                P_nonpe(groups[0], gi); P_pe(groups[0], gi, 0)
            for gi in range(len(groups[0])):
                L_(groups[0], gi, 0)
            for g, grp in enumerate(groups):
                nxt = groups[g + 1] if g + 1 < len(groups) else None
                nT = n1T2[g % 2]
                for bi in range(10):
                    if nxt is not None:
                        if 1 <= bi <= 4:
                            P_pe(nxt, bi - 1, g + 1)
                        if bi <= 3:
                            P_nonpe(nxt, bi)
                        if 2 <= bi <= 5:
                            L_(nxt, bi - 2, g + 1)
                    wb = wblk[bi % 2]; wk = "wblk%d" % (bi % 2)
                    S.dma([], [wk], wb[:], wv[:, :, bi * 512:(bi + 1) * 512])
                    for gi, (r0, nt, c0, ti) in enumerate(grp):
                        zp, zk = zbank()
                        for kc in range(8):
                            S.op("pe", [("n1T", g % 2, gi), wk], [zk], lambda e, kc=kc, gi=gi, nt=nt, zp=zp, wb=wb: e.matmul(zp[:nt, :], lhsT=nT[:, kc, gi * 128:gi * 128 + nt], rhs=wb[:, kc, :], start=(kc == 0), stop=(kc == 7)))
                        mmc[0] += 1
                        flush(3)
                        e_ = ecnt[0] % 2; e4 = ecnt[0] % 4; ecnt[0] += 1
                        zfx = zf_[e_]; zfk = "zf%d" % e_; zf2x = zf2_[e_]; zf2k = "zf2%d" % e_; zbx = zb_[e4]; zbk = "zb%d" % e4; sq = ssq_[e_]; sqk = "ssq%d" % e_
                        isA = bi < 3
                        if bi in (0, 3):
                            qknorm2(zp, zk, nt, qna_bc if isA else qnb_bc, "qna_bc" if isA else "qnb_bc", zbx, zbk, zfx, zfk, sq, sqk)
                            pend.append((mmc[0], lambda zbx=zbx, zbk=zbk, nt=nt, isA=isA, c0=c0, ti=ti: transp4d(zbx, zbk, nt, qTa if isA else qTb, c0 if not is_sample else 0, ("qTa" if isA else "qTb", ti))))
                        elif bi in (1, 4):
                            qknorm2(zp, zk, nt, kna_bc if isA else knb_bc, "kna_bc" if isA else "knb_bc", zf2x, zf2k, zfx, zfk, sq, sqk)
                            S.op("act", [zf2k], [zbk], lambda e, nt=nt, zbx=zbx, zf2x=zf2x: e.copy(out=zbx[:nt, :], in_=zf2x[:nt, :]))
                            kc0 = c0 if (isA or not is_sample) else LB
                            pend.append((mmc[0], lambda zbx=zbx, zbk=zbk, nt=nt, isA=isA, kc0=kc0, ti=ti: transp4d(zbx, zbk, nt, kTa if isA else kTb, kc0, ("kTa" if isA else "kTb", ti))))
                            if isA or is_sample:
                                S.dma([zf2k], [], (o_ak if isA else o_bk)[r0:r0 + nt, :], zf2x[:nt, :], q="pool")
                            elif ti >= 12:
                                S.dma([zf2k], [], o_bk[seq * LB + (ti - 12) * 128:seq * LB + (ti - 11) * 128, :], zf2x[:nt, :], q="pool")
                        elif bi in (2, 5):
                            S.op("act", [], [zk, zf2k], lambda e, nt=nt, zp=zp, zf2x=zf2x: e.copy(out=zf2x[:nt, :], in_=zp[:nt, :]))
                            v_ = va if isA else vb
                            tj = ti if (isA or not is_sample) else 4
                            S.op("pool", [zf2k], [("va" if isA else "vb", tj)], lambda e, nt=nt, v_=v_, tj=tj, zf2x=zf2x: e.tensor_copy(out=v_[:nt, tj, :, 0:64], in_=zf2x[:nt, :].rearrange("p (h d) -> p h d", h=NH)))
                            if isA or is_sample:
                                S.dma([zf2k], [], (o_av if isA else o_bv)[r0:r0 + nt, :], zf2x[:nt, :], q="pool")
                            elif ti >= 12:
                                S.dma([zf2k], [], o_bv[seq * LB + (ti - 12) * 128:seq * LB + (ti - 11) * 128, :], zf2x[:nt, :], q="pool")
                        else:
                            gx = gbf_[e_]; gk_ = "gbf%d" % e_
                            S.op("act", [], [zk, gk_], lambda e, nt=nt, zp=zp, gx=gx: e.activation(out=gx[:nt, :], in_=zp[:nt, :], func=AF.Sigmoid))
                            tr0 = tok0 + (c0 if not is_sample else 0)
                            S.dma([gk_], [], gate_s[tr0:tr0 + nt, (bi - 6) * 512:(bi - 5) * 512], gx[:nt, :], q="pool")
            flush()
            if not is_sample:
                S.dma([("lf", j) for j in range(16)], [], o_af[seq * T:(seq + 1) * T, :].rearrange("(j p) h -> p j h", p=128), lf_tok[:, 0:16, :])
            else:
                S.dma([("lf", 32)], [], o_af[seq * NS:(seq + 1) * NS, :], lf_tok[:NS, 32, :])

            S.barrier()
            ckeys = [("c", j) for j in range(nkt)]
            if not is_sample:
                S.op("pe", ckeys + ["sel127"], ["F5"], lambda e: e.matmul(Fb[5][:, 0:nkt * NH], lhsT=sel127[:, :], rhs=c_tok[:, :, :].rearrange("p j h -> p (j h)"), start=True, stop=True))
            else:
                S.op("pe", ckeys + ["sel15"], ["F5"], lambda e: e.matmul(Fb[5][:, 0:NH], lhsT=sel15[:NS, :], rhs=c_tok[:NS, 32, :], start=True, stop=True))
            S.op("act", [], ["F5", "crefbc"], lambda e: e.copy(out=crefbc[:, :, :].rearrange("p i h -> p (i h)"), in_=Fb[5][:, 0:nqt * NH]))
            for h in range(NH):
                S.op("dve", ckeys + ["crefbc"], [("BIASA", h)], lambda e, h=h: e.tensor_tensor(out=BIASA[:, h], in0=crefbc[:, :, h].unsqueeze(1).to_broadcast([128, nkt, nqt]),
                                                                                              in1=c_tok[:, :, h].unsqueeze(2).to_broadcast([128, nkt, nqt]), op=ALU.subtract))
                S.op("dve", ["negC", ("BIASA", h)], [("BIASA", h)], lambda e, h=h: e.tensor_scalar(out=BIASA[:, h], in0=BIASA[:, h], scalar1=negC[:, 0:1], scalar2=None, op0=ALU.add))

            cnt = [0]

            def normalize(c, h, qn, q0, dst_s, kb):
                acck = "F%d" % c
                sk = "F%d" % (4 + kb % 4); sp_ = Fb[4 + kb % 4]
                S.op("act", [], [acck, "rinv"], lambda e: e.activation(out=rinv[64:65, :qn], in_=Fb[c][64:65, :qn], func=AF.Ln))
                S.op("act", ["rinv"], ["rinv"], lambda e: e.activation(out=rinv[64:65, :qn], in_=rinv[64:65, :qn], func=AF.Exp, scale=-1.0))
                S.op("pe", ["rinv", "ones_f"], [sk], lambda e: e.matmul(sp_[:64, :qn], lhsT=ones_f[64:65, 0:64], rhs=rinv[64:65, :qn], start=True, stop=True))
                S.op("act", [], [sk, "bcs"], lambda e: e.copy(out=bcs[:64, :qn], in_=sp_[:64, :qn]))
                S.op("dve", ["bcs"], [acck, "yst"], lambda e: e.tensor_tensor(out=yst[:64, :qn], in0=Fb[c][:64, :qn], in1=bcs[:64, :qn], op=ALU.mult))
                S.dma(["yst"], [], dst_s[h * 64:(h + 1) * 64, tok0 + q0:tok0 + q0 + qn], yst[:64, :qn])

            qchunks = [(c * 512, 512) for c in range(4)] if not is_sample else [(0, NS)]
            stepsA = []
            for h in range(NH):
                for c, (q0, qn) in enumerate(qchunks):
                    jlast = (4 * c + 3) if not is_sample else 32
                    for j in range(jlast + 1):
                        stepsA.append((h, c, q0, qn, jlast, j))

            def A_qk(st):
                h, c, q0, qn, jlast, j = st
                hp, ho = h // 2, (h % 2) * 64
                nk = 128 if (not is_sample or j < 32) else NS
                qlo = max(q0, j * 128) if not is_sample else 0
                n = q0 + qn - qlo
                k = cnt[0]; cnt[0] += 1
                sk = "F%d" % (4 + k % 4); sp_ = Fb[4 + k % 4]
                S.op("pe", [("kTa", j), ("qTa", 0)], [sk], lambda e: e.matmul(sp_[:nk, :n], lhsT=kTa[ho:ho + 64, hp, j * 128:j * 128 + nk], rhs=qTa[ho:ho + 64, hp, qlo:qlo + n], start=True, stop=True))
                return (k, nk, qlo, n)

            def A_rest(st, info):
                h, c, q0, qn, jlast, j = st
                k, nk, qlo, n = info
                acck = "F%d" % c
                sk = "F%d" % (4 + k % 4); sp_ = Fb[4 + k % 4]
                pt = ptb[k % 4]; ptk = "pt%d" % (k % 4)
                nb_ = max(1, n // 128)
                for bq in range(nb_):
                    bw = min(128, n)
                    i = (qlo // 128 + bq) if not is_sample else 0
                    S.op("act", [("BIASA", h)], [sk, ptk], lambda e, bq=bq, bw=bw, i=i: e.activation(out=pt[:nk, bq * 128:bq * 128 + bw], in_=sp_[:nk, bq * 128:bq * 128 + bw], func=AF.Exp, bias=BIASA[:nk, h, j, i:i + 1], scale=1.0))
                diag = (j * 128 >= q0) if not is_sample else (j == 32)
                if diag:
                    bw = min(128, n)
                    S.op("pool", ["tri", ptk], [ptk], lambda e: e.tensor_tensor(out=pt[:nk, 0:bw], in0=pt[:nk, 0:bw], in1=tri[:nk, :bw], op=ALU.mult))
                S.op("pe", [ptk, ("va", j)], [acck], lambda e: e.matmul(Fb[c][:65, qlo - q0:qlo - q0 + n], lhsT=va[:nk, j, h, :], rhs=pt[:nk, :n], start=(j == 0), stop=(j == jlast), skip_group_check=True))
                if j == jlast:
                    pendN.append([0 if is_sample else 2, (c, h, qn, q0, yTa_s)])
                for pn in list(pendN):
                    if pn[0] == 0:
                        normalize(*pn[1], k); pendN.remove(pn)
                    else:
                        pn[0] -= 1

            LA = 2
            pendN = []
            infoA = [A_qk(stepsA[i]) for i in range(min(LA, len(stepsA)))]
            for si, st in enumerate(stepsA):
                if si + LA < len(stepsA):
                    infoA.append(A_qk(stepsA[si + LA]))
                A_rest(st, infoA[si])
            for pn in pendN:
                normalize(*pn[1], infoA[-1][0])

            blocksB = []
            for h in range(NH):
                nqi = 16 if not is_sample else 1
                for i in range(nqi):
                    js = list(range(max(0, i - 4), i + 1)) if not is_sample else list(range(5))
                    for j in js:
                        if not is_sample:
                            vi = {0: 0, 1: 1, 2: 2, 3: 2, 4: 3}[i - j]; nk = 128
                        else:
                            vi = {0: 2, 1: 2, 2: 2, 3: 1, 4: 0}[j]; nk = 128 if j < 4 else NS
                        blocksB.append((h, i, js, j, vi, nk))
            qwB = 128 if not is_sample else NS
            batchesB = []
            for blk in blocksB:
                if batchesB and len(batchesB[-1]) < 4 and batchesB[-1][0][5] == blk[5]:
                    batchesB[-1].append(blk)
                else:
                    batchesB.append([blk])

            def B_qk(batch):
                k = cnt[0]; cnt[0] += 1
                sk = "F%d" % (4 + k % 4); sp_ = Fb[4 + k % 4]
                for bx, (h, i, js, j, vi, nk) in enumerate(batch):
                    hp, ho = h // 2, (h % 2) * 64
                    S.op("pe", ["ident", "bias_bf"], [sk], lambda e, bx=bx, h=h, vi=vi, nk=nk: e.matmul(sp_[:nk, bx * qwB:(bx + 1) * qwB], lhsT=ident[:nk, :nk], rhs=bias_bf[:nk, h, vi, 0:qwB], start=True, stop=False, skip_group_check=True))
                    S.op("pe", [("kTb", j), ("qTb", 0)], [sk], lambda e, bx=bx, hp=hp, ho=ho, i=i, j=j, nk=nk: e.matmul(sp_[:nk, bx * qwB:(bx + 1) * qwB], lhsT=kTb[ho:ho + 64, hp, j * 128:j * 128 + nk], rhs=qTb[ho:ho + 64, hp, i * 128:i * 128 + qwB], start=False, stop=True, skip_group_check=True))
                return k

            def B_rest(batch, k):
                sk = "F%d" % (4 + k % 4); sp_ = Fb[4 + k % 4]
                pt = ptb[k % 4]; ptk = "pt%d" % (k % 4)
                nk = batch[0][5]; wtot = len(batch) * qwB
                S.op("act", ["negC"], [sk, ptk], lambda e: e.activation(out=pt[:nk, :wtot], in_=sp_[:nk, :wtot], func=AF.Exp, bias=negC[:nk, 1:2], scale=1.0))
                for bx, (h, i, js, j, vi, nk_) in enumerate(batch):
                    c = i // 4
                    S.op("pe", [ptk, ("vb", j)], ["F%d" % c], lambda e, bx=bx, h=h, i=i, j=j, js=js, c=c: e.matmul(Fb[c][:65, (i % 4) * 128:(i % 4) * 128 + qwB], lhsT=vb[:nk, j, h, :], rhs=pt[:nk, bx * qwB:(bx + 1) * qwB], start=(j == js[0]), stop=(j == js[-1]), skip_group_check=True))
                    if j == js[-1]:
                        if is_sample:
                            pendNB.append([0, (0, h, NS, 0, yTb_s)])
                        elif i % 4 == 3:
                            pendNB.append([1, (c, h, 512, c * 512, yTb_s)])
                for pn in list(pendNB):
                    if pn[0] == 0:
                        normalize(*pn[1], k); pendNB.remove(pn)
                    else:
                        pn[0] -= 1

            pendNB = []
            kB = B_qk(batchesB[0])
            for si, bt in enumerate(batchesB):
                kn = B_qk(batchesB[si + 1]) if si + 1 < len(batchesB) else None
                B_rest(bt, kB)
                klast = kB
                kB = kn
            for pn in pendNB:
                normalize(*pn[1], klast)

        for seq in range(NSEQ):
            with ExitStack() as es:
                run_sequence(es, seq, False)
                S.barrier()
        for seq in range(NSAMP):
            with ExitStack() as es:
                run_sequence(es, seq, True)
                S.barrier()

        alltiles = [(r, 128) for r in range(0, NTOK, 128)] + ([(NTOK, NSAMP * NS)] if NSAMP else [])

        def xrows(r0, nt):
            return xp[r0:r0 + nt, :] if r0 < NTOK else xs[0:nt, :]

        with ExitStack() as es:
            wa = SB(es, "wa", [128, 4, D], BF16); wb_ = SB(es, "wb_", [128, 4, D], BF16); wo = SB(es, "wo", [128, 8, D], BF16)
            S.dma([], ["wa"], wa[:], wupa_bf.rearrange("(kc p) c -> p kc c", p=128))
            S.dma([], ["wb_"], wb_[:], wupb_bf.rearrange("(kc p) c -> p kc c", p=128))
            S.dma([], ["wo"], wo[:], wout_bf.rearrange("(kc p) c -> p kc c", p=128))
            yta = [SB(es, "yta%d" % i, [128, 4, 128], BF16) for i in range(2)]
            ytb = [SB(es, "ytb%d" % i, [128, 4, 128], BF16) for i in range(2)]
            gsb = [SB(es, "gsb%d" % i, [128, 2048], BF16) for i in range(2)]
            xm = [SB(es, "xm%d" % i, [128, D]) for i in range(2)]
            m1 = SB(es, "m1", [128, D]); m2 = SB(es, "m2", [128, D]); mb2 = [SB(es, "mb%d" % i, [128, D], BF16) for i in range(2)]
            mT = SB(es, "mT", [128, 8, 128], BF16); h1t = [SB(es, "h1t%d" % i, [128, D]) for i in range(2)]
            def mergeA(it):
                r0, nt = alltiles[it]
                b = it % 2
                S.dma([], ["yta%d" % b], yta[b][:, :, :nt], yTa_s[:, r0:r0 + nt].rearrange("(kc p) t -> p kc t", p=128))
                S.dma([], ["ytb%d" % b], ytb[b][:, :, :nt], yTb_s[:, r0:r0 + nt].rearrange("(kc p) t -> p kc t", p=128))
                S.dma([], ["gsb%d" % b], gsb[b][:nt, :], gate_s[r0:r0 + nt, :])
                S.dma([], ["xm%d" % b], xm[b][:nt, :], xrows(r0, nt))
                for br, (yt, ytk, w_, wk) in enumerate(((yta[b], "yta%d" % b, wa, "wa"), (ytb[b], "ytb%d" % b, wb_, "wb_"))):
                    for half in range(2):
                        bank = br * 2 + half
                        for kc in range(4):
                            S.op("pe", [ytk, wk], ["F%d" % bank], lambda e, yt=yt, w_=w_, kc=kc, half=half, bank=bank, nt=nt: e.matmul(Fb[bank][:nt, :], lhsT=yt[:, kc, :nt], rhs=w_[:, kc, half * 512:(half + 1) * 512], start=(kc == 0), stop=(kc == 3)))
                for half in range(2):
                    S.op("dve", ["gsb%d" % b], ["F%d" % half, "m1"], lambda e, half=half, nt=nt, b=b: e.tensor_tensor(out=m1[:nt, half * 512:(half + 1) * 512], in0=Fb[half][:nt, :], in1=gsb[b][:nt, half * 512:(half + 1) * 512], op=ALU.mult))
                    S.op("dve", ["gsb%d" % b], ["F%d" % (2 + half), "m2"], lambda e, half=half, nt=nt, b=b: e.tensor_tensor(out=m2[:nt, half * 512:(half + 1) * 512], in0=Fb[2 + half][:nt, :], in1=gsb[b][:nt, 1024 + half * 512:1024 + (half + 1) * 512], op=ALU.mult))
                S.op("pool", ["m1", "m2"], ["mb%d" % b], lambda e, nt=nt, b=b: e.tensor_tensor(out=mb2[b][:nt, :], in0=m1[:nt, :], in1=m2[:nt, :], op=ALU.add))

            def mergeB(it):
                r0, nt = alltiles[it]
                b = it % 2
                for kc in range(8):
                    S.op("pe", ["mb%d" % b, "ident"], ["T0"], lambda e, kc=kc, nt=nt, b=b: e.transpose(out=Tb[0][:, kc * 128:kc * 128 + nt], in_=mb2[b][:nt, kc * 128:(kc + 1) * 128], identity=ident[:nt, :nt]))
                S.op("act", [], ["T0", "mT"], lambda e, nt=nt: e.copy(out=mT[:, :, :nt], in_=Tb[0][:, :].rearrange("p (k t) -> p k t", k=8)[:, :, :nt]))
                for half in range(2):
                    for kc in range(8):
                        S.op("pe", ["mT", "wo"], ["F%d" % (4 + half)], lambda e, kc=kc, half=half, nt=nt: e.matmul(Fb[4 + half][:nt, :], lhsT=mT[:, kc, :nt], rhs=wo[:, kc, half * 512:(half + 1) * 512], start=(kc == 0), stop=(kc == 7)))
                    S.op("dve", ["xm%d" % b], ["F%d" % (4 + half), "h1t%d" % b], lambda e, half=half, nt=nt, b=b: e.tensor_tensor(out=h1t[b][:nt, half * 512:(half + 1) * 512], in0=Fb[4 + half][:nt, :], in1=xm[b][:nt, half * 512:(half + 1) * 512], op=ALU.add))
                S.dma(["h1t%d" % b], [], (h1_s if PEER else y)[r0:r0 + nt, :], h1t[b][:nt, :])
            mergeA(0)
            for it in range(len(alltiles)):
                if it + 1 < len(alltiles):
                    mergeA(it + 1)
                mergeB(it)
            S.barrier()

        es1.close()
        if PEER:
          NB = 256
          blocks = []
          for b0 in range(0, NALL, NB):
              nb = min(NB, NALL - b0)
              blocks.append((b0, nb, [(b0 + o, min(128, nb - o), o) for o in range(0, nb, 128)]))
          with ExitStack() as es:
            wsh = SB(es, "wsh", [128, 8, D], BF16); wpp_sb = SB(es, "wpp_sb", [128, 2, D], BF16)
            gffn_bc = SB(es, "gffn_bc", [128, D]); gple_bc = SB(es, "gple_bc", [128, D])
            S.dma([], ["gffn_bc"], gffn_bc[:], g_ffn.partition_broadcast(128))
            S.dma([], ["gple_bc"], gple_bc[:], g_ple.partition_broadcast(128))
            S.dma([], ["wpp_sb"], wpp_sb[:], wpp_bf.rearrange("(kc p) c -> p kc c", p=128))
            Kbd = SB(es, "Kbd", [128, NH, 256], BF16)
            with ExitStack() as es2:
                Kst = SB(es2, "Kst", [128, NH, 256])
                S.op("pool", [], ["Kst"], lambda e: e.memset(Kst[:], 0.0))
                S.dma([], ["Kst"], Kst[0:64, :, 0:128], skT[:, 0].rearrange("h c n -> c h n"))
                S.dma([], ["Kst"], Kst[64:128, :, 128:256], skT[:, 1].rearrange("h c n -> c h n"))
                S.op("dve", ["Kst"], ["Kbd"], lambda e: e.tensor_copy(out=Kbd[:], in_=Kst[:]))
                S.barrier()
            h1b = [SB(es, "h1b%d" % i, [128, 2, D]) for i in range(2)]
            n2T = [SB(es, "n2T%d" % i, [128, 8, NB], BF16) for i in range(2)]
            idxT = [SB(es, "idxT%d" % i, [128, 3, NB]) for i in range(2)]
            rs = SB(es, "rs2", [128, 2]); rs3 = SB(es, "rs3", [128, 2])
            n2 = SB(es, "n2", [128, D], BF16); pqT = SB(es, "pqT", [128, NH, NB], BF16)
            sc = SB(es, "sc", [128, 2048]); wk = SB(es, "wk", [128, 2048])
            ts = SB(es, "ts", [128, 16, 16]); ti_ = SB(es, "ti_", [128, 16, 16], U32); tif = SB(es, "tif", [128, 16, 16])
            cand = SB(es, "cand", [128, NH, 256]); bs = SB(es, "bs", [128, NH, 16]); pos = SB(es, "pos", [128, NH, 16], U32)
            pa = SB(es, "pa", [128, NH, 16], U32); pb = SB(es, "pb", [128, NH, 16], U32); paf = SB(es, "paf", [128, NH, 16]); pbf = SB(es, "pbf", [128, NH, 16])
            iidx2 = [SB(es, "iidx%d" % i, [128, 128]) for i in range(2)]; jidx2 = [SB(es, "jidx%d" % i, [128, 128]) for i in range(2)]; gg2 = [SB(es, "gg%d" % i, [128, NH, 16]) for i in range(2)]; gs = SB(es, "gs", [128, NH])
            gh = SB(es, "gh", [128, 128, NB], BF16)
            sbuf_uv = [SB(es, "uv%d" % i, [128, 4, D], BF16) for i in range(3)]
            AB = [SB(es, "AB%d" % i, [128, 2, 4, 128], BF16) for i in range(2)]
            wsb = [SB(es, "wsb%d" % i, [128, 512], BF16) for i in range(2)]
            n3 = SB(es, "n3", [128, D], BF16); n3T = SB(es, "n3T", [128, 8, 128], BF16); gt = SB(es, "gt", [128, D])
            pin = SB(es, "pin", [128, 256]); pbf16 = SB(es, "pbf16", [128, 256], BF16); pT = SB(es, "pT", [128, 2, 128], BF16)
            yo = gt
            uvv = u_bf.rearrange("(j p) c -> p j c", p=128); vvv = v_bf.rearrange("(j p) c -> p j c", p=128)
            iota_bf = SB(es, "iota_bf", [128, 128], BF16)
            S.op("dve", ["iota_f"], ["iota_bf"], lambda e: e.tensor_copy(out=iota_bf[:], in_=iota_f[:]))
            niota_bf = SB(es, "niota_bf", [128, 128], BF16)
            S.op("dve", ["iota_f"], ["niota_bf"], lambda e: e.tensor_scalar(out=niota_bf[:], in0=iota_f[:], scalar1=-1.0, scalar2=None, op0=ALU.mult))
            rot = [0]
            uvn = [0]

            def rms_rstd_act(src_key, src_ap, nt, acc, acck, junk, junkk):
                S.op("act", [src_key], [junkk, acck], lambda e: e.activation(out=junk[:nt, :], in_=src_ap, func=AF.Square, scale=float(D) ** -0.5, accum_out=acc[:nt, 0:1]))
                S.op("act", [acck], [acck], lambda e: e.activation(out=acc[:nt, 0:1], in_=acc[:nt, 0:1], func=AF.Ln, bias=EPS, scale=1.0))
                S.op("act", [acck], [acck], lambda e: e.activation(out=acc[:nt, 0:1], in_=acc[:nt, 0:1], func=AF.Exp, scale=-0.5))

            def bank(lo=0, n=6):
                b = lo + rot[0] % n
                rot[0] += 1
                return Fb[b], "F%d" % b

            def FE_a1(bi, tl_):
                b0, nb, tls = blocks[bi]
                pbi = bi % 2
                hb = h1b[pbi]; nT = n2T[pbi]; iT = idxT[pbi]
                for (r0, nt, o) in tls[tl_:tl_ + 1]:
                    tl = o // 128
                    S.dma([], [("h1b", pbi, tl)], hb[:nt, tl, :], h1_s[r0:r0 + nt, :], q="pool")
                    rms_rstd_act(("h1b", pbi, tl), hb[:nt, tl, :], nt, rs, "rs2", n2, "n2")
                    S.op("act", [("h1b", pbi, tl), "rs2"], ["n2"], lambda e, nt=nt, tl=tl: e.activation(out=n2[:nt, :], in_=hb[:nt, tl, :], func=AF.Identity, scale=rs[:nt, 0:1]))
                    S.op("pool", ["n2", "gffn_bc"], ["n2"], lambda e, nt=nt: e.tensor_tensor(out=n2[:nt, :], in0=n2[:nt, :], in1=gffn_bc[:nt, :], op=ALU.mult))

            def FE_a2(bi, tl_):
                b0, nb, tls = blocks[bi]
                pbi = bi % 2
                nT = n2T[pbi]
                for (r0, nt, o) in tls[tl_:tl_ + 1]:
                    for kc in range(8):
                        S.op("pe", ["n2", "ident"], ["T0"], lambda e, kc=kc, nt=nt: e.transpose(out=Tb[0][:, kc * 128:kc * 128 + nt], in_=n2[:nt, kc * 128:(kc + 1) * 128], identity=ident[:nt, :nt]))
                    S.op("act", [], ["T0", ("n2T", pbi)], lambda e, nt=nt, o=o: e.copy(out=nT[:, :, o:o + nt], in_=Tb[0][:, :].rearrange("p (k t) -> p k t", k=8)[:, :, :nt]))

            def FE_a3(bi):
                b0, nb, tls = blocks[bi]
                pbi = bi % 2
                nT = n2T[pbi]
                S.dma([], ["wsh"], wsh[:], wq_bf.rearrange("(kc p) c -> p kc c", p=128), q="pool")
                for h in range(NH):
                    bp, bk_ = bank(4, 2)
                    for kc in range(8):
                        S.op("pe", [("n2T", pbi), "wsh"], [bk_], lambda e, bp=bp, kc=kc, h=h: e.matmul(bp[:, :nb], lhsT=wsh[:, kc, h * 128:(h + 1) * 128], rhs=nT[:, kc, :nb], start=(kc == 0), stop=(kc == 7)))
                    S.op("act", [], [bk_, ("pqT", h)], lambda e, bp=bp, h=h: e.copy(out=pqT[:, h, :nb], in_=bp[:, :nb]))

            def FE_t(bi, tl_):
                b0, nb, tls = blocks[bi]
                pbi = bi % 2
                if tl_ >= len(tls):
                    return
                iidx = iidx2[tl_]; jidx = jidx2[tl_]; gg = gg2[tl_]
                IK = "iidx%d" % tl_; JK = "jidx%d" % tl_; GK = "gg%d" % tl_
                for (r0, nt, o) in tls[tl_:tl_ + 1]:
                    for h in range(NH):
                        bp, bk_ = Fb[4 + (h // 2) % 2], "F%d" % (4 + (h // 2) % 2)
                        S.op("pe", [("pqT", h), "Kbd"], [bk_], lambda e, bp=bp, h=h, nt=nt, o=o: e.matmul(bp[:nt, (h % 2) * 256:(h % 2) * 256 + 256], lhsT=pqT[:, h, o:o + nt], rhs=Kbd[:, h, :], start=True, stop=True, skip_group_check=True))
                        if h % 2 == 1:
                            S.op("act", [], [bk_, "sc"], lambda e, bp=bp, h=h, nt=nt: e.copy(out=sc[:nt, (h // 2) * 512:(h // 2 + 1) * 512], in_=bp[:nt, :]))
                    for s_ in range(16):
                        seg = sc[:nt, s_ * 128:(s_ + 1) * 128]; wseg = wk[:nt, s_ * 128:(s_ + 1) * 128]
                        S.op("dve", ["sc"], ["ts"], lambda e, seg=seg, s_=s_, nt=nt: e.max(out=ts[:nt, s_, 0:8], in_=seg))
                        S.op("dve", ["sc", "ts"], ["ti_"], lambda e, seg=seg, s_=s_, nt=nt: e.max_index(out=ti_[:nt, s_, 0:8], in_max=ts[:nt, s_, 0:8], in_values=seg))
                        S.op("dve", ["sc", "ts"], ["wk"], lambda e, seg=seg, wseg=wseg, s_=s_, nt=nt: e.match_replace(out=wseg, in_to_replace=ts[:nt, s_, 0:8], in_values=seg, imm_value=-1e30))
                        S.op("dve", ["wk"], ["ts"], lambda e, wseg=wseg, s_=s_, nt=nt: e.max(out=ts[:nt, s_, 8:16], in_=wseg))
                        S.op("dve", ["wk", "ts"], ["ti_"], lambda e, wseg=wseg, s_=s_, nt=nt: e.max_index(out=ti_[:nt, s_, 8:16], in_max=ts[:nt, s_, 8:16], in_values=wseg))
                    ts4 = ts[:nt].rearrange("p (h two) k -> p h two k", two=2)
                    S.op("dve", ["ts"], ["cand"], lambda e, nt=nt, ts4=ts4: e.tensor_tensor(out=cand[:nt].rearrange("p h (a b) -> p h a b", a=16), in0=ts4[:, :, 0, :].unsqueeze(3).to_broadcast([nt, NH, 16, 16]),
                                                                                       in1=ts4[:, :, 1, :].unsqueeze(2).to_broadcast([nt, NH, 16, 16]), op=ALU.add))
                    for h in range(NH):
                        cs = cand[:nt, h, :]; cws = wk[:nt, h * 256:(h + 1) * 256]
                        S.op("dve", ["cand"], ["bs"], lambda e, cs=cs, h=h, nt=nt: e.max(out=bs[:nt, h, 0:8], in_=cs))
                        S.op("dve", ["cand", "bs"], ["pos"], lambda e, cs=cs, h=h, nt=nt: e.max_index(out=pos[:nt, h, 0:8], in_max=bs[:nt, h, 0:8], in_values=cs))
                        S.op("dve", ["cand", "bs"], ["wk"], lambda e, cs=cs, cws=cws, h=h, nt=nt: e.match_replace(out=cws, in_to_replace=bs[:nt, h, 0:8], in_values=cs, imm_value=-1e30))
                        S.op("dve", ["wk"], ["bs"], lambda e, cws=cws, h=h, nt=nt: e.max(out=bs[:nt, h, 8:16], in_=cws))
                        S.op("dve", ["wk", "bs"], ["pos"], lambda e, cws=cws, h=h, nt=nt: e.max_index(out=pos[:nt, h, 8:16], in_max=bs[:nt, h, 8:16], in_values=cws))
                    S.op("dve", ["pos"], ["pa"], lambda e, nt=nt: e.tensor_single_scalar(out=pa[:nt], in_=pos[:nt], scalar=4, op=ALU.logical_shift_right))
                    S.op("dve", ["pos"], ["pb"], lambda e, nt=nt: e.tensor_single_scalar(out=pb[:nt], in_=pos[:nt], scalar=15, op=ALU.bitwise_and))
                    S.op("dve", ["pa"], ["paf"], lambda e, nt=nt: e.tensor_copy(out=paf[:nt], in_=pa[:nt]))
                    S.op("dve", ["pb"], ["pbf"], lambda e, nt=nt: e.tensor_copy(out=pbf[:nt], in_=pb[:nt]))
                    S.op("dve", ["ti_"], ["tif"], lambda e, nt=nt: e.tensor_copy(out=tif[:nt], in_=ti_[:nt]))
                    tif4 = tif[:nt].rearrange("p (h two) k -> p h two k", two=2)
                    io16 = iota_f[:nt, 0:16].unsqueeze(1).unsqueeze(1).to_broadcast([nt, NH, 16, 16])
                    oh = sc[:nt, :].rearrange("p (h k a) -> p h k a", h=NH, k=16)
                    for (pf, pk, two, dst, dk) in ((paf, "paf", 0, iidx, IK), (pbf, "pbf", 1, jidx, JK)):
                        S.op("dve", [pk], ["sc"], lambda e, pf=pf, nt=nt, oh=oh, io16=io16: e.tensor_tensor(out=oh, in0=pf[:nt].unsqueeze(3).to_broadcast([nt, NH, 16, 16]), in1=io16, op=ALU.is_equal))
                        S.op("dve", ["tif", "sc"], ["sc"], lambda e, two=two, nt=nt, oh=oh, tif4=tif4: e.tensor_tensor(out=oh, in0=oh, in1=tif4[:, :, two, :].unsqueeze(2).to_broadcast([nt, NH, 16, 16]), op=ALU.mult))
                        S.op("dve", ["sc"], [dk], lambda e, dst=dst, nt=nt, oh=oh: e.tensor_reduce(out=dst[:nt, :].rearrange("p (h k) -> p h k", h=NH), in_=oh, axis=AX.X, op=ALU.add))
                    S.op("dve", ["bs"], [GK], lambda e, nt=nt: e.tensor_tensor(out=gg[:nt], in0=bs[:nt], in1=bs[:nt, :, 0:1].to_broadcast([nt, NH, 16]), op=ALU.subtract))

            def FE_x(bi, tl_):
                b0, nb, tls = blocks[bi]
                pbi = bi % 2
                iT = idxT[pbi]
                if tl_ >= len(tls):
                    return
                iidx = iidx2[tl_]; jidx = jidx2[tl_]; gg = gg2[tl_]
                GK = "gg%d" % tl_
                for (r0, nt, o) in tls[tl_:tl_ + 1]:
                    S.op("act", [GK], [GK], lambda e, nt=nt: e.activation(out=gg[:nt], in_=gg[:nt], func=AF.Exp))
                    S.op("dve", [GK], ["gs"], lambda e, nt=nt: e.tensor_reduce(out=gs[:nt], in_=gg[:nt], axis=AX.X, op=ALU.add))
                    S.op("dve", ["gs"], ["gs"], lambda e, nt=nt: e.reciprocal(out=gs[:nt], in_=gs[:nt]))
                    S.op("dve", ["gs", GK], [GK], lambda e, nt=nt: e.tensor_tensor(out=gg[:nt], in0=gg[:nt], in1=gs[:nt].unsqueeze(2).to_broadcast([nt, NH, 16]), op=ALU.mult))
                    bp, bk_ = bank(4, 2)
                    for q_, (src, sk_) in enumerate(((iidx[:nt, :], "iidx%d" % tl_), (jidx[:nt, :], "jidx%d" % tl_), (gg[:nt].rearrange("p h k -> p (h k)"), "gg%d" % tl_))):
                        S.op("pe", [sk_, "identf"], [bk_], lambda e, bp=bp, q_=q_, src=src, nt=nt: e.transpose(out=bp[:, q_ * 128:q_ * 128 + nt], in_=src, identity=identf[:nt, :nt]))
                    S.op("act", [], [bk_, ("idxT", pbi)], lambda e, bp=bp, nt=nt, o=o: e.copy(out=iT[:, 1:3, o:o + nt], in_=bp[:, 128:384].rearrange("p (q t) -> p q t", q=2)[:, :, :nt]))
                    S.op("act", [], [bk_, ("idxT", pbi)], lambda e, bp=bp, nt=nt, o=o: e.activation(out=iT[:, 0, o:o + nt], in_=bp[:, 0:nt], func=AF.Identity, scale=-1.0))

            def BE_hid(bi, inject={}):
                b0, nb, tls = blocks[bi]
                pbi = bi % 2
                hb = h1b[pbi]; nT = n2T[pbi]; iT = idxT[pbi]
                for j0 in range(0, 128, 4):
                    for fn in inject.get(j0 // 4, []):
                        fn()
                    ub = sbuf_uv[uvn[0] % 3]; uk = "uv%d" % (uvn[0] % 3); uvn[0] += 1
                    S.dma([], [uk], ub[:], uvv[:, j0:j0 + 4, :])
                    for jj in range(0, 4, 2):
                        bp, bk_ = bank(0, 4)
                        for j2 in range(2):
                            for dc in range(8):
                                S.op("pe", [uk, ("n2T", pbi)], [bk_], lambda e, bp=bp, ub=ub, jj=jj, j2=j2, dc=dc: e.matmul(bp[:, j2 * nb:(j2 + 1) * nb], lhsT=ub[:, jj + j2, dc * 128:(dc + 1) * 128], rhs=nT[:, dc, :nb], start=(dc == 0), stop=(dc == 7), skip_group_check=True))
                        S.op("act", [], [bk_, "gh"], lambda e, bp=bp, j=j0 + jj: e.activation(out=gh[:, j:j + 2, :nb], in_=bp[:, :2 * nb].rearrange("p (a t) -> p a t", a=2), func=GELU))

            def BE_W(bi):
                b0, nb, tls = blocks[bi]
                pbi = bi % 2
                hb = h1b[pbi]; nT = n2T[pbi]; iT = idxT[pbi]
                iob = iota_f[:, :].unsqueeze(1).to_broadcast([128, 4, 128])
                pend = []

                def flush():
                    bp, bk_, t0, g4 = pend.pop(0)
                    if False:
                        wb_ = wsb[(g4 // 2) % 2]; wbk = "wsb%d" % ((g4 // 2) % 2)
                        S.op("act", [], [bk_, wbk], lambda e: e.copy(out=wb_[:], in_=bp[:, :]))
                        S.op("pool", [wbk], ["gh"], lambda e: e.tensor_tensor(out=gh[:, :, t0:t0 + 4], in0=gh[:, :, t0:t0 + 4], in1=wb_[:, :].rearrange("p (t j) -> p j t", t=4), op=ALU.mult))
                    else:
                        S.op("dve", [], [bk_, "gh"], lambda e: e.tensor_tensor(out=gh[:, :, t0:t0 + 4], in0=gh[:, :, t0:t0 + 4], in1=bp[:, :].rearrange("p (t j) -> p j t", t=4), op=ALU.mult))

                for t0 in range(0, nb, 4):
                    g4 = t0 // 4
                    ab = AB[g4 % 2]; abk = "AB%d" % (g4 % 2)
                    for tt in range(4):
                        t = t0 + tt
                        if tt < 2:
                            S.op("act", [("idxT", pbi)], [abk + "a"], lambda e, ab=ab, t=t, tt=tt: e.activation(out=ab[:, 0, tt, :], in_=iota_f[:, :], func=AF.Square, bias=iT[:, 0, t:t + 1], scale=1.0))
                            S.op("act", [abk + "a"], [abk + "a"], lambda e, ab=ab, tt=tt: e.activation(out=ab[:, 0, tt, :], in_=ab[:, 0, tt, :], func=AF.Relu, bias=1.0, scale=-1.0))
                        else:
                            S.op("dve", [("idxT", pbi)], [abk + "a"], lambda e, ab=ab, t=t, tt=tt: e.tensor_scalar(out=ab[:, 0, tt, :], in0=niota_bf[:, :], scalar1=iT[:, 0, t:t + 1], scalar2=None, op0=ALU.is_equal))
                        S.op("dve", [("idxT", pbi)], [abk + "b"], lambda e, ab=ab, t=t, tt=tt: e.tensor_scalar(out=ab[:, 1, tt, :], in0=iota_bf[:, :], scalar1=iT[:, 1, t:t + 1], scalar2=iT[:, 2, t:t + 1], op0=ALU.is_equal, op1=ALU.mult))
                    bp, bk_ = bank()
                    for tt in range(4):
                        S.op("pe", [abk + "a", abk + "b"], [bk_], lambda e, bp=bp, ab=ab, tt=tt: e.matmul(bp[:, tt * 128:(tt + 1) * 128], lhsT=ab[:, 0, tt, :], rhs=ab[:, 1, tt, :], start=True, stop=True, skip_group_check=True))
                    pend.append((bp, bk_, t0, g4))
                    if len(pend) > 1:
                        flush()
                while pend:
                    flush()

            def BE_out(bi, inject={}):
                b0, nb, tls = blocks[bi]
                pbi = bi % 2
                hb = h1b[pbi]; nT = n2T[pbi]; iT = idxT[pbi]
                for j0 in range(0, 128, 4):
                    for fn in inject.get(j0 // 4, []):
                        fn()
                    vb_ = sbuf_uv[uvn[0] % 3]; vk = "uv%d" % (uvn[0] % 3); uvn[0] += 1
                    S.dma([], [vk], vb_[:], vvv[:, j0:j0 + 4, :])
                    for jj in range(4):
                        j = j0 + jj
                        for (r0, nt, o) in tls:
                            for half in range(2):
                                ab_ = (o // 128) * 2 + half
                                S.op("pe", ["gh", vk], ["F%d" % ab_], lambda e, vb_=vb_, jj=jj, j=j, half=half, nt=nt, o=o, ab_=ab_: e.matmul(Fb[ab_][:nt, :], lhsT=gh[:, j, o:o + nt], rhs=vb_[:, jj, half * 512:(half + 1) * 512], start=(j == 0), stop=(j == 127)))

            def PLE_evac(bi):
                b0, nb, tls = blocks[bi]
                pbi = bi % 2
                hb = h1b[pbi]
                for (r0, nt, o) in tls:
                    tl = o // 128
                    for half in range(2):
                        S.op("act", [], ["F%d" % (tl * 2 + half), "gt"], lambda e, half=half, nt=nt, tl=tl: e.copy(out=gt[:nt, half * 512:(half + 1) * 512], in_=Fb[tl * 2 + half][:nt, :]))
                    S.op("pool", ["gt", ("h1b", pbi, tl)], [("h1b", pbi, tl)], lambda e, nt=nt, tl=tl: e.tensor_tensor(out=hb[:nt, tl, :], in0=hb[:nt, tl, :], in1=gt[:nt, :], op=ALU.add))

            def PLE_p1(bi, tl_):
                b0, nb, tls = blocks[bi]
                pbi = bi % 2
                hb = h1b[pbi]
                if tl_ == 0:
                    S.dma([], ["wsh"], wsh[:], wpg_bf.rearrange("(kc p) c -> p kc c", p=128), q="pool")
                for (r0, nt, o) in tls[tl_:tl_ + 1]:
                    tl = o // 128
                    hk_ = ("h1b", pbi, tl)
                    rms_rstd_act(hk_, hb[:nt, tl, :], nt, rs3, "rs3", n3, "n3")
                    S.op("act", [hk_, "rs3"], ["n3"], lambda e, nt=nt, tl=tl: e.activation(out=n3[:nt, :], in_=hb[:nt, tl, :], func=AF.Identity, scale=rs3[:nt, 0:1]))
                    S.op("pool", ["n3", "gple_bc"], ["n3"], lambda e, nt=nt: e.tensor_tensor(out=n3[:nt, :], in0=n3[:nt, :], in1=gple_bc[:nt, :], op=ALU.mult))
                    S.dma([], ["pin"], pin[:nt, :], (pp[r0:r0 + nt, :] if r0 < NTOK else ps_[r0 - NTOK:r0 - NTOK + nt, :]), q="pool")
                    S.op("pool", ["pin"], ["pbf16"], lambda e, nt=nt: e.tensor_copy(out=pbf16[:nt, :], in_=pin[:nt, :]))

            def PLE_p2(bi, tl_):
                b0, nb, tls = blocks[bi]
                for (r0, nt, o) in tls[tl_:tl_ + 1]:
                    for kc in range(8):
                        S.op("pe", ["n3", "ident"], ["T1"], lambda e, kc=kc, nt=nt: e.transpose(out=Tb[1][:, kc * 128:kc * 128 + nt], in_=n3[:nt, kc * 128:(kc + 1) * 128], identity=ident[:nt, :nt]))
                    S.op("act", [], ["T1", "n3T"], lambda e, nt=nt: e.copy(out=n3T[:, :, :nt], in_=Tb[1][:, :].rearrange("p (k t) -> p k t", k=8)[:, :, :nt]))
                    for kc in range(2):
                        S.op("pe", ["pbf16", "ident"], ["T1"], lambda e, kc=kc, nt=nt: e.transpose(out=Tb[1][:, kc * 128:kc * 128 + nt], in_=pbf16[:nt, kc * 128:(kc + 1) * 128], identity=ident[:nt, :nt]))
                    S.op("act", [], ["T1", "pT"], lambda e, nt=nt: e.copy(out=pT[:, :, :nt], in_=Tb[1][:, 0:256].rearrange("p (k t) -> p k t", k=2)[:, :, :nt]))

            def PLE_p3(bi, tl_):
                b0, nb, tls = blocks[bi]
                pbi = bi % 2
                hb = h1b[pbi]
                for (r0, nt, o) in tls[tl_:tl_ + 1]:
                    tl = o // 128
                    hk_ = ("h1b", pbi, tl)
                    for half in range(2):
                        gb_, gk_ = bank(4, 2)
                        for kc in range(8):
                            S.op("pe", ["n3T", "wsh"], [gk_], lambda e, gb_=gb_, kc=kc, half=half, nt=nt: e.matmul(gb_[:nt, :], lhsT=n3T[:, kc, :nt], rhs=wsh[:, kc, half * 512:(half + 1) * 512], start=(kc == 0), stop=(kc == 7)))
                        S.op("act", [], [gk_, "gt"], lambda e, gb_=gb_, half=half, nt=nt: e.activation(out=gt[:nt, half * 512:(half + 1) * 512], in_=gb_[:nt, :], func=AF.Sigmoid))
                        pb_, pk_ = bank(4, 2)
                        for kc in range(2):
                            S.op("pe", ["pT", "wpp_sb"], [pk_], lambda e, pb_=pb_, kc=kc, half=half, nt=nt: e.matmul(pb_[:nt, :], lhsT=pT[:, kc, :nt], rhs=wpp_sb[:, kc, half * 512:(half + 1) * 512], start=(kc == 0), stop=(kc == 1)))
                        S.op("act", [], [pk_, "n3"], lambda e, pb_=pb_, half=half, nt=nt: e.copy(out=n3[:nt, half * 512:(half + 1) * 512], in_=pb_[:nt, :]))
                    S.op("pool", ["n3", "gt"], ["gt"], lambda e, nt=nt: e.tensor_tensor(out=yo[:nt, :], in0=yo[:nt, :], in1=n3[:nt, :], op=ALU.mult))
                    S.op("pool", [hk_, "gt"], ["gt"], lambda e, nt=nt, tl=tl: e.tensor_tensor(out=yo[:nt, :], in0=yo[:nt, :], in1=hb[:nt, tl, :], op=ALU.add))
                    S.dma(["gt"], [], y[r0:r0 + nt, :], yo[:nt, :], q="pool")

            def F(fn, *a):
                return lambda: fn(*a)

            nblk = len(blocks)
            FE_a1(0, 0); FE_a2(0, 0); FE_a1(0, 1); FE_a2(0, 1); FE_a3(0)
            for tl_ in range(2):
                FE_t(0, tl_); FE_x(0, tl_)
            for bi in range(nblk):
                nx = bi + 1 < nblk
                inj = {}
                if bi > 0:
                    inj.update({1: [F(PLE_p1, bi - 1, 0)], 5: [F(PLE_p2, bi - 1, 0)], 8: [F(PLE_p3, bi - 1, 0)],
                                9: [F(PLE_p1, bi - 1, 1)], 13: [F(PLE_p2, bi - 1, 1)], 16: [F(PLE_p3, bi - 1, 1)]})
                    inj[28] = [F(FE_x, bi, 1)]
                if nx:
                    inj.update({17: [F(FE_a1, bi + 1, 0)], 20: [F(FE_a2, bi + 1, 0)], 21: [F(FE_a1, bi + 1, 1)], 24: [F(FE_a2, bi + 1, 1)], 26: [F(FE_a3, bi + 1)]})
                BE_hid(bi, inj)
                BE_W(bi)
                inj = {}
                if nx:
                    inj = {0: [F(FE_t, bi + 1, 0)], 24: [F(FE_x, bi + 1, 0)], 25: [F(FE_t, bi + 1, 1)]}
                BE_out(bi, inj)
                PLE_evac(bi)
            for tl_ in range(2):
                PLE_p1(nblk - 1, tl_); PLE_p2(nblk - 1, tl_); PLE_p3(nblk - 1, tl_)

        S.barrier()
    return nc


_CACHE = {}


def kernel(**inp):
    f = lambda a: np.ascontiguousarray(np.asarray(a, dtype=np.float32))
    NSEQ, NSAMP = 4, 2
    key = "full"
    if key not in _CACHE:
        _CACHE[key] = build(NSEQ, NSAMP)
    nc = _CACHE[key]
    w_in = f(inp["w_in"])[0]
    wi = np.ascontiguousarray(np.concatenate([w_in[:, 0:1536], w_in[:, 1544:5128]], axis=1))
    wfl = np.ascontiguousarray(w_in[:, 1536:1544])
    rb = f(inp["rel_bias_b"])[0]
    kl = np.arange(128)[:, None]; ql = np.arange(128)[None, :]
    idx0 = np.clip(kl - ql, -128, 128) + 128
    idx1 = np.clip(kl - ql - 128, -128, 128) + 128
    idx2 = np.zeros((128, 128), np.int64)
    biasT = np.ascontiguousarray(np.stack([rb[:, idx0], rb[:, idx1], rb[:, idx2]], axis=1))
    u = f(inp["peer_u"])[0]; v = f(inp["peer_v"])[0]
    uL = np.ascontiguousarray(u.reshape(128, 128, 8, 128).transpose(1, 3, 2, 0)).reshape(128 * 128, 1024)
    vL = np.ascontiguousarray(v.reshape(128, 128, 1024).transpose(1, 0, 2)).reshape(128 * 128, 1024)
    skT = np.ascontiguousarray(f(inp["peer_subkeys"])[0].transpose(0, 1, 3, 2))
    shared = dict(wi=wi, wfl=wfl, g_mix=f(inp["g_mix"])[0], g_ffn=f(inp["g_ffn"])[0], g_ple=f(inp["g_ple"])[0], b_f=f(inp["b_f"])[0],
                  qn_a=f(inp["qn_a"])[0], kn_a=f(inp["kn_a"])[0], qn_b=f(inp["qn_b"])[0], kn_b=f(inp["kn_b"])[0], biasT=biasT,
                  wupa=f(inp["w_up_a"])[0], wupb=f(inp["w_up_b"])[0], wout=f(inp["w_out"])[0], wq=f(inp["peer_wq"])[0],
                  wpg=f(inp["w_ple_gate"])[0], wpp=f(inp["w_ple_proj"])[0], skT=skT, uL=uL, vL=vL)
    xpr = f(inp["x_prompt"]); xsa = f(inp["x_sample"]); ppr = f(inp["p_prompt"])[0]; psa = f(inp["p_sample"])[0]
    cak = f(inp["cache_a_k"])[0]; cav = f(inp["cache_a_v"])[0]; calf = f(inp["cache_a_logf"])[0]
    cbk = f(inp["cache_b_k"])[0]; cbv = f(inp["cache_b_v"])[0]
    in_maps = []
    for c in range(NCORES):
        m = dict(shared)
        m["xp"] = xpr[4 * c:4 * c + 4].reshape(4 * T, D); m["xs"] = xsa[2 * c:2 * c + 2].reshape(2 * NS, D)
        m["pp"] = ppr[4 * c:4 * c + 4].reshape(4 * T, 256); m["ps"] = psa[2 * c:2 * c + 2].reshape(2 * NS, 256)
        m["cak"] = cak[2 * c:2 * c + 2].reshape(2, PAST, 512); m["cav"] = cav[2 * c:2 * c + 2].reshape(2, PAST, 512)
        m["calf"] = calf[2 * c:2 * c + 2]
        m["cbk"] = cbk[2 * c:2 * c + 2].reshape(2, LB, 512); m["cbv"] = cbv[2 * c:2 * c + 2].reshape(2, LB, 512)
        in_maps.append({k: np.ascontiguousarray(a) for k, a in m.items()})
    res = run_bass_kernel_spmd(nc, in_maps, core_ids=list(range(NCORES))).results
    cat = lambda k: np.concatenate([np.asarray(r[k], dtype=np.float32) for r in res], axis=0)
    yall = [np.asarray(r["y"], dtype=np.float32) for r in res]
    y_p = np.concatenate([a[:4 * T] for a in yall], axis=0).reshape(32, T, D)
    y_s = np.concatenate([a[4 * T:] for a in yall], axis=0).reshape(16, NS, D)
    return (y_p, y_s,
            cat("ak").reshape(1, 32, T, NH, HD), cat("av").reshape(1, 32, T, NH, HD), cat("af").reshape(1, 32, T, NH),
            cat("bk").reshape(1, 32, LB, NH, HD), cat("bv").reshape(1, 32, LB, NH, HD),
            cat("aks").reshape(1, 16, NS, NH, HD), cat("avs").reshape(1, 16, NS, NH, HD), cat("afs").reshape(1, 16, NS, NH),
            cat("bks").reshape(1, 16, NS, NH, HD), cat("bvs").reshape(1, 16, NS, NH, HD))
```

```python
import numpy as np
from contextlib import ExitStack
import concourse.bass as bass
import concourse.mybir as mybir
from concourse.bass_utils import run_bass_kernel_spmd

F32 = mybir.dt.float32; BF16 = mybir.dt.bfloat16; I32 = mybir.dt.int32; U32 = mybir.dt.uint32
ALU = mybir.AluOpType; AF = mybir.ActivationFunctionType; AX = mybir.AxisListType

NCORES = 8
D = 1024; T = 2048; NH = 8; HD = 64; PAST = 4096; LB = 512; NS = 16
EPS = 1e-6
NEG = -30000.0
GELU = AF.Gelu_apprx_tanh


class Sched:
    def __init__(self, nc, es, n_dma_sems=40):
        self.nc = nc
        self.eng = {"pe": nc.tensor, "dve": nc.vector, "act": nc.scalar, "pool": nc.gpsimd, "sp": nc.sync}
        self.sem = {k: es.enter_context(nc.semaphore("s_" + k)) for k in ("pe", "dve", "act", "pool")}
        self.cnt = {k: 0 for k in self.sem}
        self.dsem = [es.enter_context(nc.semaphore("d%d" % i)) for i in range(n_dma_sems)]
        self.dcnt = [0] * n_dma_sems
        self.dpool = {"sp": list(range(0, n_dma_sems // 2)), "pool": list(range(n_dma_sems // 2, n_dma_sems))}
        self.dnext = {"sp": 0, "pool": 0}
        self.seen = {k: {} for k in self.eng}
        self.lastw = {}
        self.readers = {}
        self.ninstr = 0

    def _need(self, e, deps):
        for (src, n) in deps:
            if src == "pe" and e == "pe":
                continue
            if self.seen[e].get(src, 0) >= n:
                continue
            sem = self.sem[src] if isinstance(src, str) else self.dsem[src]
            self.eng[e].wait_ge(sem, n)
            self.seen[e][src] = n

    def _deps(self, reads, writes, e=None):
        deps = []
        for k in reads:
            if k in self.lastw:
                deps.append(self.lastw[k])
        for k in writes:
            if k in self.lastw and self.lastw[k][0] != e:
                deps.append(self.lastw[k])
            for s, n in self.readers.get(k, {}).items():
                if s != e:
                    deps.append((s, n))
        return deps

    def _commit(self, tag, reads, writes):
        for k in reads:
            self.readers.setdefault(k, {})[tag[0]] = tag[1]
        for k in writes:
            self.lastw[k] = tag
            self.readers[k] = {}

    def op(self, e, reads, writes, fn):
        self._need(e, self._deps(reads, writes, e))
        ins = fn(self.eng[e])
        self.cnt[e] += 1
        ins.then_inc(self.sem[e], 1)
        self._commit((e, self.cnt[e]), reads, writes)
        self.ninstr += 1
        return ins

    def dma(self, reads, writes, out, in_, q="sp", **kw):
        pool_ = self.dpool[q]
        i = pool_[self.dnext[q] % len(pool_)]
        self.dnext[q] += 1
        deps = self._deps(reads, writes)
        if self.dcnt[i] > 0:
            deps.append((i, self.dcnt[i]))
        self._need(q, deps)
        ins = self.eng[q].dma_start(out=out, in_=in_, **kw)
        self.dcnt[i] += 16
        ins.then_inc(self.dsem[i], 16)
        self._commit((i, self.dcnt[i]), reads, writes)
        self.ninstr += 1
        return ins

    def barrier(self):
        deps = [(i, c) for i, c in enumerate(self.dcnt) if c > 0]
        deps += [(k, c) for k, c in self.cnt.items() if c > 0]
        for e in self.eng:
            self._need(e, deps)
        self.lastw = {}
        self.readers = {}


def build(NSEQ=4, NSAMP=2, PEER=True):
    nc = bass.Bass("TRN2", target_bir_lowering=False)
    NTOK = NSEQ * T
    NALL = NTOK + NSAMP * NS

    def DI(name, shape, dt=F32):
        return nc.dram_tensor(name, list(shape), dt, kind="ExternalInput").ap()

    def DO(name, shape, dt=F32):
        return nc.dram_tensor(name, list(shape), dt, kind="ExternalOutput").ap()

    def DS(name, shape, dt=BF16):
        return nc.dram_tensor(name, list(shape), dt, kind="Internal").ap()

    xp = DI("xp", [NTOK, D]); xs = DI("xs", [NSAMP * NS, D])
    pp = DI("pp", [NTOK, 256]); ps_ = DI("ps", [NSAMP * NS, 256])
    cak = DI("cak", [NSAMP, PAST, 512]); cav = DI("cav", [NSAMP, PAST, 512]); calf = DI("calf", [NSAMP, PAST, NH])
    cbk = DI("cbk", [NSAMP, LB, 512]); cbv = DI("cbv", [NSAMP, LB, 512])
    wi = DI("wi", [D, 5120]); wfl = DI("wfl", [D, NH])
    g_mix = DI("g_mix", [D]); g_ffn = DI("g_ffn", [D]); g_ple = DI("g_ple", [D]); b_f = DI("b_f", [NH])
    qn_a = DI("qn_a", [HD]); kn_a = DI("kn_a", [HD]); qn_b = DI("qn_b", [HD]); kn_b = DI("kn_b", [HD])
    biasT = DI("biasT", [NH, 3, 128, 128])
    wupa = DI("wupa", [512, D]); wupb = DI("wupb", [512, D]); wout = DI("wout", [D, D])
    wq = DI("wq", [D, D]); wpg = DI("wpg", [D, D]); wpp = DI("wpp", [256, D])
    skT = DI("skT", [NH, 2, 64, 128])
    uL = DI("uL", [128 * 128, D]); vL = DI("vL", [128 * 128, D])

    y = DO("y", [NALL, D])
    ak = DO("ak", [NTOK, 512]); av = DO("av", [NTOK, 512]); af = DO("af", [NTOK, NH])
    bk = DO("bk", [NSEQ * LB, 512]); bv = DO("bv", [NSEQ * LB, 512])
    aks = DO("aks", [NSAMP * NS, 512]); avs = DO("avs", [NSAMP * NS, 512]); afs = DO("afs", [NSAMP * NS, NH])
    bks = DO("bks", [NSAMP * NS, 512]); bvs = DO("bvs", [NSAMP * NS, 512])

    wi_bf = DS("wi_bf", [D, 5120])
    wupa_bf = DS("wupa_bf", [512, D]); wupb_bf = DS("wupb_bf", [512, D]); wout_bf = DS("wout_bf", [D, D])
    wq_bf = DS("wq_bf", [D, D]); wpg_bf = DS("wpg_bf", [D, D]); wpp_bf = DS("wpp_bf", [256, D])
    u_bf = DS("u_bf", [128 * 128, D]); v_bf = DS("v_bf", [128 * 128, D])
    gate_s = DS("gate_s", [NALL, 2048])
    yTa_s = DS("yTa_s", [512, NALL]); yTb_s = DS("yTb_s", [512, NALL])
    h1_s = DS("h1_s", [NALL, D], F32)

    with ExitStack() as es0:
        S = Sched(nc, es0)

        def PS(name, shape, dt=F32):
            return es0.enter_context(nc.psum_tensor(name, shape, dt))

        Fb = [PS("f%d" % i, [128, 512]) for i in range(8)]
        Tb = [Fb[6][:, :].bitcast(BF16), Fb[7][:, :].bitcast(BF16)]

        uid = [0]

        def SB(es, name, shape, dt=F32):
            uid[0] += 1
            return es.enter_context(nc.sbuf_tensor("%s_%d" % (name, uid[0]), shape, dt))

        ident = SB(es0, "ident", [128, 128], BF16)
        identf = SB(es0, "identf", [128, 128], F32)
        iota_f = SB(es0, "iota_f", [128, 128], F32)
        ones_f = SB(es0, "ones_f", [128, 64], F32)
        es1 = es0.enter_context(ExitStack())
        Umat = SB(es1, "Umat", [128, 128], F32)
        sel127 = SB(es1, "sel127", [128, 128], F32)
        tri = SB(es1, "tri", [128, 128], BF16)
        S.op("pool", [], ["ident"], lambda e: e.memset(ident[:], 1.0))
        S.op("pool", ["ident"], ["ident"], lambda e: e.affine_select(out=ident[:], in_=ident[:], pattern=[[-1, 128]], compare_op=ALU.is_equal, fill=0.0, base=0, channel_multiplier=1))
        S.op("pool", [], ["identf"], lambda e: e.memset(identf[:], 1.0))
        S.op("pool", ["identf"], ["identf"], lambda e: e.affine_select(out=identf[:], in_=identf[:], pattern=[[-1, 128]], compare_op=ALU.is_equal, fill=0.0, base=0, channel_multiplier=1))
        S.op("pool", [], ["Umat"], lambda e: e.memset(Umat[:], 1.0))
        S.op("pool", ["Umat"], ["Umat"], lambda e: e.affine_select(out=Umat[:], in_=Umat[:], pattern=[[1, 128]], compare_op=ALU.is_ge, fill=0.0, base=0, channel_multiplier=-1))
        S.op("pool", [], ["tri"], lambda e: e.memset(tri[:], 1.0))
        S.op("pool", ["tri"], ["tri"], lambda e: e.affine_select(out=tri[:], in_=tri[:], pattern=[[1, 128]], compare_op=ALU.is_ge, fill=0.0, base=0, channel_multiplier=-1))
        S.op("pool", [], ["sel127"], lambda e: e.memset(sel127[:], 1.0))
        S.op("pool", ["sel127"], ["sel127"], lambda e: e.affine_select(out=sel127[:], in_=sel127[:], pattern=[[0, 128]], compare_op=ALU.is_equal, fill=0.0, base=-127, channel_multiplier=1))
        S.op("pool", [], ["iota_f"], lambda e: e.iota(iota_f[:], pattern=[[1, 128]], base=0, channel_multiplier=0, allow_small_or_imprecise_dtypes=True))
        S.op("pool", [], ["ones_f"], lambda e: e.memset(ones_f[:], 1.0))

        with ExitStack() as es:
            stg = [SB(es, "stg%d" % i, [128, 2048], F32) for i in range(6)]
            stb = [SB(es, "stb%d" % i, [128, 2048], BF16) for i in range(6)]
            cnt = [0]

            def conv(src, dst, R, C):
                sv = src.rearrange("(n p) c -> n p c", p=128)
                dv = dst.rearrange("(n p) c -> n p c", p=128)
                for n in range(R // 128):
                    for c0 in range(0, C, 2048):
                        cw = min(2048, C - c0)
                        i = cnt[0] % 6
                        cnt[0] += 1
                        S.dma([], ["stg%d" % i], stg[i][:, :cw], sv[n, :, c0:c0 + cw])
                        e = ("dve", "act", "dve", "act", "dve", "act")[i]
                        if e == "act":
                            S.op(e, ["stg%d" % i], ["stb%d" % i], lambda g, i=i, cw=cw: g.copy(out=stb[i][:, :cw], in_=stg[i][:, :cw]))
                        else:
                            S.op(e, ["stg%d" % i], ["stb%d" % i], lambda g, i=i, cw=cw: g.tensor_copy(out=stb[i][:, :cw], in_=stg[i][:, :cw]))
                        S.dma(["stb%d" % i], [], dv[n, :, c0:c0 + cw], stb[i][:, :cw], q="pool")

            conv(wi, wi_bf, D, 5120)
            conv(wupa, wupa_bf, 512, D); conv(wupb, wupb_bf, 512, D); conv(wout, wout_bf, D, D)
            conv(wq, wq_bf, D, D); conv(wpg, wpg_bf, D, D); conv(wpp, wpp_bf, 256, D)
            if PEER:
                conv(uL, u_bf, 128 * 128, D); conv(vL, v_bf, 128 * 128, D)
        S.barrier()

        gmix_bc = SB(es1, "gmix_bc", [128, D])
        bf_bc = SB(es1, "bf_bc", [128, NH])
        qna_bc = SB(es1, "qna_bc", [128, HD]); kna_bc = SB(es1, "kna_bc", [128, HD])
        qnb_bc = SB(es1, "qnb_bc", [128, HD]); knb_bc = SB(es1, "knb_bc", [128, HD])
        negC = SB(es1, "negC", [128, 4])
        for nm, t_, src in (("gmix_bc", gmix_bc, g_mix), ("bf_bc", bf_bc, b_f),
                            ("qna_bc", qna_bc, qn_a), ("kna_bc", kna_bc, kn_a), ("qnb_bc", qnb_bc, qn_b), ("knb_bc", knb_bc, kn_b)):
            S.dma([], [nm], t_[:], src.partition_broadcast(128))
        for col, (qk, kk, qt, kt) in enumerate((("qna_bc", "kna_bc", qna_bc, kna_bc), ("qnb_bc", "knb_bc", qnb_bc, knb_bc))):
            S.op("dve", [qk], ["negC"], lambda e, qt=qt, col=col: e.tensor_reduce(out=negC[:, 2 + col:3 + col], in_=qt[:], axis=AX.X, op=ALU.max, apply_absolute_value=True))
            S.op("dve", [kk], ["negC"], lambda e, kt=kt, col=col: e.tensor_reduce(out=negC[:, col:col + 1], in_=kt[:], axis=AX.X, op=ALU.max, apply_absolute_value=True))
            S.op("dve", ["negC"], ["negC"], lambda e, col=col: e.scalar_tensor_tensor(out=negC[:, col:col + 1], in0=negC[:, col:col + 1], scalar=-8.0, in1=negC[:, 2 + col:3 + col], op0=ALU.mult, op1=ALU.mult))
        S.op("dve", ["qna_bc"], ["qna_bc"], lambda e: e.tensor_scalar(out=qna_bc[:], in0=qna_bc[:], scalar1=0.125, scalar2=None, op0=ALU.mult))
        S.op("dve", ["qnb_bc"], ["qnb_bc"], lambda e: e.tensor_scalar(out=qnb_bc[:], in0=qnb_bc[:], scalar1=0.125, scalar2=None, op0=ALU.mult))
        wfl_bf = SB(es1, "wfl_bf", [128, 8, NH], BF16)
        bias_bf = SB(es1, "bias_bf", [128, NH, 4, 128], BF16)
        with ExitStack() as es:
            wfl_f = SB(es, "wfl_f", [128, 8, NH]); bias_f = SB(es, "bias_f", [128, NH, 3, 128])
            S.dma([], ["wfl_f"], wfl_f[:], wfl.rearrange("(kc p) h -> p kc h", p=128))
            S.op("dve", ["wfl_f"], ["wfl_bf"], lambda e: e.tensor_copy(out=wfl_bf[:], in_=wfl_f[:]))
            for h in range(NH):
                S.dma([], ["bias_f%d" % h], bias_f[:, h], biasT[h].rearrange("v k q -> k v q"))
                S.op("dve", ["bias_f%d" % h], ["bias_bf"], lambda e, h=h: e.tensor_copy(out=bias_bf[:, h, 0:3, :], in_=bias_f[:, h]))
                S.op("dve", ["bias_f%d" % h], ["bias_bf"], lambda e, h=h: e.tensor_copy(out=bias_bf[:, h, 3, :], in_=bias_f[:, h, 2, :]))
            S.op("pool", [], ["bias_bf"], lambda e: e.memset(bias_bf[64:128, :, 0, 0:64], NEG))
            S.op("pool", [], ["bias_bf"], lambda e: e.memset(bias_bf[0:64, :, 3, 64:128], NEG))
            S.barrier()

        sel15 = SB(es1, "sel15", [128, 128], F32)
        S.op("pool", [], ["sel15"], lambda e: e.memset(sel15[:], 1.0))
        S.op("pool", ["sel15"], ["sel15"], lambda e: e.affine_select(out=sel15[:], in_=sel15[:], pattern=[[0, 128]], compare_op=ALU.is_equal, fill=0.0, base=-15, channel_multiplier=1))

        def rms_rstd(src_key, src_ap, nt, width, acc, acck, scr, scrk):
            S.op("act", [src_key], [scrk, acck], lambda e: e.activation(out=scr[:nt, :width], in_=src_ap, func=AF.Square, scale=float(width) ** -0.5, accum_out=acc[:nt, 0:1]))
            S.op("act", [acck], [acck], lambda e: e.activation(out=acc[:nt, 0:1], in_=acc[:nt, 0:1], func=AF.Ln, bias=EPS, scale=1.0))
            S.op("act", [acck], [acck], lambda e: e.activation(out=acc[:nt, 0:1], in_=acc[:nt, 0:1], func=AF.Exp, scale=-0.5))

        def run_sequence(es, seq, is_sample):
            if not is_sample:
                tiles = [(seq * T + i * 128, 128, i * 128, i) for i in range(16)]
                groups = [tiles[g * 4:(g + 1) * 4] for g in range(4)]
                TK = T; nkt = 16; x_d = xp; tok0 = seq * T
                o_ak, o_av, o_af, o_bk, o_bv = ak, av, af, bk, bv
                TKB = T; nktb = 16; TQ = T; nqt = 16
            else:
                tiles = [(seq * NS, NS, PAST, 32)]
                groups = [tiles]
                TK = PAST + NS; nkt = 33; x_d = xs; tok0 = NTOK + seq * NS
                o_ak, o_av, o_af, o_bk, o_bv = aks, avs, afs, bks, bvs
                TKB = LB + NS; nktb = 5; TQ = NS; nqt = 1
            qTa = SB(es, "qTa", [128, 4, TQ], BF16); kTa = SB(es, "kTa", [128, 4, TK], BF16)
            qTb = SB(es, "qTb", [128, 4, TQ], BF16); kTb = SB(es, "kTb", [128, 4, TKB], BF16)
            va = SB(es, "va", [128, nkt, NH, 65], BF16); vb = SB(es, "vb", [128, nktb, NH, 65], BF16)
            lf_tok = SB(es, "lf_tok", [128, nkt, NH]); c_tok = SB(es, "c_tok", [128, nkt, NH])
            crefbc = SB(es, "crefbc", [128, nqt, NH]); BIASA = SB(es, "BIASA", [128, NH, nkt, nqt])
            xt = [SB(es, "xt%d" % i, [128, D]) for i in range(2)]
            scr = SB(es, "scr", [128, D]); rs = SB(es, "rs", [128, 2])
            n1 = SB(es, "n1", [128, D], BF16); n1T = SB(es, "n1T", [128, 8, 512], BF16)
            wblk = [SB(es, "wblk%d" % i, [128, 8, 512], BF16) for i in range(2)]
            zf = SB(es, "zf", [128, 512]); zf2 = SB(es, "zf2", [128, 512]); zb = SB(es, "zb", [128, 512], BF16)
            ssq = SB(es, "ssq", [128, NH]); lft = SB(es, "lft", [128, NH])
            gbf = SB(es, "gbf", [128, 512], BF16)
            ptb = [SB(es, "pt%d" % i, [128, 512], BF16) for i in range(4)]
            rinv = SB(es, "rinv", [128, 512]); bcs = SB(es, "bcs", [64, 512]); yst = SB(es, "yst", [64, 512], BF16)
            S.op("pool", [], [("c", j) for j in range(nkt)], lambda e: e.memset(c_tok[:], 0.0))
            S.op("pool", [], [("lf", j) for j in range(nkt)], lambda e: e.memset(lf_tok[:], 0.0))
            S.op("pool", [], ["va"], lambda e: e.memset(va[:, :, :, 64:65], 1.0))
            S.op("pool", [], ["vb"], lambda e: e.memset(vb[:, :, :, 64:65], 1.0))

            def cumsum_tile(j, nt):
                first = (j == 0)
                if not first:
                    S.op("pe", [("c", j - 1), "sel127"], ["F5"], lambda e: e.matmul(Fb[5][:nt, 0:NH], lhsT=sel127[:, :nt], rhs=c_tok[:, j - 1, :], start=True, stop=False))
                S.op("pe", [("lf", j), "Umat"], ["F5"], lambda e: e.matmul(Fb[5][:nt, 0:NH], lhsT=Umat[:nt, :nt], rhs=lf_tok[:nt, j, :], start=first, stop=True))
                S.op("act", [], ["F5", ("c", j)], lambda e: e.copy(out=c_tok[:nt, j, :], in_=Fb[5][:nt, 0:NH]))

            def transp4(src, nt, dst, c0, dkey):
                for c in range(4):
                    S.op("pe", ["zb", "ident"], ["T1"], lambda e, c=c: e.transpose(out=Tb[1][:, c * 128:c * 128 + nt], in_=src[:nt, c * 128:(c + 1) * 128], identity=ident[:nt, :nt]))
                S.op("dve", [], ["T1", dkey], lambda e: e.tensor_copy(out=dst[:, :, c0:c0 + nt], in_=Tb[1][:, 0:512].rearrange("p (c t) -> p c t", c=4)[:, :, :nt]))

            if is_sample:
                for (ck, cv, kT_, v_, ntl, kn, vn) in ((cak, cav, kTa, va, 32, "kTa", "va"), (cbk, cbv, kTb, vb, 4, "kTb", "vb")):
                    for j in range(ntl):
                        S.dma([], ["xt0"], xt[0][:, 0:512], ck[seq, j * 128:(j + 1) * 128, :])
                        S.dma([], ["xt1"], xt[1][:, 0:512], cv[seq, j * 128:(j + 1) * 128, :])
                        S.op("act", ["xt0"], ["zb"], lambda e: e.copy(out=zb[:], in_=xt[0][:, 0:512]))
                        transp4(zb, 128, kT_, j * 128, (kn, j))
                        S.op("pool", ["xt1"], [(vn, j)], lambda e, v_=v_, j=j: e.tensor_copy(out=v_[:, j, :, 0:64], in_=xt[1][:, 0:512].rearrange("p (h d) -> p h d", h=NH)))
                S.dma([], [("lf", j) for j in range(32)], lf_tok[:, 0:32, :], calf[seq].rearrange("(j p) h -> p j h", p=128))
                for j in range(32):
                    cumsum_tile(j, 128)

            def qknorm(zp, zk, nt, gain, gk, dst, dk):
                S.op("act", [], [zk, "zf"], lambda e: e.activation(out=zf[:nt, :], in_=zp[:nt, :], func=AF.Square))
                S.op("dve", ["zf"], ["ssq"], lambda e: e.tensor_reduce(out=ssq[:nt, :], in_=zf[:nt, :].rearrange("p (h d) -> p h d", h=NH), axis=AX.X, op=ALU.add))
                S.op("act", ["ssq"], ["ssq"], lambda e: e.activation(out=ssq[:nt, :], in_=ssq[:nt, :], func=AF.Sqrt, bias=EPS, scale=1.0 / HD))
                S.op("dve", ["ssq"], ["ssq"], lambda e: e.reciprocal(out=ssq[:nt, :], in_=ssq[:nt, :]))
                S.op("dve", ["ssq"], [zk, "zf"], lambda e: e.tensor_tensor(out=zf[:nt, :].rearrange("p (h d) -> p h d", h=NH), in0=zp[:nt, :].rearrange("p (h d) -> p h d", h=NH),
                                                                           in1=ssq[:nt, :].unsqueeze(2).to_broadcast([nt, NH, HD]), op=ALU.mult))
                S.op("pool", ["zf", gk], [dk], lambda e: e.tensor_tensor(out=dst[:nt, :].rearrange("p (h d) -> p h d", h=NH), in0=zf[:nt, :].rearrange("p (h d) -> p h d", h=NH),
                                                                         in1=gain[:nt, :].unsqueeze(1).to_broadcast([nt, NH, HD]), op=ALU.mult))

            n1T2 = [n1T, SB(es, "n1Tb", [128, 8, 512], BF16)]
            n1b = [n1, SB(es, "n1b", [128, D], BF16)]
            zf_ = [zf, SB(es, "zfb", [128, 512])]; zf2_ = [zf2, SB(es, "zf2b", [128, 512])]; zb_ = [zb] + [SB(es, "zbb%d" % i, [128, 512], BF16) for i in range(3)]
            ssq_ = [ssq, SB(es, "ssqb", [128, NH])]; gbf_ = [gbf, SB(es, "gbfb", [128, 512], BF16)]
            pend = []
            ecnt = [0]; zrot = [0]; trot = [0]

            mmc = [0]

            def flush(lag=0):
                while pend and mmc[0] - pend[0][0] >= lag:
                    pend.pop(0)[1]()

            def zbank():
                k = zrot[0] % 5; zrot[0] += 1
                return Fb[k], "F%d" % k

            def P_nonpe(grp, gi):
                r0, nt, c0, ti = grp[gi]
                xb = xt[gi % 2]; xk = "xt%d" % (gi % 2); nb_ = n1b[gi % 2]; nk_ = "n1b%d" % (gi % 2)
                S.dma([], [xk], xb[:nt, :], x_d[r0:r0 + nt, :])
                rms_rstd(xk, xb[:nt, :], nt, D, rs, "rs", scr, "scr")
                S.op("dve", [xk, "rs", "gmix_bc"], [nk_], lambda e: e.scalar_tensor_tensor(out=nb_[:nt, :], in0=xb[:nt, :], scalar=rs[:nt, 0:1], in1=gmix_bc[:nt, :], op0=ALU.mult, op1=ALU.mult))

            def P_pe(grp, gi, g):
                r0, nt, c0, ti = grp[gi]
                nb_ = n1b[gi % 2]; nk_ = "n1b%d" % (gi % 2); nT = n1T2[g % 2]
                for kc in range(8):
                    S.op("pe", [nk_, "ident"], ["T0"], lambda e, kc=kc: e.transpose(out=Tb[0][:, kc * 128:kc * 128 + nt], in_=nb_[:nt, kc * 128:(kc + 1) * 128], identity=ident[:nt, :nt]))
                S.op("act", [], ["T0", ("n1T", g % 2, gi)], lambda e: e.copy(out=nT[:, :, gi * 128:gi * 128 + nt], in_=Tb[0][:, :].rearrange("p (k t) -> p k t", k=8)[:, :, :nt]))

            def L_(grp, gi, g):
                r0, nt, c0, ti = grp[gi]
                nT = n1T2[g % 2]
                zp, zk = zbank()
                for kc in range(8):
                    S.op("pe", [("n1T", g % 2, gi), "wfl_bf"], [zk], lambda e, kc=kc: e.matmul(zp[:nt, 0:NH], lhsT=nT[:, kc, gi * 128:gi * 128 + nt], rhs=wfl_bf[:, kc, :], start=(kc == 0), stop=(kc == 7)))
                S.op("dve", ["bf_bc"], [zk, "lft"], lambda e: e.tensor_tensor(out=lft[:nt, :], in0=zp[:nt, 0:NH], in1=bf_bc[:nt, :], op=ALU.add))
                S.op("act", ["lft"], ["lft"], lambda e: e.activation(out=lft[:nt, :], in_=lft[:nt, :], func=AF.Exp, scale=-1.0))
                S.op("act", ["lft"], ["lft"], lambda e: e.activation(out=lft[:nt, :], in_=lft[:nt, :], func=AF.Ln, bias=1.0, scale=1.0))
                S.op("dve", ["lft"], [("lf", ti)], lambda e: e.tensor_scalar(out=lf_tok[:nt, ti, :], in0=lft[:nt, :], scalar1=-1.0, scalar2=None, op0=ALU.mult))
                pend.append((mmc[0], lambda: cumsum_tile(ti, nt)))

            def qknorm2(zp, zk, nt, gain, gk, dst, dk, zfx, zfk, sq, sqk):
                S.op("act", [], [zk, zfk], lambda e: e.activation(out=zfx[:nt, :], in_=zp[:nt, :], func=AF.Square))
                S.op("dve", [zfk], [sqk], lambda e: e.tensor_reduce(out=sq[:nt, :], in_=zfx[:nt, :].rearrange("p (h d) -> p h d", h=NH), axis=AX.X, op=ALU.add))
                S.op("act", [sqk], [sqk], lambda e: e.activation(out=sq[:nt, :], in_=sq[:nt, :], func=AF.Ln, bias=EPS, scale=1.0 / HD))
                S.op("act", [sqk], [sqk], lambda e: e.activation(out=sq[:nt, :], in_=sq[:nt, :], func=AF.Exp, scale=-0.5))
                S.op("dve", [sqk], [zk, zfk], lambda e: e.tensor_tensor(out=zfx[:nt, :].rearrange("p (h d) -> p h d", h=NH), in0=zp[:nt, :].rearrange("p (h d) -> p h d", h=NH),
                                                                        in1=sq[:nt, :].unsqueeze(2).to_broadcast([nt, NH, HD]), op=ALU.mult))
                S.op("pool", [zfk, gk], [dk], lambda e: e.tensor_tensor(out=dst[:nt, :].rearrange("p (h d) -> p h d", h=NH), in0=zfx[:nt, :].rearrange("p (h d) -> p h d", h=NH),
                                                                       in1=gain[:nt, :].unsqueeze(1).to_broadcast([nt, NH, HD]), op=ALU.mult))

            def transp4d(src, sk_, nt, dst, c0_, dkey):
                k = trot[0] % 2; trot[0] += 1
                tb = Tb[k]; tk = "T%d" % k
                for c in range(4):
                    S.op("pe", [sk_, "ident"], [tk], lambda e, c=c: e.transpose(out=tb[:, c * 128:c * 128 + nt], in_=src[:nt, c * 128:(c + 1) * 128], identity=ident[:nt, :nt]))
                S.op("dve", [], [tk, dkey], lambda e: e.tensor_copy(out=dst[:, :, c0_:c0_ + nt], in_=tb[:, 0:512].rearrange("p (c t) -> p c t", c=4)[:, :, :nt]))

            wv = wi_bf.rearrange("(kc p) c -> p kc c", p=128)
            for gi in range(len(groups[0])):
                P_nonpe(groups[0], gi); P_pe(groups[0], gi, 0)
            for gi in range(len(groups[0])):
                L_(groups[0], gi, 0)
            for g, grp in enumerate(groups):
                nxt = groups[g + 1] if g + 1 < len(groups) else None
                nT = n1T2[g % 2]
                for bi in range(10):
                    if nxt is not None:
                        if 1 <= bi <= 4:
                            P_pe(nxt, bi - 1, g + 1)
                        if bi <= 3:
                            P_nonpe(nxt, bi)
                        if 2 <= bi <= 5:
                            L_(nxt, bi - 2, g + 1)
                    wb = wblk[bi % 2]; wk = "wblk%d" % (bi % 2)
                    S.dma([], [wk], wb[:], wv[:, :, bi * 512:(bi + 1) * 512])
                    for gi, (r0, nt, c0, ti) in enumerate(grp):
                        zp, zk = zbank()
                        for kc in range(8):
                            S.op("pe", [("n1T", g % 2, gi), wk], [zk], lambda e, kc=kc, gi=gi, nt=nt, zp=zp, wb=wb: e.matmul(zp[:nt, :], lhsT=nT[:, kc, gi * 128:gi * 128 + nt], rhs=wb[:, kc, :], start=(kc == 0), stop=(kc == 7)))
                        mmc[0] += 1
                        flush(3)
                        e_ = ecnt[0] % 2; e4 = ecnt[0] % 4; ecnt[0] += 1
                        zfx = zf_[e_]; zfk = "zf%d" % e_; zf2x = zf2_[e_]; zf2k = "zf2%d" % e_; zbx = zb_[e4]; zbk = "zb%d" % e4; sq = ssq_[e_]; sqk = "ssq%d" % e_
                        isA = bi < 3
                        if bi in (0, 3):
                            qknorm2(zp, zk, nt, qna_bc if isA else qnb_bc, "qna_bc" if isA else "qnb_bc", zbx, zbk, zfx, zfk, sq, sqk)
                            pend.append((mmc[0], lambda zbx=zbx, zbk=zbk, nt=nt, isA=isA, c0=c0, ti=ti: transp4d(zbx, zbk, nt, qTa if isA else qTb, c0 if not is_sample else 0, ("qTa" if isA else "qTb", ti))))
                        elif bi in (1, 4):
                            qknorm2(zp, zk, nt, kna_bc if isA else knb_bc, "kna_bc" if isA else "knb_bc", zf2x, zf2k, zfx, zfk, sq, sqk)
                            S.op("act", [zf2k], [zbk], lambda e, nt=nt, zbx=zbx, zf2x=zf2x: e.copy(out=zbx[:nt, :], in_=zf2x[:nt, :]))
                            kc0 = c0 if (isA or not is_sample) else LB
                            pend.append((mmc[0], lambda zbx=zbx, zbk=zbk, nt=nt, isA=isA, kc0=kc0, ti=ti: transp4d(zbx, zbk, nt, kTa if isA else kTb, kc0, ("kTa" if isA else "kTb", ti))))
                            if isA or is_sample:
                                S.dma([zf2k], [], (o_ak if isA else o_bk)[r0:r0 + nt, :], zf2x[:nt, :], q="pool")
                            elif ti >= 12:
                                S.dma([zf2k], [], o_bk[seq * LB + (ti - 12) * 128:seq * LB + (ti - 11) * 128, :], zf2x[:nt, :], q="pool")
                        elif bi in (2, 5):
                            S.op("act", [], [zk, zf2k], lambda e, nt=nt, zp=zp, zf2x=zf2x: e.copy(out=zf2x[:nt, :], in_=zp[:nt, :]))
                            v_ = va if isA else vb
                            tj = ti if (isA or not is_sample) else 4
                            S.op("pool", [zf2k], [("va" if isA else "vb", tj)], lambda e, nt=nt, v_=v_, tj=tj, zf2x=zf2x: e.tensor_copy(out=v_[:nt, tj, :, 0:64], in_=zf2x[:nt, :].rearrange("p (h d) -> p h d", h=NH)))
                            if isA or is_sample:
                                S.dma([zf2k], [], (o_av if isA else o_bv)[r0:r0 + nt, :], zf2x[:nt, :], q="pool")
                            elif ti >= 12:
                                S.dma([zf2k], [], o_bv[seq * LB + (ti - 12) * 128:seq * LB + (ti - 11) * 128, :], zf2x[:nt, :], q="pool")
                        else:
                            gx = gbf_[e_]; gk_ = "gbf%d" % e_
                            S.op("act", [], [zk, gk_], lambda e, nt=nt, zp=zp, gx=gx: e.activation(out=gx[:nt, :], in_=zp[:nt, :], func=AF.Sigmoid))
                            tr0 = tok0 + (c0 if not is_sample else 0)
                            S.dma([gk_], [], gate_s[tr0:tr0 + nt, (bi - 6) * 512:(bi - 5) * 512], gx[:nt, :], q="pool")
            flush()
            if not is_sample:
                S.dma([("lf", j) for j in range(16)], [], o_af[seq * T:(seq + 1) * T, :].rearrange("(j p) h -> p j h", p=128), lf_tok[:, 0:16, :])
            else:
                S.dma([("lf", 32)], [], o_af[seq * NS:(seq + 1) * NS, :], lf_tok[:NS, 32, :])

            S.barrier()
            ckeys = [("c", j) for j in range(nkt)]
            if not is_sample:
                S.op("pe", ckeys + ["sel127"], ["F5"], lambda e: e.matmul(Fb[5][:, 0:nkt * NH], lhsT=sel127[:, :], rhs=c_tok[:, :, :].rearrange("p j h -> p (j h)"), start=True, stop=True))
            else:
                S.op("pe", ckeys + ["sel15"], ["F5"], lambda e: e.matmul(Fb[5][:, 0:NH], lhsT=sel15[:NS, :], rhs=c_tok[:NS, 32, :], start=True, stop=True))
            S.op("act", [], ["F5", "crefbc"], lambda e: e.copy(out=crefbc[:, :, :].rearrange("p i h -> p (i h)"), in_=Fb[5][:, 0:nqt * NH]))
            for h in range(NH):
                S.op("dve", ckeys + ["crefbc"], [("BIASA", h)], lambda e, h=h: e.tensor_tensor(out=BIASA[:, h], in0=crefbc[:, :, h].unsqueeze(1).to_broadcast([128, nkt, nqt]),
                                                                                              in1=c_tok[:, :, h].unsqueeze(2).to_broadcast([128, nkt, nqt]), op=ALU.subtract))
                S.op("dve", ["negC", ("BIASA", h)], [("BIASA", h)], lambda e, h=h: e.tensor_scalar(out=BIASA[:, h], in0=BIASA[:, h], scalar1=negC[:, 0:1], scalar2=None, op0=ALU.add))

            cnt = [0]

            def normalize(c, h, qn, q0, dst_s, kb):
                acck = "F%d" % c
                sk = "F%d" % (4 + kb % 4); sp_ = Fb[4 + kb % 4]
                S.op("act", [], [acck, "rinv"], lambda e: e.activation(out=rinv[64:65, :qn], in_=Fb[c][64:65, :qn], func=AF.Ln))
                S.op("act", ["rinv"], ["rinv"], lambda e: e.activation(out=rinv[64:65, :qn], in_=rinv[64:65, :qn], func=AF.Exp, scale=-1.0))
                S.op("pe", ["rinv", "ones_f"], [sk], lambda e: e.matmul(sp_[:64, :qn], lhsT=ones_f[64:65, 0:64], rhs=rinv[64:65, :qn], start=True, stop=True))
                S.op("act", [], [sk, "bcs"], lambda e: e.copy(out=bcs[:64, :qn], in_=sp_[:64, :qn]))
                S.op("dve", ["bcs"], [acck, "yst"], lambda e: e.tensor_tensor(out=yst[:64, :qn], in0=Fb[c][:64, :qn], in1=bcs[:64, :qn], op=ALU.mult))
                S.dma(["yst"], [], dst_s[h * 64:(h + 1) * 64, tok0 + q0:tok0 + q0 + qn], yst[:64, :qn])

            qchunks = [(c * 512, 512) for c in range(4)] if not is_sample else [(0, NS)]
            stepsA = []
            for h in range(NH):
                for c, (q0, qn) in enumerate(qchunks):
                    jlast = (4 * c + 3) if not is_sample else 32
                    for j in range(jlast + 1):
                        stepsA.append((h, c, q0, qn, jlast, j))

            def A_qk(st):
                h, c, q0, qn, jlast, j = st
                hp, ho = h // 2, (h % 2) * 64
                nk = 128 if (not is_sample or j < 32) else NS
                qlo = max(q0, j * 128) if not is_sample else 0
                n = q0 + qn - qlo
                k = cnt[0]; cnt[0] += 1
                sk = "F%d" % (4 + k % 4); sp_ = Fb[4 + k % 4]
                S.op("pe", [("kTa", j), ("qTa", 0)], [sk], lambda e: e.matmul(sp_[:nk, :n], lhsT=kTa[ho:ho + 64, hp, j * 128:j * 128 + nk], rhs=qTa[ho:ho + 64, hp, qlo:qlo + n], start=True, stop=True))
                return (k, nk, qlo, n)

            def A_rest(st, info):
                h, c, q0, qn, jlast, j = st
                k, nk, qlo, n = info
                acck = "F%d" % c
                sk = "F%d" % (4 + k % 4); sp_ = Fb[4 + k % 4]
                pt = ptb[k % 4]; ptk = "pt%d" % (k % 4)
                nb_ = max(1, n // 128)
                for bq in range(nb_):
                    bw = min(128, n)
                    i = (qlo // 128 + bq) if not is_sample else 0
                    S.op("act", [("BIASA", h)], [sk, ptk], lambda e, bq=bq, bw=bw, i=i: e.activation(out=pt[:nk, bq * 128:bq * 128 + bw], in_=sp_[:nk, bq * 128:bq * 128 + bw], func=AF.Exp, bias=BIASA[:nk, h, j, i:i + 1], scale=1.0))
                diag = (j * 128 >= q0) if not is_sample else (j == 32)
                if diag:
                    bw = min(128, n)
                    S.op("pool", ["tri", ptk], [ptk], lambda e: e.tensor_tensor(out=pt[:nk, 0:bw], in0=pt[:nk, 0:bw], in1=tri[:nk, :bw], op=ALU.mult))
                S.op("pe", [ptk, ("va", j)], [acck], lambda e: e.matmul(Fb[c][:65, qlo - q0:qlo - q0 + n], lhsT=va[:nk, j, h, :], rhs=pt[:nk, :n], start=(j == 0), stop=(j == jlast), skip_group_check=True))
                if j == jlast:
                    pendN.append([0 if is_sample else 2, (c, h, qn, q0, yTa_s)])
                for pn in list(pendN):
                    if pn[0] == 0:
                        normalize(*pn[1], k); pendN.remove(pn)
                    else:
                        pn[0] -= 1

            LA = 2
            pendN = []
            infoA = [A_qk(stepsA[i]) for i in range(min(LA, len(stepsA)))]
            for si, st in enumerate(stepsA):
                if si + LA < len(stepsA):
                    infoA.append(A_qk(stepsA[si + LA]))
                A_rest(st, infoA[si])
            for pn in pendN:
                normalize(*pn[1], infoA[-1][0])

            blocksB = []
            for h in range(NH):
                nqi = 16 if not is_sample else 1
                for i in range(nqi):
                    js = list(range(max(0, i - 4), i + 1)) if not is_sample else list(range(5))
                    for j in js:
                        if not is_sample:
                            vi = {0: 0, 1: 1, 2: 2, 3: 2, 4: 3}[i - j]; nk = 128
                        else:
                            vi = {0: 2, 1: 2, 2: 2, 3: 1, 4: 0}[j]; nk = 128 if j < 4 else NS
                        blocksB.append((h, i, js, j, vi, nk))
            qwB = 128 if not is_sample else NS
            batchesB = []
            for blk in blocksB:
                if batchesB and len(batchesB[-1]) < 4 and batchesB[-1][0][5] == blk[5]:
                    batchesB[-1].append(blk)
                else:
                    batchesB.append([blk])

            def B_qk(batch):
                k = cnt[0]; cnt[0] += 1
                sk = "F%d" % (4 + k % 4); sp_ = Fb[4 + k % 4]
                for bx, (h, i, js, j, vi, nk) in enumerate(batch):
                    hp, ho = h // 2, (h % 2) * 64
                    S.op("pe", ["ident", "bias_bf"], [sk], lambda e, bx=bx, h=h, vi=vi, nk=nk: e.matmul(sp_[:nk, bx * qwB:(bx + 1) * qwB], lhsT=ident[:nk, :nk], rhs=bias_bf[:nk, h, vi, 0:qwB], start=True, stop=False, skip_group_check=True))
                    S.op("pe", [("kTb", j), ("qTb", 0)], [sk], lambda e, bx=bx, hp=hp, ho=ho, i=i, j=j, nk=nk: e.matmul(sp_[:nk, bx * qwB:(bx + 1) * qwB], lhsT=kTb[ho:ho + 64, hp, j * 128:j * 128 + nk], rhs=qTb[ho:ho + 64, hp, i * 128:i * 128 + qwB], start=False, stop=True, skip_group_check=True))
                return k

            def B_rest(batch, k):
                sk = "F%d" % (4 + k % 4); sp_ = Fb[4 + k % 4]
                pt = ptb[k % 4]; ptk = "pt%d" % (k % 4)
                nk = batch[0][5]; wtot = len(batch) * qwB
                S.op("act", ["negC"], [sk, ptk], lambda e: e.activation(out=pt[:nk, :wtot], in_=sp_[:nk, :wtot], func=AF.Exp, bias=negC[:nk, 1:2], scale=1.0))
                for bx, (h, i, js, j, vi, nk_) in enumerate(batch):
                    c = i // 4
                    S.op("pe", [ptk, ("vb", j)], ["F%d" % c], lambda e, bx=bx, h=h, i=i, j=j, js=js, c=c: e.matmul(Fb[c][:65, (i % 4) * 128:(i % 4) * 128 + qwB], lhsT=vb[:nk, j, h, :], rhs=pt[:nk, bx * qwB:(bx + 1) * qwB], start=(j == js[0]), stop=(j == js[-1]), skip_group_check=True))
                    if j == js[-1]:
                        if is_sample:
                            pendNB.append([0, (0, h, NS, 0, yTb_s)])
                        elif i % 4 == 3:
                            pendNB.append([1, (c, h, 512, c * 512, yTb_s)])
                for pn in list(pendNB):
                    if pn[0] == 0:
                        normalize(*pn[1], k); pendNB.remove(pn)
                    else:
                        pn[0] -= 1

            pendNB = []
            kB = B_qk(batchesB[0])
            for si, bt in enumerate(batchesB):
                kn = B_qk(batchesB[si + 1]) if si + 1 < len(batchesB) else None
                B_rest(bt, kB)
                klast = kB
                kB = kn
            for pn in pendNB:
                normalize(*pn[1], klast)

        for seq in range(NSEQ):
            with ExitStack() as es:
                run_sequence(es, seq, False)
                S.barrier()
        for seq in range(NSAMP):
            with ExitStack() as es:
                run_sequence(es, seq, True)
                S.barrier()

        alltiles = [(r, 128) for r in range(0, NTOK, 128)] + ([(NTOK, NSAMP * NS)] if NSAMP else [])

        def xrows(r0, nt):
            return xp[r0:r0 + nt, :] if r0 < NTOK else xs[0:nt, :]

        with ExitStack() as es:
            wa = SB(es, "wa", [128, 4, D], BF16); wb_ = SB(es, "wb_", [128, 4, D], BF16); wo = SB(es, "wo", [128, 8, D], BF16)
            S.dma([], ["wa"], wa[:], wupa_bf.rearrange("(kc p) c -> p kc c", p=128))
            S.dma([], ["wb_"], wb_[:], wupb_bf.rearrange("(kc p) c -> p kc c", p=128))
            S.dma([], ["wo"], wo[:], wout_bf.rearrange("(kc p) c -> p kc c", p=128))
            yta = [SB(es, "yta%d" % i, [128, 4, 128], BF16) for i in range(2)]
            ytb = [SB(es, "ytb%d" % i, [128, 4, 128], BF16) for i in range(2)]
            gsb = [SB(es, "gsb%d" % i, [128, 2048], BF16) for i in range(2)]
            xm = [SB(es, "xm%d" % i, [128, D]) for i in range(2)]
            m1 = SB(es, "m1", [128, D]); m2 = SB(es, "m2", [128, D]); mb2 = [SB(es, "mb%d" % i, [128, D], BF16) for i in range(2)]
            mT = SB(es, "mT", [128, 8, 128], BF16); h1t = [SB(es, "h1t%d" % i, [128, D]) for i in range(2)]
            def mergeA(it):
                r0, nt = alltiles[it]
                b = it % 2
                S.dma([], ["yta%d" % b], yta[b][:, :, :nt], yTa_s[:, r0:r0 + nt].rearrange("(kc p) t -> p kc t", p=128))
                S.dma([], ["ytb%d" % b], ytb[b][:, :, :nt], yTb_s[:, r0:r0 + nt].rearrange("(kc p) t -> p kc t", p=128))
                S.dma([], ["gsb%d" % b], gsb[b][:nt, :], gate_s[r0:r0 + nt, :])
                S.dma([], ["xm%d" % b], xm[b][:nt, :], xrows(r0, nt))
                for br, (yt, ytk, w_, wk) in enumerate(((yta[b], "yta%d" % b, wa, "wa"), (ytb[b], "ytb%d" % b, wb_, "wb_"))):
                    for half in range(2):
                        bank = br * 2 + half
                        for kc in range(4):
                            S.op("pe", [ytk, wk], ["F%d" % bank], lambda e, yt=yt, w_=w_, kc=kc, half=half, bank=bank, nt=nt: e.matmul(Fb[bank][:nt, :], lhsT=yt[:, kc, :nt], rhs=w_[:, kc, half * 512:(half + 1) * 512], start=(kc == 0), stop=(kc == 3)))
                for half in range(2):
                    S.op("dve", ["gsb%d" % b], ["F%d" % half, "m1"], lambda e, half=half, nt=nt, b=b: e.tensor_tensor(out=m1[:nt, half * 512:(half + 1) * 512], in0=Fb[half][:nt, :], in1=gsb[b][:nt, half * 512:(half + 1) * 512], op=ALU.mult))
                    S.op("dve", ["gsb%d" % b], ["F%d" % (2 + half), "m2"], lambda e, half=half, nt=nt, b=b: e.tensor_tensor(out=m2[:nt, half * 512:(half + 1) * 512], in0=Fb[2 + half][:nt, :], in1=gsb[b][:nt, 1024 + half * 512:1024 + (half + 1) * 512], op=ALU.mult))
                S.op("pool", ["m1", "m2"], ["mb%d" % b], lambda e, nt=nt, b=b: e.tensor_tensor(out=mb2[b][:nt, :], in0=m1[:nt, :], in1=m2[:nt, :], op=ALU.add))

            def mergeB(it):
                r0, nt = alltiles[it]
                b = it % 2
                for kc in range(8):
                    S.op("pe", ["mb%d" % b, "ident"], ["T0"], lambda e, kc=kc, nt=nt, b=b: e.transpose(out=Tb[0][:, kc * 128:kc * 128 + nt], in_=mb2[b][:nt, kc * 128:(kc + 1) * 128], identity=ident[:nt, :nt]))
                S.op("act", [], ["T0", "mT"], lambda e, nt=nt: e.copy(out=mT[:, :, :nt], in_=Tb[0][:, :].rearrange("p (k t) -> p k t", k=8)[:, :, :nt]))
                for half in range(2):
                    for kc in range(8):
                        S.op("pe", ["mT", "wo"], ["F%d" % (4 + half)], lambda e, kc=kc, half=half, nt=nt: e.matmul(Fb[4 + half][:nt, :], lhsT=mT[:, kc, :nt], rhs=wo[:, kc, half * 512:(half + 1) * 512], start=(kc == 0), stop=(kc == 7)))
                    S.op("dve", ["xm%d" % b], ["F%d" % (4 + half), "h1t%d" % b], lambda e, half=half, nt=nt, b=b: e.tensor_tensor(out=h1t[b][:nt, half * 512:(half + 1) * 512], in0=Fb[4 + half][:nt, :], in1=xm[b][:nt, half * 512:(half + 1) * 512], op=ALU.add))
                S.dma(["h1t%d" % b], [], (h1_s if PEER else y)[r0:r0 + nt, :], h1t[b][:nt, :])
            mergeA(0)
            for it in range(len(alltiles)):
                if it + 1 < len(alltiles):
                    mergeA(it + 1)
                mergeB(it)
            S.barrier()

        es1.close()
        if PEER:
          NB = 256
          blocks = []
          for b0 in range(0, NALL, NB):
              nb = min(NB, NALL - b0)
              blocks.append((b0, nb, [(b0 + o, min(128, nb - o), o) for o in range(0, nb, 128)]))
          with ExitStack() as es:
            wsh = SB(es, "wsh", [128, 8, D], BF16); wpp_sb = SB(es, "wpp_sb", [128, 2, D], BF16)
            gffn_bc = SB(es, "gffn_bc", [128, D]); gple_bc = SB(es, "gple_bc", [128, D])
            S.dma([], ["gffn_bc"], gffn_bc[:], g_ffn.partition_broadcast(128))
            S.dma([], ["gple_bc"], gple_bc[:], g_ple.partition_broadcast(128))
            S.dma([], ["wpp_sb"], wpp_sb[:], wpp_bf.rearrange("(kc p) c -> p kc c", p=128))
            Kbd = SB(es, "Kbd", [128, NH, 256], BF16)
            with ExitStack() as es2:
                Kst = SB(es2, "Kst", [128, NH, 256])
                S.op("pool", [], ["Kst"], lambda e: e.memset(Kst[:], 0.0))
                S.dma([], ["Kst"], Kst[0:64, :, 0:128], skT[:, 0].rearrange("h c n -> c h n"))
                S.dma([], ["Kst"], Kst[64:128, :, 128:256], skT[:, 1].rearrange("h c n -> c h n"))
                S.op("dve", ["Kst"], ["Kbd"], lambda e: e.tensor_copy(out=Kbd[:], in_=Kst[:]))
                S.barrier()
            h1b = [SB(es, "h1b%d" % i, [128, 2, D]) for i in range(2)]
            n2T = [SB(es, "n2T%d" % i, [128, 8, NB], BF16) for i in range(2)]
            idxT = [SB(es, "idxT%d" % i, [128, 3, NB]) for i in range(2)]
            rs = SB(es, "rs2", [128, 2]); rs3 = SB(es, "rs3", [128, 2])
            n2 = SB(es, "n2", [128, D], BF16); pqT = SB(es, "pqT", [128, NH, NB], BF16)
            sc = SB(es, "sc", [128, 2048]); wk = SB(es, "wk", [128, 2048])
            ts = SB(es, "ts", [128, 16, 16]); ti_ = SB(es, "ti_", [128, 16, 16], U32); tif = SB(es, "tif", [128, 16, 16])
            cand = SB(es, "cand", [128, NH, 256]); bs = SB(es, "bs", [128, NH, 16]); pos = SB(es, "pos", [128, NH, 16], U32)
            pa = SB(es, "pa", [128, NH, 16], U32); pb = SB(es, "pb", [128, NH, 16], U32); paf = SB(es, "paf", [128, NH, 16]); pbf = SB(es, "pbf", [128, NH, 16])
            iidx2 = [SB(es, "iidx%d" % i, [128, 128]) for i in range(2)]; jidx2 = [SB(es, "jidx%d" % i, [128, 128]) for i in range(2)]; gg2 = [SB(es, "gg%d" % i, [128, NH, 16]) for i in range(2)]; gs = SB(es, "gs", [128, NH])
            gh = SB(es, "gh", [128, 128, NB], BF16)
            sbuf_uv = [SB(es, "uv%d" % i, [128, 4, D], BF16) for i in range(3)]
            AB = [SB(es, "AB%d" % i, [128, 2, 4, 128], BF16) for i in range(2)]
            wsb = [SB(es, "wsb%d" % i, [128, 512], BF16) for i in range(2)]
            n3 = SB(es, "n3", [128, D], BF16); n3T = SB(es, "n3T", [128, 8, 128], BF16); gt = SB(es, "gt", [128, D])
            pin = SB(es, "pin", [128, 256]); pbf16 = SB(es, "pbf16", [128, 256], BF16); pT = SB(es, "pT", [128, 2, 128], BF16)
            yo = gt
            uvv = u_bf.rearrange("(j p) c -> p j c", p=128); vvv = v_bf.rearrange("(j p) c -> p j c", p=128)
            iota_bf = SB(es, "iota_bf", [128, 128], BF16)
            S.op("dve", ["iota_f"], ["iota_bf"], lambda e: e.tensor_copy(out=iota_bf[:], in_=iota_f[:]))
            niota_bf = SB(es, "niota_bf", [128, 128], BF16)
            S.op("dve", ["iota_f"], ["niota_bf"], lambda e: e.tensor_scalar(out=niota_bf[:], in0=iota_f[:], scalar1=-1.0, scalar2=None, op0=ALU.mult))
            rot = [0]
            uvn = [0]

            def rms_rstd_act(src_key, src_ap, nt, acc, acck, junk, junkk):
                S.op("act", [src_key], [junkk, acck], lambda e: e.activation(out=junk[:nt, :], in_=src_ap, func=AF.Square, scale=float(D) ** -0.5, accum_out=acc[:nt, 0:1]))
                S.op("act", [acck], [acck], lambda e: e.activation(out=acc[:nt, 0:1], in_=acc[:nt, 0:1], func=AF.Ln, bias=EPS, scale=1.0))
                S.op("act", [acck], [acck], lambda e: e.activation(out=acc[:nt, 0:1], in_=acc[:nt, 0:1], func=AF.Exp, scale=-0.5))

            def bank(lo=0, n=6):
                b = lo + rot[0] % n
                rot[0] += 1
                return Fb[b], "F%d" % b

            def FE_a1(bi, tl_):
                b0, nb, tls = blocks[bi]
                pbi = bi % 2
                hb = h1b[pbi]; nT = n2T[pbi]; iT = idxT[pbi]
                for (r0, nt, o) in tls[tl_:tl_ + 1]:
                    tl = o // 128
                    S.dma([], [("h1b", pbi, tl)], hb[:nt, tl, :], h1_s[r0:r0 + nt, :], q="pool")
                    rms_rstd_act(("h1b", pbi, tl), hb[:nt, tl, :], nt, rs, "rs2", n2, "n2")
                    S.op("act", [("h1b", pbi, tl), "rs2"], ["n2"], lambda e, nt=nt, tl=tl: e.activation(out=n2[:nt, :], in_=hb[:nt, tl, :], func=AF.Identity, scale=rs[:nt, 0:1]))
                    S.op("pool", ["n2", "gffn_bc"], ["n2"], lambda e, nt=nt: e.tensor_tensor(out=n2[:nt, :], in0=n2[:nt, :], in1=gffn_bc[:nt, :], op=ALU.mult))

            def FE_a2(bi, tl_):
                b0, nb, tls = blocks[bi]
                pbi = bi % 2
                nT = n2T[pbi]
                for (r0, nt, o) in tls[tl_:tl_ + 1]:
                    for kc in range(8):
                        S.op("pe", ["n2", "ident"], ["T0"], lambda e, kc=kc, nt=nt: e.transpose(out=Tb[0][:, kc * 128:kc * 128 + nt], in_=n2[:nt, kc * 128:(kc + 1) * 128], identity=ident[:nt, :nt]))
                    S.op("act", [], ["T0", ("n2T", pbi)], lambda e, nt=nt, o=o: e.copy(out=nT[:, :, o:o + nt], in_=Tb[0][:, :].rearrange("p (k t) -> p k t", k=8)[:, :, :nt]))

            def FE_a3(bi):
                b0, nb, tls = blocks[bi]
                pbi = bi % 2
                nT = n2T[pbi]
                S.dma([], ["wsh"], wsh[:], wq_bf.rearrange("(kc p) c -> p kc c", p=128), q="pool")
                for h in range(NH):
                    bp, bk_ = bank(4, 2)
                    for kc in range(8):
                        S.op("pe", [("n2T", pbi), "wsh"], [bk_], lambda e, bp=bp, kc=kc, h=h: e.matmul(bp[:, :nb], lhsT=wsh[:, kc, h * 128:(h + 1) * 128], rhs=nT[:, kc, :nb], start=(kc == 0), stop=(kc == 7)))
                    S.op("act", [], [bk_, ("pqT", h)], lambda e, bp=bp, h=h: e.copy(out=pqT[:, h, :nb], in_=bp[:, :nb]))

            def FE_t(bi, tl_):
                b0, nb, tls = blocks[bi]
                pbi = bi % 2
                if tl_ >= len(tls):
                    return
                iidx = iidx2[tl_]; jidx = jidx2[tl_]; gg = gg2[tl_]
                IK = "iidx%d" % tl_; JK = "jidx%d" % tl_; GK = "gg%d" % tl_
                for (r0, nt, o) in tls[tl_:tl_ + 1]:
                    for h in range(NH):
                        bp, bk_ = Fb[4 + (h // 2) % 2], "F%d" % (4 + (h // 2) % 2)
                        S.op("pe", [("pqT", h), "Kbd"], [bk_], lambda e, bp=bp, h=h, nt=nt, o=o: e.matmul(bp[:nt, (h % 2) * 256:(h % 2) * 256 + 256], lhsT=pqT[:, h, o:o + nt], rhs=Kbd[:, h, :], start=True, stop=True, skip_group_check=True))
                        if h % 2 == 1:
                            S.op("act", [], [bk_, "sc"], lambda e, bp=bp, h=h, nt=nt: e.copy(out=sc[:nt, (h // 2) * 512:(h // 2 + 1) * 512], in_=bp[:nt, :]))
                    for s_ in range(16):
                        seg = sc[:nt, s_ * 128:(s_ + 1) * 128]; wseg = wk[:nt, s_ * 128:(s_ + 1) * 128]
                        S.op("dve", ["sc"], ["ts"], lambda e, seg=seg, s_=s_, nt=nt: e.max(out=ts[:nt, s_, 0:8], in_=seg))
                        S.op("dve", ["sc", "ts"], ["ti_"], lambda e, seg=seg, s_=s_, nt=nt: e.max_index(out=ti_[:nt, s_, 0:8], in_max=ts[:nt, s_, 0:8], in_values=seg))
                        S.op("dve", ["sc", "ts"], ["wk"], lambda e, seg=seg, wseg=wseg, s_=s_, nt=nt: e.match_replace(out=wseg, in_to_replace=ts[:nt, s_, 0:8], in_values=seg, imm_value=-1e30))
                        S.op("dve", ["wk"], ["ts"], lambda e, wseg=wseg, s_=s_, nt=nt: e.max(out=ts[:nt, s_, 8:16], in_=wseg))
                        S.op("dve", ["wk", "ts"], ["ti_"], lambda e, wseg=wseg, s_=s_, nt=nt: e.max_index(out=ti_[:nt, s_, 8:16], in_max=ts[:nt, s_, 8:16], in_values=wseg))
                    ts4 = ts[:nt].rearrange("p (h two) k -> p h two k", two=2)
                    S.op("dve", ["ts"], ["cand"], lambda e, nt=nt, ts4=ts4: e.tensor_tensor(out=cand[:nt].rearrange("p h (a b) -> p h a b", a=16), in0=ts4[:, :, 0, :].unsqueeze(3).to_broadcast([nt, NH, 16, 16]),
                                                                                       in1=ts4[:, :, 1, :].unsqueeze(2).to_broadcast([nt, NH, 16, 16]), op=ALU.add))
                    for h in range(NH):
                        cs = cand[:nt, h, :]; cws = wk[:nt, h * 256:(h + 1) * 256]
                        S.op("dve", ["cand"], ["bs"], lambda e, cs=cs, h=h, nt=nt: e.max(out=bs[:nt, h, 0:8], in_=cs))
                        S.op("dve", ["cand", "bs"], ["pos"], lambda e, cs=cs, h=h, nt=nt: e.max_index(out=pos[:nt, h, 0:8], in_max=bs[:nt, h, 0:8], in_values=cs))
                        S.op("dve", ["cand", "bs"], ["wk"], lambda e, cs=cs, cws=cws, h=h, nt=nt: e.match_replace(out=cws, in_to_replace=bs[:nt, h, 0:8], in_values=cs, imm_value=-1e30))
                        S.op("dve", ["wk"], ["bs"], lambda e, cws=cws, h=h, nt=nt: e.max(out=bs[:nt, h, 8:16], in_=cws))
                        S.op("dve", ["wk", "bs"], ["pos"], lambda e, cws=cws, h=h, nt=nt: e.max_index(out=pos[:nt, h, 8:16], in_max=bs[:nt, h, 8:16], in_values=cws))
                    S.op("dve", ["pos"], ["pa"], lambda e, nt=nt: e.tensor_single_scalar(out=pa[:nt], in_=pos[:nt], scalar=4, op=ALU.logical_shift_right))
                    S.op("dve", ["pos"], ["pb"], lambda e, nt=nt: e.tensor_single_scalar(out=pb[:nt], in_=pos[:nt], scalar=15, op=ALU.bitwise_and))
                    S.op("dve", ["pa"], ["paf"], lambda e, nt=nt: e.tensor_copy(out=paf[:nt], in_=pa[:nt]))
                    S.op("dve", ["pb"], ["pbf"], lambda e, nt=nt: e.tensor_copy(out=pbf[:nt], in_=pb[:nt]))
                    S.op("dve", ["ti_"], ["tif"], lambda e, nt=nt: e.tensor_copy(out=tif[:nt], in_=ti_[:nt]))
                    tif4 = tif[:nt].rearrange("p (h two) k -> p h two k", two=2)
                    io16 = iota_f[:nt, 0:16].unsqueeze(1).unsqueeze(1).to_broadcast([nt, NH, 16, 16])
                    oh = sc[:nt, :].rearrange("p (h k a) -> p h k a", h=NH, k=16)
                    for (pf, pk, two, dst, dk) in ((paf, "paf", 0, iidx, IK), (pbf, "pbf", 1, jidx, JK)):
                        S.op("dve", [pk], ["sc"], lambda e, pf=pf, nt=nt, oh=oh, io16=io16: e.tensor_tensor(out=oh, in0=pf[:nt].unsqueeze(3).to_broadcast([nt, NH, 16, 16]), in1=io16, op=ALU.is_equal))
                        S.op("dve", ["tif", "sc"], ["sc"], lambda e, two=two, nt=nt, oh=oh, tif4=tif4: e.tensor_tensor(out=oh, in0=oh, in1=tif4[:, :, two, :].unsqueeze(2).to_broadcast([nt, NH, 16, 16]), op=ALU.mult))
                        S.op("dve", ["sc"], [dk], lambda e, dst=dst, nt=nt, oh=oh: e.tensor_reduce(out=dst[:nt, :].rearrange("p (h k) -> p h k", h=NH), in_=oh, axis=AX.X, op=ALU.add))
                    S.op("dve", ["bs"], [GK], lambda e, nt=nt: e.tensor_tensor(out=gg[:nt], in0=bs[:nt], in1=bs[:nt, :, 0:1].to_broadcast([nt, NH, 16]), op=ALU.subtract))

            def FE_x(bi, tl_):
                b0, nb, tls = blocks[bi]
                pbi = bi % 2
                iT = idxT[pbi]
                if tl_ >= len(tls):
                    return
                iidx = iidx2[tl_]; jidx = jidx2[tl_]; gg = gg2[tl_]
                GK = "gg%d" % tl_
                for (r0, nt, o) in tls[tl_:tl_ + 1]:
                    S.op("act", [GK], [GK], lambda e, nt=nt: e.activation(out=gg[:nt], in_=gg[:nt], func=AF.Exp))
                    S.op("dve", [GK], ["gs"], lambda e, nt=nt: e.tensor_reduce(out=gs[:nt], in_=gg[:nt], axis=AX.X, op=ALU.add))
                    S.op("dve", ["gs"], ["gs"], lambda e, nt=nt: e.reciprocal(out=gs[:nt], in_=gs[:nt]))
                    S.op("dve", ["gs", GK], [GK], lambda e, nt=nt: e.tensor_tensor(out=gg[:nt], in0=gg[:nt], in1=gs[:nt].unsqueeze(2).to_broadcast([nt, NH, 16]), op=ALU.mult))
                    bp, bk_ = bank(4, 2)
                    for q_, (src, sk_) in enumerate(((iidx[:nt, :], "iidx%d" % tl_), (jidx[:nt, :], "jidx%d" % tl_), (gg[:nt].rearrange("p h k -> p (h k)"), "gg%d" % tl_))):
                        S.op("pe", [sk_, "identf"], [bk_], lambda e, bp=bp, q_=q_, src=src, nt=nt: e.transpose(out=bp[:, q_ * 128:q_ * 128 + nt], in_=src, identity=identf[:nt, :nt]))
                    S.op("act", [], [bk_, ("idxT", pbi)], lambda e, bp=bp, nt=nt, o=o: e.copy(out=iT[:, 1:3, o:o + nt], in_=bp[:, 128:384].rearrange("p (q t) -> p q t", q=2)[:, :, :nt]))
                    S.op("act", [], [bk_, ("idxT", pbi)], lambda e, bp=bp, nt=nt, o=o: e.activation(out=iT[:, 0, o:o + nt], in_=bp[:, 0:nt], func=AF.Identity, scale=-1.0))

            def BE_hid(bi, inject={}):
                b0, nb, tls = blocks[bi]
                pbi = bi % 2
                hb = h1b[pbi]; nT = n2T[pbi]; iT = idxT[pbi]
                for j0 in range(0, 128, 4):
                    for fn in inject.get(j0 // 4, []):
                        fn()
                    ub = sbuf_uv[uvn[0] % 3]; uk = "uv%d" % (uvn[0] % 3); uvn[0] += 1
                    S.dma([], [uk], ub[:], uvv[:, j0:j0 + 4, :])
                    for jj in range(0, 4, 2):
                        bp, bk_ = bank(0, 4)
                        for j2 in range(2):
                            for dc in range(8):
                                S.op("pe", [uk, ("n2T", pbi)], [bk_], lambda e, bp=bp, ub=ub, jj=jj, j2=j2, dc=dc: e.matmul(bp[:, j2 * nb:(j2 + 1) * nb], lhsT=ub[:, jj + j2, dc * 128:(dc + 1) * 128], rhs=nT[:, dc, :nb], start=(dc == 0), stop=(dc == 7), skip_group_check=True))
                        S.op("act", [], [bk_, "gh"], lambda e, bp=bp, j=j0 + jj: e.activation(out=gh[:, j:j + 2, :nb], in_=bp[:, :2 * nb].rearrange("p (a t) -> p a t", a=2), func=GELU))

            def BE_W(bi):
                b0, nb, tls = blocks[bi]
                pbi = bi % 2
                hb = h1b[pbi]; nT = n2T[pbi]; iT = idxT[pbi]
                iob = iota_f[:, :].unsqueeze(1).to_broadcast([128, 4, 128])
                pend = []

                def flush():
                    bp, bk_, t0, g4 = pend.pop(0)
                    if False:
                        wb_ = wsb[(g4 // 2) % 2]; wbk = "wsb%d" % ((g4 // 2) % 2)
                        S.op("act", [], [bk_, wbk], lambda e: e.copy(out=wb_[:], in_=bp[:, :]))
                        S.op("pool", [wbk], ["gh"], lambda e: e.tensor_tensor(out=gh[:, :, t0:t0 + 4], in0=gh[:, :, t0:t0 + 4], in1=wb_[:, :].rearrange("p (t j) -> p j t", t=4), op=ALU.mult))
                    else:
                        S.op("dve", [], [bk_, "gh"], lambda e: e.tensor_tensor(out=gh[:, :, t0:t0 + 4], in0=gh[:, :, t0:t0 + 4], in1=bp[:, :].rearrange("p (t j) -> p j t", t=4), op=ALU.mult))

                for t0 in range(0, nb, 4):
                    g4 = t0 // 4
                    ab = AB[g4 % 2]; abk = "AB%d" % (g4 % 2)
                    for tt in range(4):
                        t = t0 + tt
                        if tt < 2:
                            S.op("act", [("idxT", pbi)], [abk + "a"], lambda e, ab=ab, t=t, tt=tt: e.activation(out=ab[:, 0, tt, :], in_=iota_f[:, :], func=AF.Square, bias=iT[:, 0, t:t + 1], scale=1.0))
                            S.op("act", [abk + "a"], [abk + "a"], lambda e, ab=ab, tt=tt: e.activation(out=ab[:, 0, tt, :], in_=ab[:, 0, tt, :], func=AF.Relu, bias=1.0, scale=-1.0))
                        else:
                            S.op("dve", [("idxT", pbi)], [abk + "a"], lambda e, ab=ab, t=t, tt=tt: e.tensor_scalar(out=ab[:, 0, tt, :], in0=niota_bf[:, :], scalar1=iT[:, 0, t:t + 1], scalar2=None, op0=ALU.is_equal))
                        S.op("dve", [("idxT", pbi)], [abk + "b"], lambda e, ab=ab, t=t, tt=tt: e.tensor_scalar(out=ab[:, 1, tt, :], in0=iota_bf[:, :], scalar1=iT[:, 1, t:t + 1], scalar2=iT[:, 2, t:t + 1], op0=ALU.is_equal, op1=ALU.mult))
                    bp, bk_ = bank()
                    for tt in range(4):
                        S.op("pe", [abk + "a", abk + "b"], [bk_], lambda e, bp=bp, ab=ab, tt=tt: e.matmul(bp[:, tt * 128:(tt + 1) * 128], lhsT=ab[:, 0, tt, :], rhs=ab[:, 1, tt, :], start=True, stop=True, skip_group_check=True))
                    pend.append((bp, bk_, t0, g4))
                    if len(pend) > 1:
                        flush()
                while pend:
                    flush()

            def BE_out(bi, inject={}):
                b0, nb, tls = blocks[bi]
                pbi = bi % 2
                hb = h1b[pbi]; nT = n2T[pbi]; iT = idxT[pbi]
                for j0 in range(0, 128, 4):
                    for fn in inject.get(j0 // 4, []):
                        fn()
                    vb_ = sbuf_uv[uvn[0] % 3]; vk = "uv%d" % (uvn[0] % 3); uvn[0] += 1
                    S.dma([], [vk], vb_[:], vvv[:, j0:j0 + 4, :])
                    for jj in range(4):
                        j = j0 + jj
                        for (r0, nt, o) in tls:
                            for half in range(2):
                                ab_ = (o // 128) * 2 + half
                                S.op("pe", ["gh", vk], ["F%d" % ab_], lambda e, vb_=vb_, jj=jj, j=j, half=half, nt=nt, o=o, ab_=ab_: e.matmul(Fb[ab_][:nt, :], lhsT=gh[:, j, o:o + nt], rhs=vb_[:, jj, half * 512:(half + 1) * 512], start=(j == 0), stop=(j == 127)))

            def PLE_evac(bi):
                b0, nb, tls = blocks[bi]
                pbi = bi % 2
                hb = h1b[pbi]
                for (r0, nt, o) in tls:
                    tl = o // 128
                    for half in range(2):
                        S.op("act", [], ["F%d" % (tl * 2 + half), "gt"], lambda e, half=half, nt=nt, tl=tl: e.copy(out=gt[:nt, half * 512:(half + 1) * 512], in_=Fb[tl * 2 + half][:nt, :]))
                    S.op("pool", ["gt", ("h1b", pbi, tl)], [("h1b", pbi, tl)], lambda e, nt=nt, tl=tl: e.tensor_tensor(out=hb[:nt, tl, :], in0=hb[:nt, tl, :], in1=gt[:nt, :], op=ALU.add))

            def PLE_p1(bi, tl_):
                b0, nb, tls = blocks[bi]
                pbi = bi % 2
                hb = h1b[pbi]
                if tl_ == 0:
                    S.dma([], ["wsh"], wsh[:], wpg_bf.rearrange("(kc p) c -> p kc c", p=128), q="pool")
                for (r0, nt, o) in tls[tl_:tl_ + 1]:
                    tl = o // 128
                    hk_ = ("h1b", pbi, tl)
                    rms_rstd_act(hk_, hb[:nt, tl, :], nt, rs3, "rs3", n3, "n3")
                    S.op("act", [hk_, "rs3"], ["n3"], lambda e, nt=nt, tl=tl: e.activation(out=n3[:nt, :], in_=hb[:nt, tl, :], func=AF.Identity, scale=rs3[:nt, 0:1]))
                    S.op("pool", ["n3", "gple_bc"], ["n3"], lambda e, nt=nt: e.tensor_tensor(out=n3[:nt, :], in0=n3[:nt, :], in1=gple_bc[:nt, :], op=ALU.mult))
                    S.dma([], ["pin"], pin[:nt, :], (pp[r0:r0 + nt, :] if r0 < NTOK else ps_[r0 - NTOK:r0 - NTOK + nt, :]), q="pool")
                    S.op("pool", ["pin"], ["pbf16"], lambda e, nt=nt: e.tensor_copy(out=pbf16[:nt, :], in_=pin[:nt, :]))

            def PLE_p2(bi, tl_):
                b0, nb, tls = blocks[bi]
                for (r0, nt, o) in tls[tl_:tl_ + 1]:
                    for kc in range(8):
                        S.op("pe", ["n3", "ident"], ["T1"], lambda e, kc=kc, nt=nt: e.transpose(out=Tb[1][:, kc * 128:kc * 128 + nt], in_=n3[:nt, kc * 128:(kc + 1) * 128], identity=ident[:nt, :nt]))
                    S.op("act", [], ["T1", "n3T"], lambda e, nt=nt: e.copy(out=n3T[:, :, :nt], in_=Tb[1][:, :].rearrange("p (k t) -> p k t", k=8)[:, :, :nt]))
                    for kc in range(2):
                        S.op("pe", ["pbf16", "ident"], ["T1"], lambda e, kc=kc, nt=nt: e.transpose(out=Tb[1][:, kc * 128:kc * 128 + nt], in_=pbf16[:nt, kc * 128:(kc + 1) * 128], identity=ident[:nt, :nt]))
                    S.op("act", [], ["T1", "pT"], lambda e, nt=nt: e.copy(out=pT[:, :, :nt], in_=Tb[1][:, 0:256].rearrange("p (k t) -> p k t", k=2)[:, :, :nt]))

            def PLE_p3(bi, tl_):
                b0, nb, tls = blocks[bi]
                pbi = bi % 2
                hb = h1b[pbi]
                for (r0, nt, o) in tls[tl_:tl_ + 1]:
                    tl = o // 128
                    hk_ = ("h1b", pbi, tl)
                    for half in range(2):
                        gb_, gk_ = bank(4, 2)
                        for kc in range(8):
                            S.op("pe", ["n3T", "wsh"], [gk_], lambda e, gb_=gb_, kc=kc, half=half, nt=nt: e.matmul(gb_[:nt, :], lhsT=n3T[:, kc, :nt], rhs=wsh[:, kc, half * 512:(half + 1) * 512], start=(kc == 0), stop=(kc == 7)))
                        S.op("act", [], [gk_, "gt"], lambda e, gb_=gb_, half=half, nt=nt: e.activation(out=gt[:nt, half * 512:(half + 1) * 512], in_=gb_[:nt, :], func=AF.Sigmoid))
                        pb_, pk_ = bank(4, 2)
                        for kc in range(2):
                            S.op("pe", ["pT", "wpp_sb"], [pk_], lambda e, pb_=pb_, kc=kc, half=half, nt=nt: e.matmul(pb_[:nt, :], lhsT=pT[:, kc, :nt], rhs=wpp_sb[:, kc, half * 512:(half + 1) * 512], start=(kc == 0), stop=(kc == 1)))
                        S.op("act", [], [pk_, "n3"], lambda e, pb_=pb_, half=half, nt=nt: e.copy(out=n3[:nt, half * 512:(half + 1) * 512], in_=pb_[:nt, :]))
                    S.op("pool", ["n3", "gt"], ["gt"], lambda e, nt=nt: e.tensor_tensor(out=yo[:nt, :], in0=yo[:nt, :], in1=n3[:nt, :], op=ALU.mult))
                    S.op("pool", [hk_, "gt"], ["gt"], lambda e, nt=nt, tl=tl: e.tensor_tensor(out=yo[:nt, :], in0=yo[:nt, :], in1=hb[:nt, tl, :], op=ALU.add))
                    S.dma(["gt"], [], y[r0:r0 + nt, :], yo[:nt, :], q="pool")

            def F(fn, *a):
                return lambda: fn(*a)

            nblk = len(blocks)
            FE_a1(0, 0); FE_a2(0, 0); FE_a1(0, 1); FE_a2(0, 1); FE_a3(0)
            for tl_ in range(2):
                FE_t(0, tl_); FE_x(0, tl_)
            for bi in range(nblk):
                nx = bi + 1 < nblk
                inj = {}
                if bi > 0:
                    inj.update({1: [F(PLE_p1, bi - 1, 0)], 5: [F(PLE_p2, bi - 1, 0)], 8: [F(PLE_p3, bi - 1, 0)],
                                9: [F(PLE_p1, bi - 1, 1)], 13: [F(PLE_p2, bi - 1, 1)], 16: [F(PLE_p3, bi - 1, 1)]})
                    inj[28] = [F(FE_x, bi, 1)]
                if nx:
                    inj.update({17: [F(FE_a1, bi + 1, 0)], 20: [F(FE_a2, bi + 1, 0)], 21: [F(FE_a1, bi + 1, 1)], 24: [F(FE_a2, bi + 1, 1)], 26: [F(FE_a3, bi + 1)]})
                BE_hid(bi, inj)
                BE_W(bi)
                inj = {}
                if nx:
                    inj = {0: [F(FE_t, bi + 1, 0)], 24: [F(FE_x, bi + 1, 0)], 25: [F(FE_t, bi + 1, 1)]}
                BE_out(bi, inj)
                PLE_evac(bi)
            for tl_ in range(2):
                PLE_p1(nblk - 1, tl_); PLE_p2(nblk - 1, tl_); PLE_p3(nblk - 1, tl_)

        S.barrier()
    return nc


_CACHE = {}


def kernel(**inp):
    f = lambda a: np.ascontiguousarray(np.asarray(a, dtype=np.float32))
    NSEQ, NSAMP = 4, 2
    key = "full"
    if key not in _CACHE:
        _CACHE[key] = build(NSEQ, NSAMP)
    nc = _CACHE[key]
    w_in = f(inp["w_in"])[0]
    wi = np.ascontiguousarray(np.concatenate([w_in[:, 0:1536], w_in[:, 1544:5128]], axis=1))
    wfl = np.ascontiguousarray(w_in[:, 1536:1544])
    rb = f(inp["rel_bias_b"])[0]
    kl = np.arange(128)[:, None]; ql = np.arange(128)[None, :]
    idx0 = np.clip(kl - ql, -128, 128) + 128
    idx1 = np.clip(kl - ql - 128, -128, 128) + 128
    idx2 = np.zeros((128, 128), np.int64)
    biasT = np.ascontiguousarray(np.stack([rb[:, idx0], rb[:, idx1], rb[:, idx2]], axis=1))
    u = f(inp["peer_u"])[0]; v = f(inp["peer_v"])[0]
    uL = np.ascontiguousarray(u.reshape(128, 128, 8, 128).transpose(1, 3, 2, 0)).reshape(128 * 128, 1024)
    vL = np.ascontiguousarray(v.reshape(128, 128, 1024).transpose(1, 0, 2)).reshape(128 * 128, 1024)
    skT = np.ascontiguousarray(f(inp["peer_subkeys"])[0].transpose(0, 1, 3, 2))
    shared = dict(wi=wi, wfl=wfl, g_mix=f(inp["g_mix"])[0], g_ffn=f(inp["g_ffn"])[0], g_ple=f(inp["g_ple"])[0], b_f=f(inp["b_f"])[0],
                  qn_a=f(inp["qn_a"])[0], kn_a=f(inp["kn_a"])[0], qn_b=f(inp["qn_b"])[0], kn_b=f(inp["kn_b"])[0], biasT=biasT,
                  wupa=f(inp["w_up_a"])[0], wupb=f(inp["w_up_b"])[0], wout=f(inp["w_out"])[0], wq=f(inp["peer_wq"])[0],
                  wpg=f(inp["w_ple_gate"])[0], wpp=f(inp["w_ple_proj"])[0], skT=skT, uL=uL, vL=vL)
    xpr = f(inp["x_prompt"]); xsa = f(inp["x_sample"]); ppr = f(inp["p_prompt"])[0]; psa = f(inp["p_sample"])[0]
    cak = f(inp["cache_a_k"])[0]; cav = f(inp["cache_a_v"])[0]; calf = f(inp["cache_a_logf"])[0]
    cbk = f(inp["cache_b_k"])[0]; cbv = f(inp["cache_b_v"])[0]
    in_maps = []
    for c in range(NCORES):
        m = dict(shared)
        m["xp"] = xpr[4 * c:4 * c + 4].reshape(4 * T, D); m["xs"] = xsa[2 * c:2 * c + 2].reshape(2 * NS, D)
        m["pp"] = ppr[4 * c:4 * c + 4].reshape(4 * T, 256); m["ps"] = psa[2 * c:2 * c + 2].reshape(2 * NS, 256)
        m["cak"] = cak[2 * c:2 * c + 2].reshape(2, PAST, 512); m["cav"] = cav[2 * c:2 * c + 2].reshape(2, PAST, 512)
        m["calf"] = calf[2 * c:2 * c + 2]
        m["cbk"] = cbk[2 * c:2 * c + 2].reshape(2, LB, 512); m["cbv"] = cbv[2 * c:2 * c + 2].reshape(2, LB, 512)
        in_maps.append({k: np.ascontiguousarray(a) for k, a in m.items()})
    res = run_bass_kernel_spmd(nc, in_maps, core_ids=list(range(NCORES))).results
    cat = lambda k: np.concatenate([np.asarray(r[k], dtype=np.float32) for r in res], axis=0)
    yall = [np.asarray(r["y"], dtype=np.float32) for r in res]
    y_p = np.concatenate([a[:4 * T] for a in yall], axis=0).reshape(32, T, D)
    y_s = np.concatenate([a[4 * T:] for a in yall], axis=0).reshape(16, NS, D)
    return (y_p, y_s,
            cat("ak").reshape(1, 32, T, NH, HD), cat("av").reshape(1, 32, T, NH, HD), cat("af").reshape(1, 32, T, NH),
            cat("bk").reshape(1, 32, LB, NH, HD), cat("bv").reshape(1, 32, LB, NH, HD),
            cat("aks").reshape(1, 16, NS, NH, HD), cat("avs").reshape(1, 16, NS, NH, HD), cat("afs").reshape(1, 16, NS, NH),
            cat("bks").reshape(1, 16, NS, NH, HD), cat("bvs").reshape(1, 16, NS, NH, HD))
```
